# Optimizing a Trainium2 kernel written in Bass

```python
import jax, jax.numpy as jnp
from jax import lax
import numpy as np

D_MODEL = 4096
BATCH = 4
SEQ = 4096
DEPTH = 1

CHUNK = 64
N_MEM = 256
EPS = 1e-6

SSD_D_INNER = D_MODEL
SSD_HEAD_DIM = 64
SSD_N_HEADS = SSD_D_INNER // SSD_HEAD_DIM
SSD_N_GROUPS = 8
SSD_HEADS_PER_GROUP = SSD_N_HEADS // SSD_N_GROUPS
SSD_D_STATE = 128
SSD_CONV_WIDTH = 4
SSD_CONV_DIM = SSD_D_INNER + 2 * SSD_N_GROUPS * SSD_D_STATE

SB_HEAD_DIM = 128
SB_N_HEADS = 32
SB_D_INNER = SB_N_HEADS * SB_HEAD_DIM
SB_Q_BLOCK = 128

XA_N_HEADS = 4
XA_HEAD_DIM = D_MODEL // XA_N_HEADS

D_FF = ((8 * D_MODEL // 3 + 255) // 256) * 256

IN_SIZES = (SSD_D_INNER, SSD_CONV_DIM, SSD_N_HEADS, SB_D_INNER, SB_D_INNER, SB_D_INNER, 2 * D_MODEL)
D_IN_PROJ = sum(IN_SIZES)
IN_OFFSETS = tuple(int(v) for v in np.cumsum(IN_SIZES)[:-1])

kernel_name = "hybrid_ssd_stickbreaking_griffin_merge"


def rmsnorm(x, w):
    xf = x.astype(jnp.float32)
    y = xf * lax.rsqrt(jnp.mean(xf * xf, axis=-1, keepdims=True) + EPS)
    return (y * w.astype(jnp.float32)).astype(x.dtype)


def gated_group_rmsnorm(y, z, w):
    b, S, d = y.shape
    g = (y * jax.nn.silu(z)).astype(jnp.float32).reshape(b, S, SSD_N_GROUPS, d // SSD_N_GROUPS)
    g = g * lax.rsqrt(jnp.mean(g * g, axis=-1, keepdims=True) + EPS)
    return (g.reshape(b, S, d) * w.astype(jnp.float32)).astype(y.dtype)


def causal_depthwise_conv(u, w, bias):
    C = u.shape[-1]
    out = lax.conv_general_dilated(
        u, w[:, None, :].astype(u.dtype), window_strides=(1,), padding=[(SSD_CONV_WIDTH - 1, 0)],
        dimension_numbers=("NWC", "WIO", "NWC"), feature_group_count=C)
    return out + bias.astype(u.dtype)


def ssd_chunked_scan(xs, dt, A, Bm, Cm):
    b, S, H, P = xs.shape
    G, HPG, N = SSD_N_GROUPS, SSD_HEADS_PER_GROUP, SSD_D_STATE
    c = S // CHUNK
    dtype = xs.dtype
    x_c = xs.reshape(b, c, CHUNK, G, HPG, P)
    dt_c = dt.reshape(b, c, CHUNK, G, HPG)
    B_c = Bm.reshape(b, c, CHUNK, G, N)
    C_c = Cm.reshape(b, c, CHUNK, G, N)
    a_cum = jnp.cumsum(dt_c * A.reshape(G, HPG), axis=2)
    Xd = x_c * dt_c[..., None].astype(dtype)
    diff = a_cum[:, :, :, None] - a_cum[:, :, None, :]
    causal = jnp.tril(jnp.ones((CHUNK, CHUNK), dtype=bool))[:, :, None, None]
    L = jnp.exp(jnp.where(causal, diff, -jnp.inf)).astype(dtype)
    CB = jnp.einsum('bclgn,bcsgn->bclsg', C_c, B_c)
    y_diag = jnp.einsum('bclsg,bclsgh,bcsghp->bclghp', CB, L, Xd)
    decay_to_end = jnp.exp(a_cum[:, :, -1:] - a_cum).astype(dtype)
    states = jnp.einsum('bclgn,bclgh,bclghp->bcghpn', B_c, decay_to_end, Xd)
    chunk_decay = jnp.exp(a_cum[:, :, -1]).astype(dtype)

    def step(h, inp):
        st, dec = inp
        return h * dec[..., None, None] + st, h

    h0 = jnp.zeros((b, G, HPG, P, N), dtype)
    _, prev = lax.scan(step, h0, (jnp.moveaxis(states, 1, 0), jnp.moveaxis(chunk_decay, 1, 0)))
    prev = jnp.moveaxis(prev, 0, 1)
    y_off = jnp.einsum('bclgn,bcghpn,bclgh->bclghp', C_c, prev, jnp.exp(a_cum).astype(dtype))
    return (y_diag + y_off).reshape(b, S, H, P)


def stick_breaking_attention(q, k, v):
    S = q.shape[1]
    scale = SB_HEAD_DIM ** -0.5
    outs = []
    for i in range(S // SB_Q_BLOCK):
        q0 = i * SB_Q_BLOCK
        kend = q0 + SB_Q_BLOCK
        z = jnp.einsum('bqhd,bkhd->bhqk', q[:, q0:kend], k[:, :kend]).astype(jnp.float32) * scale
        t_idx = q0 + jnp.arange(SB_Q_BLOCK)[:, None]
        s_idx = jnp.arange(kend)[None, :]
        before = s_idx < t_idx
        log_keep = jnp.where(before, jax.nn.log_sigmoid(-z), 0.0)
        cum = jnp.cumsum(log_keep, axis=-1)
        log_w = jax.nn.log_sigmoid(z) + (cum[..., -1:] - cum)
        w = jnp.where(before, jnp.exp(log_w), 0.0).astype(v.dtype)
        outs.append(jnp.einsum('bhqk,bkhd->bqhd', w, v[:, :kend]))
    return jnp.concatenate(outs, axis=1)


def memory_cross_attention(h, m, w_q, w_kv, w_o):
    b, S, _ = h.shape
    q = (h @ w_q).reshape(b, S, XA_N_HEADS, XA_HEAD_DIM)
    k, v = jnp.split(m @ w_kv, 2, axis=-1)
    k = k.reshape(b, -1, XA_N_HEADS, XA_HEAD_DIM)
    v = v.reshape(b, -1, XA_N_HEADS, XA_HEAD_DIM)
    s = jnp.einsum('bqhd,bkhd->bhqk', q, k).astype(jnp.float32) * (XA_HEAD_DIM ** -0.5)
    p = jax.nn.softmax(s, axis=-1).astype(v.dtype)
    o = jnp.einsum('bhqk,bkhd->bqhd', p, v).reshape(b, S, D_MODEL)
    return o @ w_o


def swiglu_ffn(h, w_in, w_out):
    g, u = jnp.split(h @ w_in, 2, axis=-1)
    return (jax.nn.silu(g) * u) @ w_out


def setup_inputs(seed: int = 0) -> dict:
    key = jax.random.key(seed)
    ks = jax.random.split(key, 24)
    f32 = jnp.float32
    L = DEPTH

    def nrm(k, shape, scale):
        return jax.random.normal(k, shape, f32) * scale

    def gain(k, shape):
        return 1.0 + 0.02 * jax.random.normal(k, shape, f32)

    dt_init = jnp.exp(jax.random.uniform(ks[7], (L, SSD_N_HEADS), f32, np.log(1e-3), np.log(1e-1)))
    dt_bias = dt_init + jnp.log(-jnp.expm1(-dt_init))
    return {
        "x": nrm(ks[0], (BATCH, SEQ, D_MODEL), 1.0),
        "mem": nrm(ks[1], (BATCH, N_MEM, D_MODEL), 1.0),
        "norm_mix": gain(ks[2], (L, D_MODEL)),
        "w_in": nrm(ks[3], (L, D_MODEL, D_IN_PROJ), D_MODEL ** -0.5),
        "b_gate": nrm(ks[4], (L, 2 * D_MODEL), 0.02),
        "conv_w": nrm(ks[5], (L, SSD_CONV_WIDTH, SSD_CONV_DIM), SSD_CONV_WIDTH ** -0.5),
        "conv_b": nrm(ks[6], (L, SSD_CONV_DIM), 0.02),
        "dt_bias": dt_bias,
        "a_log": jnp.log(jax.random.uniform(ks[8], (L, SSD_N_HEADS), f32, 1.0, 16.0)),
        "d_skip": gain(ks[9], (L, SSD_N_HEADS)),
        "ssd_norm": gain(ks[10], (L, SSD_D_INNER)),
        "w_ssd_out": nrm(ks[11], (L, SSD_D_INNER, D_MODEL), SSD_D_INNER ** -0.5),
        "w_sb_out": nrm(ks[12], (L, SB_D_INNER, D_MODEL), SB_D_INNER ** -0.5),
        "w_out": nrm(ks[13], (L, D_MODEL, D_MODEL), D_MODEL ** -0.5),
        "norm_xa": gain(ks[14], (L, D_MODEL)),
        "norm_mem": gain(ks[15], (L, D_MODEL)),
        "w_xa_q": nrm(ks[16], (L, D_MODEL, D_MODEL), D_MODEL ** -0.5),
        "w_xa_kv": nrm(ks[17], (L, D_MODEL, 2 * D_MODEL), D_MODEL ** -0.5),
        "w_xa_o": nrm(ks[18], (L, D_MODEL, D_MODEL), D_MODEL ** -0.5),
        "norm_ffn": gain(ks[19], (L, D_MODEL)),
        "w_ffn_in": nrm(ks[20], (L, D_MODEL, 2 * D_FF), D_MODEL ** -0.5),
        "w_ffn_out": nrm(ks[21], (L, D_FF, D_MODEL), D_FF ** -0.5),
        "norm_final": gain(ks[22], (D_MODEL,)),
    }


def reference(x, mem, norm_mix, w_in, b_gate, conv_w, conv_b, dt_bias, a_log, d_skip, ssd_norm,
              w_ssd_out, w_sb_out, w_out, norm_xa, norm_mem, w_xa_q, w_xa_kv, w_xa_o,
              norm_ffn, w_ffn_in, w_ffn_out, norm_final):
    b, S, _ = x.shape
    for l in range(DEPTH):
        h = rmsnorm(x, norm_mix[l])
        proj = h @ w_in[l]
        z, xbc, dt_raw, q, k, v, gate_raw = jnp.split(proj, IN_OFFSETS, axis=-1)

        xbc = jax.nn.silu(causal_depthwise_conv(xbc, conv_w[l], conv_b[l]))
        xs, Bm, Cm = jnp.split(xbc, [SSD_D_INNER, SSD_D_INNER + SSD_N_GROUPS * SSD_D_STATE], axis=-1)
        xs = xs.reshape(b, S, SSD_N_HEADS, SSD_HEAD_DIM)
        Bm = Bm.reshape(b, S, SSD_N_GROUPS, SSD_D_STATE)
        Cm = Cm.reshape(b, S, SSD_N_GROUPS, SSD_D_STATE)
        dt = jax.nn.softplus(dt_raw.astype(jnp.float32) + dt_bias[l].astype(jnp.float32))
        A = -jnp.exp(a_log[l].astype(jnp.float32))
        y = ssd_chunked_scan(xs, dt, A, Bm, Cm) + d_skip[l][:, None] * xs
        y = gated_group_rmsnorm(y.reshape(b, S, SSD_D_INNER), z, ssd_norm[l])
        branch_ssd = y @ w_ssd_out[l]

        o = stick_breaking_attention(q.reshape(b, S, SB_N_HEADS, SB_HEAD_DIM),
                                     k.reshape(b, S, SB_N_HEADS, SB_HEAD_DIM),
                                     v.reshape(b, S, SB_N_HEADS, SB_HEAD_DIM))
        branch_sb = o.reshape(b, S, SB_D_INNER) @ w_sb_out[l]

        g_ssd, g_sb = jnp.split(jax.nn.sigmoid(gate_raw + b_gate[l]), 2, axis=-1)
        x = x + (g_ssd * branch_ssd + g_sb * branch_sb) @ w_out[l]

        x = x + memory_cross_attention(rmsnorm(x, norm_xa[l]), rmsnorm(mem, norm_mem[l]),
                                       w_xa_q[l], w_xa_kv[l], w_xa_o[l])

        x = x + swiglu_ffn(rmsnorm(x, norm_ffn[l]), w_ffn_in[l], w_ffn_out[l])
    return rmsnorm(x, norm_final)
```

```python
import numpy as np
import ml_dtypes
import concourse.bass as bass
import concourse.mybir as mybir
from concourse.bass_utils import run_bass_kernel_spmd
from contextlib import ExitStack

F32 = mybir.dt.float32
BF16 = mybir.dt.bfloat16
AF = mybir.ActivationFunctionType
ALU = mybir.AluOpType
AX = mybir.AxisListType

COMPUTE = ("pe", "act", "dve", "pool")
ALLENG = ("pe", "act", "dve", "pool", "sp")
EPS = 1e-6
NEG = -30000.0
SAME_ENGINE_SYNC = True


class Op:
    __slots__ = ("eng", "fn", "reads", "writes", "chan", "idx", "deps", "sig",
                 "waits", "ordinal", "needs_sig")

    def __init__(self, eng, fn, reads, writes, chan, idx):
        self.eng = eng
        self.fn = fn
        self.reads = reads
        self.writes = writes
        self.chan = chan
        self.idx = idx
        self.deps = set()
        self.sig = None
        self.waits = []
        self.ordinal = 0
        self.needs_sig = False


class Prog:
    EPOCH = 12000

    def __init__(self, nc, same_engine_sync=True):
        self.nc = nc
        self.ops = []
        self.same_engine_sync = same_engine_sync
        self.barriers = []

    def add(self, eng, fn, reads=(), writes=(), chan=None):
        op = Op(eng, fn, tuple(reads), tuple(writes), chan, len(self.ops))
        self.ops.append(op)
        return op

    def barrier(self):
        self.barriers.append(len(self.ops))

    def analyze(self):
        ops = self.ops
        last_writer = {}
        readers = {}
        chan_last = {}
        chan_count = {}
        eng_last = {}
        bset = set(self.barriers)
        pending = {}
        for op in ops:
            if op.idx in bset:
                deps = set(eng_last.values()) | set(chan_last.values())
                for e in ALLENG:
                    pending.setdefault(e, set()).update(deps)
            d = op.deps
            if op.eng in pending:
                d |= pending.pop(op.eng)
            for k in op.reads:
                if k in last_writer:
                    d.add(last_writer[k])
            for k in op.writes:
                if k in last_writer:
                    d.add(last_writer[k])
                r = readers.get(k)
                if r:
                    for kk, v in r.items():
                        if kk == "dma":
                            d.update(v)
                        else:
                            d.add(v)
            if op.chan is not None:
                if op.chan in chan_last:
                    d.add(chan_last[op.chan])
                chan_last[op.chan] = op.idx
                chan_count[op.chan] = chan_count.get(op.chan, 0) + 1
                op.ordinal = chan_count[op.chan]
            d.discard(op.idx)
            for k in op.reads:
                r = readers.setdefault(k, {})
                if op.chan is not None:
                    r.setdefault("dma", []).append(op.idx)
                else:
                    r[op.eng] = op.idx
            for k in op.writes:
                last_writer[k] = op.idx
                readers[k] = {}
            if op.chan is None:
                eng_last[op.eng] = op.idx
        for op in ops:
            for j in op.deps:
                dj = ops[j]
                if dj.chan is None:
                    if dj.eng != op.eng or op.chan is not None:
                        dj.needs_sig = True
                    elif self.same_engine_sync and dj.eng != "pe":
                        dj.needs_sig = True
        cnt = {e: 0 for e in COMPUTE}
        for op in ops:
            if op.chan is None and op.needs_sig:
                assert op.fn is not None
                cnt[op.eng] += 1
                op.sig = cnt[op.eng]
        self.sig_counts = cnt
        self.chans = sorted(chan_count.keys())
        waited = {e: {} for e in ALLENG}
        for op in ops:
            w = waited[op.eng]
            need = {}
            for j in op.deps:
                dj = ops[j]
                if dj.chan is not None:
                    key = ("c", dj.chan)
                    val = 16 * dj.ordinal
                else:
                    if dj.sig is None:
                        continue
                    if dj.eng == op.eng and op.chan is None and (
                            not self.same_engine_sync or dj.eng == "pe"):
                        continue
                    ep = (dj.sig - 1) // self.EPOCH
                    key = ("e", dj.eng, ep)
                    val = dj.sig - ep * self.EPOCH
                if need.get(key, 0) < val:
                    need[key] = val
            for key, val in need.items():
                if w.get(key, 0) < val:
                    w[key] = val
                    op.waits.append((key, val))

    def emit(self):
        nc = self.nc
        self.analyze()
        sems = {}
        with ExitStack() as es:
            for e in COMPUTE:
                nep = (self.sig_counts[e] + self.EPOCH - 1) // self.EPOCH
                for ep in range(max(nep, 1)):
                    sems[("e", e, ep)] = es.enter_context(nc.semaphore(f"s_{e}_{ep}"))
            for c in self.chans:
                sems[("c", c)] = es.enter_context(nc.semaphore(f"c_{c}"))
            self.nsems = len(sems)
            block = es.enter_context(nc.Block())
            per_eng = {e: [op for op in self.ops if op.eng == e] for e in ALLENG}
            EP = self.EPOCH

            def run(engobj, lst):
                for op in lst:
                    for key, val in op.waits:
                        engobj.wait_ge(sems[key], val)
                    if op.fn is None:
                        continue
                    ins = op.fn(engobj)
                    if op.chan is not None:
                        ins.then_inc(sems[("c", op.chan)], 16)
                    elif op.sig is not None:
                        ep = (op.sig - 1) // EP
                        ins.then_inc(sems[("e", op.eng, ep)], 1)

            @block.tensor
            def _(e):
                run(e, per_eng["pe"])

            @block.scalar
            def _(e):
                run(e, per_eng["act"])

            @block.vector
            def _(e):
                run(e, per_eng["dve"])

            @block.gpsimd
            def _(e):
                run(e, per_eng["pool"])

            @block.sync
            def _(e):
                run(e, per_eng["sp"])


class Cfg:
    def __init__(self, D=4096, TT=2048, G=8, NMEM=256, DFF=11008, stop_after=None):
        self.D = D
        self.TT = TT
        self.G = G
        self.H = 8 * G
        self.DI = 512 * G
        self.CD = self.DI + 2 * G * 128
        self.NH = D // 128
        self.NMEM = NMEM
        self.DFF = DFF
        self.XH = 4
        self.XD = D // 4
        sizes = (self.DI, self.CD, self.H, D, D, D, 2 * D)
        self.off = np.concatenate([[0], np.cumsum(sizes)]).astype(int)
        self.NIN = int(self.off[-1])
        self.stop_after = stop_after


def make_consts():
    c = {}
    c["ident"] = np.eye(128, dtype=np.float32)
    i = np.arange(128)
    c["ones"] = np.ones((128, 128), np.float32)
    c["triinc"] = (i[:, None] <= i[None, :]).astype(np.float32)
    c["negm"] = np.where(i[None, :] < i[:, None], NEG, 0.0).astype(np.float32)
    c["nstrict"] = -(i[:, None] > i[None, :]).astype(np.float32)
    t = np.arange(512)
    m = np.stack([((kb * 128 + i)[:, None] < t[None, :]).astype(np.float32) for kb in range(4)])
    c["m01"] = m.transpose(1, 0, 2).reshape(128, 4 * 512)
    c["mneg"] = ((m - 1.0) * (-NEG)).transpose(1, 0, 2).reshape(128, 4 * 512)
    names = ["ident", "ones", "triinc", "negm", "nstrict"]
    offs = {}
    o = 0
    for n in names:
        offs[n] = (o, c[n].shape[1])
        o += c[n].shape[1]
    packed = np.concatenate([c[n] for n in names], axis=1).astype(np.float32)
    masks = np.concatenate([c["m01"], c["mneg"]], axis=1).astype(np.float32)
    return packed, offs, masks


class Builder:
    def __init__(self, cfg):
        self.cfg = cfg
        self.nc = bass.Bass("TRN2", target_bir_lowering=False)
        self.P = Prog(self.nc, same_engine_sync=SAME_ENGINE_SYNC)
        self.arena_top = 0
        self.arena_base = 0
        self.uid = 0
        self.dram = {}
        self.psum = []
        self.done = False

    def sb(self, name, shape, dt):
        nbytes = int(np.prod(shape[1:])) * (4 if dt == F32 else 2)
        nbytes = (nbytes + 63) // 64 * 64
        self.uid += 1
        t = self.nc.alloc_sbuf_tensor_at(f"{name}_{self.uid}", list(shape), dt, offset=self.arena_top)
        self.arena_top += nbytes
        assert self.arena_top <= self.sb_limit, (name, self.arena_top, self.sb_limit)
        return t

    def phase_reset(self):
        self.P.barrier()
        self.arena_top = self.arena_base

    def dr(self, name, shape, dt):
        t = self.nc.dram_tensor(name, list(shape), dt, kind="Internal")
        self.dram[name] = t
        return t

    def dma(self, q, out, in_, reads, writes, chan, slow=False):
        if slow:
            self.P.add(q, lambda e: e.dma_start(out=out, in_=in_, allow_slow_non_contiguous=True),
                       reads, writes, chan=chan)
        else:
            self.P.add(q, lambda e: e.dma_start(out=out, in_=in_), reads, writes, chan=chan)

    def dma_tiles(self, q, sb_tile, dram, k0, kn, tsl, reads, writes, chans, store=False, step=8):
        for i, a in enumerate(range(0, kn, step)):
            n = min(step, kn - a)
            d = dram[k0 + a:k0 + a + n, :, tsl].rearrange("k p t -> p k t")
            t_ = sb_tile[:, a:a + n, :]
            ch = chans[i % len(chans)]
            if store:
                self.dma(q, d, t_, reads, writes, ch)
            else:
                self.dma(q, t_, d, reads, writes, ch)

    def mm(self, out, lhsT, rhs, start, stop, reads, writes, **kw):
        self.P.add("pe", lambda e: e.matmul(out, lhsT, rhs, start=start, stop=stop, **kw), reads, writes)

    def tr(self, out, in_, ident, reads, writes):
        self.P.add("pe", lambda e: e.transpose(out, in_, ident), reads, writes)

    def act(self, out, in_, func, reads, writes, bias=None, scale=None, accum_out=None, eng="act"):
        kw = {}
        if bias is not None:
            kw["bias"] = bias
        if scale is not None:
            kw["scale"] = scale
        if accum_out is not None:
            kw["accum_out"] = accum_out
        self.P.add("act", lambda e: e.activation(out=out, in_=in_, func=func, **kw), reads, writes)

    def tt(self, eng, out, in0, in1, op, reads, writes):
        self.P.add(eng, lambda e: e.tensor_tensor(out=out, in0=in0, in1=in1, op=op), reads, writes)

    def ts(self, eng, out, in0, s1, s2, op0, op1, reads, writes):
        if op1 is None:
            self.P.add(eng, lambda e: e.tensor_scalar(out=out, in0=in0, scalar1=s1, scalar2=None, op0=op0),
                       reads, writes)
        else:
            self.P.add(eng, lambda e: e.tensor_scalar(out=out, in0=in0, scalar1=s1, scalar2=s2, op0=op0, op1=op1),
                       reads, writes)

    def stt(self, out, in0, scalar, in1, op0, op1, reads, writes):
        self.P.add("dve", lambda e: e.scalar_tensor_tensor(out=out, in0=in0, scalar=scalar, in1=in1,
                                                           op0=op0, op1=op1), reads, writes)

    def cp(self, eng, out, in_, reads, writes):
        if eng == "act":
            self.P.add("act", lambda e: e.copy(out=out, in_=in_), reads, writes)
        else:
            self.P.add(eng, lambda e: e.tensor_copy(out=out, in_=in_), reads, writes)

    def memset(self, eng, ap, val, writes):
        self.P.add(eng, lambda e: e.memset(ap, val), (), writes)

    def build(self):
        cfg = self.cfg
        nc = self.nc
        D, TT, G, H, DI, CD, NH = cfg.D, cfg.TT, cfg.G, cfg.H, cfg.DI, cfg.CD, cfg.NH
        KT = D // 128

        early = cfg.stop_after in ("tin", "pinproj", "pssd", "inproj", "ssd", "attn", "ssdonly")
        self.tiny = set()

        def ein(name, shape):
            if early and name in ("w_ffn_in", "w_ffn_out", "w_xa_kv", "w_xa_q", "w_xa_o", "w_out", "w_ssd_out",
                                  "w_sb_out") or (cfg.stop_after == "ssdonly" and name == "w_in"):
                shape = [128, 128]
                self.tiny.add(name)
            return nc.dram_tensor(name, list(shape), F32, kind="ExternalInput")

        self.packed, self.coffs, self.masks = make_consts()
        I = {}
        I["x_prev"] = ein("x_prev", [TT, D])
        I["x_own"] = ein("x_own", [TT, D])
        I["mem"] = ein("mem", [cfg.NMEM, D])
        I["flag"] = ein("flag", [128, 1])
        I["consts"] = ein("consts", list(self.packed.shape))
        I["cmask"] = ein("cmask", list(self.masks.shape))
        I["norm_mix"] = ein("norm_mix", [D])
        I["w_in"] = ein("w_in", [D, cfg.NIN])
        I["b_gate"] = ein("b_gate", [2 * D])
        I["conv_w"] = ein("conv_w", [4, CD])
        I["conv_b"] = ein("conv_b", [CD])
        I["dt_bias"] = ein("dt_bias", [H])
        I["a_log"] = ein("a_log", [H])
        I["d_skip"] = ein("d_skip", [H])
        I["ssd_norm"] = ein("ssd_norm", [DI])
        I["w_ssd_out"] = ein("w_ssd_out", [DI, D])
        I["w_sb_out"] = ein("w_sb_out", [D, D])
        I["w_out"] = ein("w_out", [D, D])
        I["norm_xa"] = ein("norm_xa", [D])
        I["norm_mem"] = ein("norm_mem", [D])
        I["w_xa_q"] = ein("w_xa_q", [D, D])
        I["w_xa_kv"] = ein("w_xa_kv", [D, 2 * D])
        I["w_xa_o"] = ein("w_xa_o", [D, D])
        I["norm_ffn"] = ein("norm_ffn", [D])
        I["w_ffn_in"] = ein("w_ffn_in", [D, 2 * cfg.DFF])
        I["w_ffn_out"] = ein("w_ffn_out", [cfg.DFF, D])
        I["norm_final"] = ein("norm_final", [D])
        self.I = I
        self.out = nc.dram_tensor("out", [TT, D], F32, kind="ExternalOutput")
        self.dbg = None

        self.ps = [nc.alloc_psum_tensor(f"ps{i}", [128, 512], F32) for i in range(8)]
        self.psb = [p.bitcast(BF16) for p in self.ps]

        self.arena_top = (nc.sbuf_base + 63) // 64 * 64
        self.sb_limit = nc.sbuf_top - 64
        C = {}
        ncol = self.packed.shape[1]
        cst = self.sb("cst", [128, ncol], F32)
        self.dma("sp", cst[:, :], I["consts"][:, :], (), ("cst",), "ld0")

        def cs(name):
            o, n = self.coffs[name]
            return cst[:, o:o + n]
        self.cs = cs
        identb = self.sb("identb", [128, 128], BF16)
        self.cp("dve", identb[:, :], cs("ident"), ("cst",), ("identb",))
        onesb = self.sb("onesb", [128, 128], BF16)
        self.cp("dve", onesb[:, :], cs("ones"), ("cst",), ("onesb",))
        self.identb, self.onesb = identb, onesb
        nstrictb = self.sb("nstrictb", [128, 128], BF16)
        self.cp("dve", nstrictb[:, :], cs("nstrict"), ("cst",), ("nstrictb",))
        self.nstrictb = nstrictb
        nonesb = self.sb("nonesb", [128, 128], BF16)
        self.ts("dve", nonesb[:, :], cs("ones"), -1.0, None, ALU.mult, None, ("cst",), ("nonesb",))
        self.nonesb = nonesb
        flag = self.sb("flag", [128, 1], F32)
        self.dma("sp", flag[:, :], I["flag"][:, :], (), ("flag",), "ld1")
        self.flag = flag

        vstage = self.sb("vstage", [128, 128], F32)
        self.vcount = 0

        def colvec_into(dst_ap, src1d, n, key):
            nt = n // 128
            b = self.vcount % 2
            self.vcount += 1
            self.dma("sp", vstage[:nt, :], src1d.rearrange("(t p) -> t p", p=128), (), ("vstage",), "ld0")
            self.tr(self.ps[b][:, :nt], vstage[:nt, :], self.cs("ident")[:nt, :nt], ("vstage", "cst"), (("ps", b),))
            self.cp("dve", dst_ap, self.ps[b][:, :nt], (("ps", b),), (key,))

        def colvec(name, src, n, chan):
            t_ = self.sb(name, [128, n // 128], F32)
            colvec_into(t_[:, :], src[:], n, name)
            return (t_, name)
        self.colvec = colvec
        self.v_norm_mix = colvec("v_norm_mix", I["norm_mix"], D, "ld0")
        self.v_norm_xa = colvec("v_norm_xa", I["norm_xa"], D, "ld1")
        self.v_norm_mem = colvec("v_norm_mem", I["norm_mem"], D, "ld0")
        self.v_norm_ffn = colvec("v_norm_ffn", I["norm_ffn"], D, "ld1")
        self.v_norm_final = colvec("v_norm_final", I["norm_final"], D, "ld0")
        self.v_ssd_norm = colvec("v_ssd_norm", I["ssd_norm"], DI, "ld1")
        self.v_b_gate = colvec("v_b_gate", I["b_gate"], 2 * D, "ld0")
        self.v_conv_b = colvec("v_conv_b", I["conv_b"], CD, "ld1")
        v_conv_w = self.sb("v_conv_w", [128, 4, CD // 128], F32)
        for j in range(4):
            colvec_into(v_conv_w[:, j, :], I["conv_w"][j, :], CD, "v_conv_w")
        self.v_conv_w = v_conv_w
        hv = self.sb("hv", [H, 4], F32)
        for j, nm in enumerate(["dt_bias", "a_log", "d_skip"]):
            self.dma("sp", hv[:, j:j + 1], I[nm].rearrange("(h o) -> h o", o=1), (), ("hv",), "ld1", slow=True)
        self.hv = hv
        negA = self.sb("negA", [H, 1], F32)
        self.act(negA[:, :], hv[:, 1:2], AF.Exp, ("hv",), ("negA",))
        self.ts("dve", negA[:, :], negA[:, :], -1.0, None, ALU.mult, None, ("negA",), ("negA",))
        self.negA = negA
        self.halo = self.sb("halo", [128, CD // 128, 3], F32)
        self.memset("pool", self.halo[:, :, :], 0.0, ("halo",))
        self.arena_base = self.arena_top

        self.s_xT = self.dr("s_xT", [KT, 128, TT], F32)
        self.s_h = self.dr("s_h", [KT, 128, TT], BF16)
        self.s_sz = self.dr("s_sz", [DI // 128, 128, TT], BF16)
        self.s_xbc = self.dr("s_xbc", [CD // 128, 128, TT], BF16)
        self.s_q = self.dr("s_q", [NH, 128, TT], BF16)
        self.s_k = self.dr("s_k", [NH, 128, 2 * TT], BF16)
        self.s_vT = self.dr("s_vT", [NH, 128, 2 * TT], BF16)
        self.s_gate = self.dr("s_gate", [2 * KT, 128, TT], BF16)
        self.s_yn = self.dr("s_yn", [DI // 128, 128, TT], BF16)
        self.s_o = self.dr("s_o", [NH, 128, TT], BF16)
        self.s_bs = self.dr("s_bs", [KT, 128, TT], BF16)
        self.s_mg = self.dr("s_mg", [KT, 128, TT], BF16)
        self.s_dt = self.dr("s_dt", [2, H, TT], F32)
        self.s_state = self.dr("s_state", [128, H * 64], F32)
        self.s_act = self.dr("s_act", [cfg.DFF // 128, 128, TT], BF16)
        self.s_xq = self.dr("s_xq", [KT, 128, TT], BF16)
        self.s_xo = self.dr("s_xo", [KT, 128, TT], BF16)
        self.s_hm = self.dr("s_hm", [KT, 128, cfg.NMEM], BF16)
        self.s_mT = self.dr("s_mT", [KT, 128, cfg.NMEM], F32)
        self.s_mk = self.dr("s_mk", [KT, 128, cfg.NMEM], BF16)
        self.s_mv = self.dr("s_mv", [KT, 128, cfg.NMEM], BF16)

        self.main()
        self.P.emit()
        return nc

    def stop(self, name):
        if self.cfg.stop_after == name:
            self.done = True
        return self.done

    def main(self):
        cfg = self.cfg
        I = self.I
        if cfg.stop_after == "ssdonly":
            self.phase_transpose_in(I["x_own"], "own")
            self.phase_reset()
            zf = self.sb("zf", [128, cfg.TT], F32)
            zb = self.sb("zb", [128, cfg.TT], BF16)
            self.memset("pool", zf[:, :], 0.01, ("zf",))
            self.memset("pool", zb[:, :], 0.01, ("zb",))
            self.dma("sp", self.s_dt[0, :, :], zf[:cfg.H, :], ("zf",), ("s_dt",), "st0")
            self.ts("dve", zf[:, :], zf[:, :], -1.0, None, ALU.mult, None, ("zf",), ("zf",))
            self.dma("sp", self.s_dt[1, :, :], zf[:cfg.H, :], ("zf",), ("s_dt",), "st0")
            for k in range(cfg.CD // 128):
                self.dma("sp", self.s_xbc[k, :, :], zb[:, :], ("zb",), ("s_xbc",), "st1")
            import os
            for k in range(cfg.DI // 128):
                self.dma("sp", self.s_sz[k, :, :], zb[:, :], ("zb",), ("s_sz",), "st1")
            self.phase_ssd(prev=True)
            if os.environ.get("SSD_OWN"):
                self.phase_ssd(prev=False)
            self.phase_final(norm=False)
            return
        if cfg.stop_after == "tin":
            self.phase_transpose_in(I["x_own"], "own")
            self.phase_norm(self.v_norm_mix, "mix")
            self.phase_final(norm=True)
            return
        self.phase_transpose_in(I["x_prev"], "prev")
        self.phase_norm(self.v_norm_mix, "mix")
        self.phase_inproj(prev=True)
        if self.stop("pinproj"):
            return self.finish_debug()
        self.phase_ssd(prev=True)
        if self.stop("pssd"):
            return self.finish_debug()
        self.phase_transpose_in(I["x_own"], "own")
        self.phase_norm(self.v_norm_mix, "mix")
        self.phase_inproj(prev=False)
        if self.stop("inproj"):
            return self.finish_debug()
        self.phase_ssd(prev=False)
        if self.stop("ssd"):
            return self.finish_debug()
        self.phase_attn()
        if self.stop("attn"):
            return self.finish_debug()
        self.phase_merge()
        if self.stop("merge"):
            return self.finish_debug()
        self.phase_xattn()
        if self.stop("xattn"):
            return self.finish_debug()
        self.phase_ffn()
        self.phase_final()

    def finish_debug(self):
        self.phase_final(norm=False)

    def phase_transpose_in(self, x, tag, TT=None, dst=None):
        cfg = self.cfg
        D = cfg.D
        TT = TT or cfg.TT
        dst = dst if dst is not None else self.s_xT
        KT = D // 128
        self.phase_reset()
        xt = [self.sb(f"xt{i}", [128, D], F32) for i in range(2)]
        ot = [self.sb(f"xo{i}", [128, KT, 128], F32) for i in range(2)]
        ident = self.cs("ident")
        for tt in range(TT // 128):
            s = tt % 2
            self.dma("sp", xt[s][:, :], x[tt * 128:(tt + 1) * 128, :], (), (f"xt{s}",), f"ld{s}")
            for g in range(KT // 4):
                bank = (tt * (KT // 4) + g) % 8
                for j in range(4):
                    ft = g * 4 + j
                    self.tr(self.ps[bank][:, j * 128:(j + 1) * 128], xt[s][:, ft * 128:(ft + 1) * 128],
                            ident, (f"xt{s}", "cst"), (("ps", bank),))
                eng = "act" if g % 2 == 0 else "dve"
                self.cp(eng, ot[s][:, g * 4:(g + 1) * 4, :],
                        self.ps[bank][:, :].rearrange("p (j t) -> p j t", j=4),
                        (("ps", bank),), (f"xo{s}",))
            self.dma_tiles("sp", ot[s], dst, 0, KT, slice(tt * 128, (tt + 1) * 128), (f"xo{s}",), (dst.name,),
                           (f"st{s}",), store=True)

    def phase_norm(self, wv, tag, src=None, dst=None, ntok=None, D=None):
        cfg = self.cfg
        wvec, wkey = wv
        D = D or cfg.D
        TT = ntok or cfg.TT
        KT = D // 128
        src = src if src is not None else self.s_xT
        dst = dst if dst is not None else self.s_h
        skey = src.name
        dkey = dst.name
        self.phase_reset()
        xin = [self.sb(f"nx{i}", [128, TT], F32) for i in range(3)]
        sq = [self.sb(f"nsq{i}", [128, TT], F32) for i in range(2)]
        acc = self.sb("nacc", [128, TT], F32)
        rstd = self.sb("nrstd", [128, TT], F32)
        ho = [self.sb(f"nho{i}", [128, TT], BF16) for i in range(2)]
        for kt in range(KT):
            s = kt % 3
            self.dma("sp", xin[s][:, :], src[kt, :, :], (skey,), (f"nx{s}",), f"ld{s}")
            if kt == 0:
                self.act(acc[:, :], xin[s][:, :], AF.Square, (f"nx{s}",), ("nacc",))
            else:
                s2 = kt % 2
                self.act(sq[s2][:, :], xin[s][:, :], AF.Square, (f"nx{s}",), (f"nsq{s2}",))
                self.tt("pool", acc[:, :], acc[:, :], sq[s2][:, :], ALU.add, ("nacc", f"nsq{s2}"), ("nacc",))
        ones = self.cs("ones")
        nb = (TT + 511) // 512
        for b in range(nb):
            w = min(512, TT - b * 512)
            self.mm(self.ps[b][:, :w], ones, acc[:, b * 512:b * 512 + w], True, True,
                    ("cst", "nacc"), (("ps", b),))
            self.act(rstd[:, b * 512:b * 512 + w], self.ps[b][:, :w], AF.Ln, (("ps", b),), ("nrstd",),
                     bias=EPS, scale=1.0 / D)
        self.act(rstd[:, :], rstd[:, :], AF.Exp, ("nrstd",), ("nrstd",), scale=-0.5)
        for kt in range(KT):
            s = kt % 3
            s2 = kt % 2
            self.dma("sp", xin[s][:, :], src[kt, :, :], (skey,), (f"nx{s}",), f"ld{s}")
            self.stt(ho[s2][:, :], xin[s][:, :], wvec[:, kt:kt + 1], rstd[:, :], ALU.mult, ALU.mult,
                     (f"nx{s}", "nrstd", wkey), (f"nho{s2}",))
            self.dma("sp", dst[kt, :, :], ho[s2][:, :], (f"nho{s2}",), (dkey,), f"st{s2}")


    def phase_final(self, norm=True):
        cfg = self.cfg
        D, TT = cfg.D, cfg.TT
        KT = D // 128
        NTT = TT // 128
        src = self.s_xT
        self.phase_reset()
        xin = [self.sb(f"fx{i}", [128, TT], F32) for i in range(2)]
        sq = [self.sb(f"fsq{i}", [128, TT], F32) for i in range(2)]
        acc = self.sb("facc", [128, TT], F32)
        rstd = self.sb("frstd", [128, TT], F32)
        yk = [self.sb(f"fy{i}", [128, TT], F32) for i in range(2)]
        ot = [self.sb(f"fo{i}", [128, NTT, 128], F32) for i in range(2)]
        wvec, wkey = self.v_norm_final
        if norm:
            for kt in range(KT):
                s = kt % 2
                self.dma("sp", xin[s][:, :], src[kt, :, :], ("s_xT",), (f"fx{s}",), f"ld{s}")
                if kt == 0:
                    self.act(acc[:, :], xin[s][:, :], AF.Square, (f"fx{s}",), ("facc",))
                else:
                    self.act(sq[s][:, :], xin[s][:, :], AF.Square, (f"fx{s}",), (f"fsq{s}",))
                    self.tt("pool", acc[:, :], acc[:, :], sq[s][:, :], ALU.add, ("facc", f"fsq{s}"), ("facc",))
            ones = self.cs("ones")
            for b in range((TT + 511) // 512):
                w = min(512, TT - b * 512)
                self.mm(self.ps[b][:, :w], ones, acc[:, b * 512:b * 512 + w], True, True,
                        ("cst", "facc"), (("ps", b),))
                self.act(rstd[:, b * 512:b * 512 + w], self.ps[b][:, :w], AF.Ln, (("ps", b),), ("frstd",),
                         bias=EPS, scale=1.0 / D)
            self.act(rstd[:, :], rstd[:, :], AF.Exp, ("frstd",), ("frstd",), scale=-0.5)
        ident = self.cs("ident")
        for kt in range(KT):
            s = kt % 2
            self.dma("sp", xin[s][:, :], src[kt, :, :], ("s_xT",), (f"fx{s}",), f"ld{s}")
            if norm:
                self.stt(yk[s][:, :], xin[s][:, :], wvec[:, kt:kt + 1], rstd[:, :], ALU.mult, ALU.mult,
                         (f"fx{s}", "frstd", wkey), (f"fy{s}",))
                y, ykey = yk[s], f"fy{s}"
            else:
                y, ykey = xin[s], f"fx{s}"
            for g in range(NTT // 4):
                bank = (kt * (NTT // 4) + g) % 8
                for j in range(4):
                    tt = g * 4 + j
                    self.tr(self.ps[bank][:, j * 128:(j + 1) * 128], y[:, tt * 128:(tt + 1) * 128],
                            ident, (ykey, "cst"), (("ps", bank),))
                eng = "act" if g % 2 == 0 else "dve"
                self.cp(eng, ot[s][:, g * 4:(g + 1) * 4, :],
                        self.ps[bank][:, :].rearrange("p (j t) -> p j t", j=4),
                        (("ps", bank),), (f"fo{s}",))
            self.dma("sp", self.out[:, kt * 128:(kt + 1) * 128].rearrange("(t p) f -> p t f", p=128),
                     ot[s][:, :, :], (f"fo{s}",), ("out",), f"st{s}")
        self.P.add("sp", None, ("out",), ())


_CACHE = {}


def _get_nc(cfg_key, cfg):
    if cfg_key not in _CACHE:
        b = Builder(cfg)
        nc = b.build()
        _CACHE[cfg_key] = (nc, b)
    return _CACHE[cfg_key]


def run_cfg(cfg, inputs, n_batch, cfg_key):
    nc, b = _get_nc(cfg_key, cfg)
    TT, D = cfg.TT, cfg.D
    f32 = np.float32
    x = np.ascontiguousarray(inputs["x"], dtype=f32)
    mem = np.ascontiguousarray(inputs["mem"], dtype=f32)
    shared = {}
    for k in ["norm_mix", "w_in", "b_gate", "conv_w", "conv_b", "dt_bias", "a_log", "d_skip", "ssd_norm",
              "w_ssd_out", "w_sb_out", "w_out", "norm_xa", "norm_mem", "w_xa_q", "w_xa_kv", "w_xa_o",
              "norm_ffn", "w_ffn_in", "w_ffn_out"]:
        shared[k] = np.ascontiguousarray(np.asarray(inputs[k], dtype=f32)[0])
    shared["norm_final"] = np.ascontiguousarray(inputs["norm_final"], dtype=f32)
    for k in b.tiny:
        shared[k] = np.zeros((128, 128), f32)
    shared["consts"] = b.packed
    shared["cmask"] = b.masks
    zeros = np.zeros((TT, D), f32)
    in_maps = []
    ncores = 2 * n_batch
    for c in range(ncores):
        bi, half = c // 2, c % 2
        m = dict(shared)
        m["x_own"] = np.ascontiguousarray(x[bi, half * TT:(half + 1) * TT])
        m["x_prev"] = np.ascontiguousarray(x[bi, 0:TT]) if half == 1 else zeros
        m["mem"] = np.ascontiguousarray(mem[bi])
        m["flag"] = np.full((128, 1), float(half), f32)
        in_maps.append(m)
    res = run_bass_kernel_spmd(nc, in_maps, core_ids=list(range(ncores)))
    out = np.zeros((n_batch, 2 * TT, D), f32)
    for c in range(ncores):
        bi, half = c // 2, c % 2
        out[bi, half * TT:(half + 1) * TT] = np.asarray(res.results[c]["out"], dtype=f32)
    return out


def kernel(**inputs):
    cfg = Cfg()
    return run_cfg(cfg, inputs, 4, "full")


KCH = 16


def gemm_setup(self, K, TTg):
    KT = K // 128
    self.g_panel = self.sb("panel", [128, KT, TTg], BF16)
    self.g_st = [self.sb(f"gst{i}", [128, KCH, 128], F32) for i in range(3)]
    self.g_wb = [self.sb(f"gwb{i}", [128, KCH, 128], BF16) for i in range(3)]
    self.g_u = 0
    self.g_nt = 0
    self.g_TT = TTg
    self.g_KT = KT


def gemm_load_panel(self, src, skey, t0=0):
    KT, TTg = self.g_KT, self.g_TT
    step = 8
    for k0 in range(0, KT, step):
        kn = min(step, KT - k0)
        self.dma("sp", self.g_panel[:, k0:k0 + kn, :],
                 src[k0:k0 + kn, :, t0:t0 + TTg].rearrange("k p t -> p k t"),
                 (skey,), ("panel",), f"ld{(k0 // step) % 3}")


def gemm(self, W, tiles, epi):
    KT, TTg = self.g_KT, self.g_TT
    nb = (TTg + 511) // 512
    nsets = 8 // nb
    units = []
    for i, (c0, wd) in enumerate(tiles):
        nk = (KT + KCH - 1) // KCH
        for kc in range(nk):
            units.append((i, c0, wd, kc, kc == nk - 1))

    def load(u):
        i, c0, wd, kc, last = units[u]
        s = (self.g_u + u) % 3
        k0 = kc * KCH
        kn = min(KCH, KT - k0)
        self.dma("sp", self.g_st[s][:, :kn, :wd],
                 W[k0 * 128:(k0 + kn) * 128, c0:c0 + wd].rearrange("(kt p) n -> p kt n", p=128),
                 (), (f"gst{s}",), f"w{s}")
        self.cp("pool", self.g_wb[s][:, :kn, :wd], self.g_st[s][:, :kn, :wd], (f"gst{s}",), (f"gwb{s}",))

    LA = 2
    for u in range(min(LA, len(units))):
        load(u)
    for u in range(len(units)):
        if u + LA < len(units):
            load(u + LA)
        i, c0, wd, kc, last = units[u]
        s = (self.g_u + u) % 3
        k0 = kc * KCH
        kn = min(KCH, KT - k0)
        setn = (self.g_nt + i) % nsets
        banks = [setn * nb + b for b in range(nb)]
        for kt in range(kn):
            for b in range(nb):
                w = min(512, TTg - b * 512)
                self.mm(self.ps[banks[b]][:wd, :w], self.g_wb[s][:, kt, :wd],
                        self.g_panel[:, k0 + kt, b * 512:b * 512 + w],
                        (k0 + kt == 0), (k0 + kt == KT - 1),
                        (f"gwb{s}", "panel"), (("ps", banks[b]),))
        if last:
            epi(i, banks, wd)
    self.g_u += len(units)
    self.g_nt += len(tiles)


def phase_inproj(self, prev):
    cfg = self.cfg
    D, TT, G, H, DI, CD, NH = cfg.D, cfg.TT, cfg.G, cfg.H, cfg.DI, cfg.CD, cfg.NH
    off = cfg.off
    I = self.I
    self.phase_reset()
    gemm_setup(self, D, TT)
    ob = [self.sb(f"ob{i}", [128, TT], BF16) for i in range(2)]
    u_t = self.sb("cu", [128, TT + 3], F32)
    acc = self.sb("cacc", [128, TT], F32)
    gemm_load_panel(self, self.s_h, "s_h")
    koff = 0 if prev else TT
    tiles = []
    kinds = []

    def addseg(kind, seg, n):
        for j in range(0, n, 128):
            tiles.append((int(off[seg]) + j, min(128, n - j)))
            kinds.append((kind, j // 128))
    if not prev:
        addseg("z", 0, DI)
    addseg("xbc", 1, CD)
    addseg("dt", 2, H)
    if not prev:
        addseg("q", 3, D)
    addseg("k", 4, D)
    addseg("v", 5, D)
    if not prev:
        addseg("gate", 6, 2 * D)
    nb = (TT + 511) // 512
    cnt = [0]

    def evac(dst, banks, func, okey, wd=128, bias=None, scale=None):
        for b in range(nb):
            w = min(512, TT - b * 512)
            self.act(dst[:wd, b * 512:b * 512 + w], self.ps[banks[b]][:wd, :w], func,
                     (("ps", banks[b]),), (okey,), bias=bias, scale=scale)

    def epi(i, banks, wd):
        kind, idx = kinds[i]
        s = cnt[0] % 2
        cnt[0] += 1
        okey = f"ob{s}"
        o = ob[s]
        if kind == "z":
            evac(o, banks, AF.Silu, okey)
            self.dma("act", self.s_sz[idx, :, :], o[:, :], (okey,), ("s_sz",), f"st{s}")
        elif kind == "q":
            evac(o, banks, AF.Copy, okey, scale=float(128 ** -0.5))
            self.dma("act", self.s_q[idx, :, :], o[:, :], (okey,), ("s_q",), f"st{s}")
        elif kind == "k":
            evac(o, banks, AF.Copy, okey)
            self.dma("act", self.s_k[idx, :, koff:koff + TT], o[:, :], (okey,), ("s_k",), f"st{s}")
        elif kind == "v":
            if prev:
                evac(o, banks, AF.Copy, okey, scale=self.flag[:, 0:1])
            else:
                evac(o, banks, AF.Copy, okey)
            self.dma("act", self.s_vT[idx, :, koff:koff + TT], o[:, :], (okey,), ("s_vT",), f"st{s}")
        elif kind == "gate":
            evac(o, banks, AF.Sigmoid, okey, bias=self.v_b_gate[0][:, idx:idx + 1])
            self.dma("act", self.s_gate[idx, :, :], o[:, :], (okey,), ("s_gate",), f"st{s}")
        elif kind == "dt":
            evac(acc, banks, AF.Exp, "cacc", wd=H, bias=self.hv[:, 0:1])
            self.act(acc[:H, :], acc[:H, :], AF.Ln, ("cacc",), ("cacc",), bias=1.0)
            self.ts("dve", u_t[:H, :TT], acc[:H, :], self.negA[:, 0:1], None, ALU.mult, None,
                    ("cacc", "negA"), ("cu",))
            self.dma("act", self.s_dt[0, :, :], acc[:H, :], ("cacc",), ("s_dt",), "st0")
            self.dma("act", self.s_dt[1, :, :], u_t[:H, :TT], ("cu",), ("s_dt",), "st1")
        elif kind == "xbc":
            cw = self.v_conv_w
            self.cp("dve", u_t[:, 0:3], self.halo[:, idx, :], ("halo",), ("cu",))
            evac(u_t[:, 3:], banks, AF.Copy, "cu")
            self.cp("dve", self.halo[:, idx, :], u_t[:, TT:TT + 3], ("cu",), ("halo",))
            self.ts("dve", acc[:, :], u_t[:, 3:3 + TT], cw[:, 3, idx:idx + 1], self.v_conv_b[0][:, idx:idx + 1],
                    ALU.mult, ALU.add, ("cu", "v_conv_w", "v_conv_b"), ("cacc",))
            for j in (2, 1, 0):
                self.stt(acc[:, :], u_t[:, j:j + TT], cw[:, j, idx:idx + 1], acc[:, :], ALU.mult, ALU.add,
                         ("cu", "cacc", "v_conv_w"), ("cacc",))
            self.act(o[:, :], acc[:, :], AF.Silu, ("cacc",), (okey,))
            self.dma("act", self.s_xbc[idx, :, :], o[:, :], (okey,), ("s_xbc",), f"st{s}")

    gemm(self, I["w_in"], tiles, epi)


Builder.phase_inproj = phase_inproj


def phase_ssd(self, prev):
    cfg = self.cfg
    D, TT, G, H, DI, CD = cfg.D, cfg.TT, cfg.G, cfg.H, cfg.DI, cfg.CD
    NDI = DI // 128
    NCD = CD // 128
    BLK = min(TT, 256)
    NCH = BLK // 64
    self.phase_reset()
    sb = self.sb
    ident = self.cs("ident")
    ones = self.cs("ones")
    triinc = self.cs("triinc")
    negm = self.cs("negm")
    xbc = sb("xbc", [128, NCD, BLK], BF16)
    dtb = sb("dtb", [H, 2, BLK], F32)
    Hst = sb("Hst", [128, H * 64], F32)
    prevT = sb("prevT", [128, H * 64], BF16)
    X = sb("X", [64, DI], BF16)
    Btok = sb("Btok", [64, G * 128], BF16)
    dts = sb("dts", [64, 2 * H], F32)
    acum = sb("acum", [64, H], F32)
    cdec = sb("cdec", [128, H], F32)
    w2 = sb("w2", [64, H], F32)
    Xw = [sb(f"Xw{i}", [64, 8, 64], BF16) for i in range(2)]
    if not prev:
        szb = sb("szb", [128, NDI, BLK], BF16)
        gbuf = sb("gbuf", [128, NDI, BLK], F32)
        ynb = sb("ynb", [128, NDI, BLK], BF16)
        sq = sb("sq", [128, 4, BLK], BF16)
        rstd = sb("rstd", [128, BLK], F32)
        Dg = [sb(f"Dg{i}", [64, 8, 64], F32) for i in range(2)]
        Eg = [sb(f"Eg{i}", [64, 8, 64], F32) for i in range(2)]
        EA = [sb(f"EA{i}", [128, 8, 64], BF16) for i in range(2)]
        LT = [sb(f"LT{i}", [64, 8, 64], BF16) for i in range(2)]
        MT = [sb(f"MT{i}", [64, 8, 64], BF16) for i in range(2)]
        Cs = [sb(f"Cs{i}", [128, 8, 64], BF16) for i in range(2)]
        Xd = [sb(f"Xd{i}", [64, 8, 64], BF16) for i in range(2)]
        cb = [sb(f"cb{i}", [64, 64], F32) for i in range(2)]
        ysb = [sb(f"ysb{i}", [64, 512], F32) for i in range(2)]
        DSI = sb("DSI", [64, H, 64], BF16)
        dsr = sb("dsr", [64, H], F32)
        d2 = sb("d2", [H, H], F32)
        self.ts("dve", d2[:, :], ident[:H, :H], self.hv[:, 2:3], None, ALU.mult, None, ("cst", "hv"), ("d2",))
        self.mm(self.ps[0][:64, :H], ones[:H, :64], d2[:, :], True, True, ("cst", "d2"), (("ps", 0),))
        self.cp("dve", dsr[:, :], self.ps[0][:64, :H], (("ps", 0),), ("dsr",))
        self.tt("dve", DSI[:, :, :], ident[:64, :64].unsqueeze(1).to_broadcast([64, H, 64]),
                dsr[:, :].unsqueeze(2).to_broadcast([64, H, 64]), ALU.mult, ("cst", "dsr"), ("DSI",))
    if prev:
        self.memset("pool", Hst[:, :], 0.0, ("Hst",))
    else:
        self.dma("sp", Hst[:, :], self.s_state[:, :], ("s_state",), ("Hst",), "ld0")
        self.ts("dve", Hst[:, :], Hst[:, :], self.flag[:, 0:1], None, ALU.mult, None, ("Hst", "flag"), ("Hst",))
        self.cp("act", prevT[:, :], Hst[:, :], ("Hst",), ("prevT",))

    def bc_h(ap2, n=8):
        return ap2.unsqueeze(2).to_broadcast([ap2.shape[0], n, 64])

    def bc_m(ap2, n=8):
        return ap2.unsqueeze(1).to_broadcast([ap2.shape[0], n, 64])

    it = 0
    import os
    NBLK_ = int(os.environ.get("SSD_NBLK", TT // BLK))
    SEC_ = int(os.environ.get("SSD_SEC", 9))
    for blk in range(min(NBLK_, TT // BLK)):
        t0 = blk * BLK
        tiles_needed = list(range(NCD)) if not prev else list(range(NDI + G))
        nld = len(tiles_needed)
        self.dma_tiles("sp", xbc, self.s_xbc, 0, nld, slice(t0, t0 + BLK), ("s_xbc",), ("xbc",), ("ld0", "ld2"))
        self.dma("sp", dtb[:, :, :], self.s_dt[:, :, t0:t0 + BLK].rearrange("a h t -> h a t"),
                 ("s_dt",), ("dtb",), "ld1")
        if not prev:
            self.dma_tiles("sp", szb, self.s_sz, 0, NDI, slice(t0, t0 + BLK), ("s_sz",), ("szb",), ("ld2", "ld1"))
        for ch in range(NCH if SEC_ > 0 else 0):
            lo = ch * 64
            self.tr(self.ps[0][:64, 0:H], dtb[:, 0, lo:lo + 64], ident[:H, :H], ("dtb", "cst"), (("ps", 0),))
            self.tr(self.ps[0][:64, H:2 * H], dtb[:, 1, lo:lo + 64], ident[:H, :H], ("dtb", "cst"), (("ps", 0),))
            self.cp("dve", dts[:, :], self.ps[0][:64, :2 * H], (("ps", 0),), ("dts",))
            if SEC_ < 2:
                continue
            self.mm(self.ps[1][:64, 0:H], triinc[:64, :64], dts[:, H:2 * H], True, True, ("cst", "dts"), (("ps", 1),))
            self.cp("act", acum[:, :], self.ps[1][:64, 0:H], (("ps", 1),), ("acum",))
            if SEC_ < 3:
                continue
            self.mm(self.ps[3][:, 0:2 * H], ones[:64, :], dts[:, 0:2 * H], True, True, ("cst", "dts"), (("ps", 3),))
            if SEC_ < 4:
                continue
            self.act(cdec[:, :], self.ps[3][:, H:2 * H], AF.Exp, (("ps", 3),), ("cdec",))
            if SEC_ < 5:
                continue
            self.cp("act", w2[:, :], self.ps[3][:64, H:2 * H], (("ps", 3),), ("w2",))
            self.tt("dve", w2[:, :], w2[:, :], acum[:, :], ALU.subtract, ("w2", "acum"), ("w2",))
            if SEC_ < 6:
                continue
            self.act(w2[:, :], w2[:, :], AF.Exp, ("w2",), ("w2",))
            if SEC_ < 7:
                continue
            self.tt("dve", w2[:, :], w2[:, :], dts[:, 0:H], ALU.mult, ("w2", "dts"), ("w2",))
            if SEC_ < 8:
                continue
            for c8 in range(0, NDI, 8):
                n8 = min(8, NDI - c8)
                for j in range(n8):
                    self.tr(self.psb[2][:64, j * 128:(j + 1) * 128], xbc[:, c8 + j, lo:lo + 64], self.identb[:, :],
                            ("xbc", "identb"), (("ps", 2),))
                self.cp("act" if (c8 // 8) % 2 == 0 else "dve", X[:, c8 * 128:(c8 + n8) * 128],
                        self.psb[2][:64, :n8 * 128], (("ps", 2),), ("X",))
            for g in range(G):
                self.tr(self.psb[2][:64, g * 128:(g + 1) * 128], xbc[:, NDI + g, lo:lo + 64], self.identb[:, :],
                        ("xbc", "identb"), (("ps", 2),))
            self.cp("dve", Btok[:, :], self.psb[2][:64, :G * 128], (("ps", 2),), ("Btok",))
            for g in range(G if SEC_ > 8 else 0):
                s = it % 2
                it += 1
                hs = slice(8 * g, 8 * g + 8)
                Xg = X[:, g * 512:(g + 1) * 512].rearrange("p (h e) -> p h e", h=8)
                if not prev:
                    Bt = xbc[:, NDI + g, lo:lo + 64]
                    Ct = xbc[:, NDI + G + g, lo:lo + 64]
                    self.tt("dve", Dg[s][:, :, :], bc_h(acum[:, hs]), bc_m(ident[:64, :64]), ALU.mult,
                            ("acum", "cst"), (f"Dg{s}",))
                    self.tt("dve", Eg[s][:, :, :], bc_m(negm[:64, :64]), bc_h(acum[:, hs]), ALU.subtract,
                            ("acum", "cst"), (f"Eg{s}",))
                    Dg2 = Dg[s][:, :, :].rearrange("p h l -> p (h l)")
                    Eg2 = Eg[s][:, :, :].rearrange("p h l -> p (h l)")
                    self.mm(self.ps[3][:, :], ones[:64, :], Dg2, True, True, ("cst", f"Dg{s}"), (("ps", 3),))
                    self.mm(self.ps[4][:64, :], ones[:64, :64], Dg2, True, False, ("cst", f"Dg{s}"), (("ps", 4),))
                    self.mm(self.ps[4][:64, :], ident[:64, :64], Eg2, False, True, ("cst", f"Eg{s}"), (("ps", 4),))
                    self.act(EA[s][:, :, :].rearrange("p h l -> p (h l)"), self.ps[3][:, :], AF.Exp,
                             (("ps", 3),), (f"EA{s}",))
                    self.act(LT[s][:, :, :].rearrange("p h l -> p (h l)"), self.ps[4][:64, :], AF.Exp,
                             (("ps", 4),), (f"LT{s}",))
                    self.mm(self.ps[5][:64, 0:64], Bt, Ct, True, True, ("xbc",), (("ps", 5),))
                    self.cp("act", cb[s][:, :], self.ps[5][:64, 0:64], (("ps", 5),), (f"cb{s}",))
                    self.tt("dve", MT[s][:, :, :], LT[s][:, :, :], bc_m(cb[s][:, :]), ALU.mult,
                            (f"LT{s}", f"cb{s}"), (f"MT{s}",))
                    self.tt("pool", Cs[s][:, :, :], EA[s][:, :, :], bc_m(Ct), ALU.mult,
                            (f"EA{s}", "xbc"), (f"Cs{s}",))
                    self.tt("pool", Xd[s][:, :, :], Xg, bc_h(dts[:, hs]), ALU.mult, ("X", "dts"), (f"Xd{s}",))
                self.tt("dve", Xw[s][:, :, :], Xg, bc_h(w2[:, hs]), ALU.mult, ("X", "w2"), (f"Xw{s}",))
                if not prev:
                    for h in range(8):
                        hg = 8 * g + h
                        yo = self.ps[6][:64, h * 64:(h + 1) * 64]
                        self.mm(yo, MT[s][:, h, :], Xd[s][:, h, :], True, False, (f"MT{s}", f"Xd{s}"), (("ps", 6),))
                        self.mm(yo, DSI[:, hg, :], Xg[:, h, :], False, False, ("DSI", "X"), (("ps", 6),))
                        self.mm(yo, Cs[s][:, h, :], prevT[:, hg * 64:(hg + 1) * 64], False, True,
                                (f"Cs{s}", "prevT"), (("ps", 6),))
                    self.cp("act", ysb[s][:, :], self.ps[6][:64, :], (("ps", 6),), (f"ysb{s}",))
                    for j in range(4):
                        self.tr(self.ps[5][:, 128 + j * 64:128 + (j + 1) * 64], ysb[s][:, j * 128:(j + 1) * 128],
                                ident[:64, :64], (f"ysb{s}", "cst"), (("ps", 5),))
                    self.tt("dve", gbuf[:, 4 * g:4 * g + 4, lo:lo + 64],
                            self.ps[5][:, 128:384].rearrange("p (j l) -> p j l", j=4),
                            szb[:, 4 * g:4 * g + 4, lo:lo + 64], ALU.mult, (("ps", 5), "szb"), ("gbuf",))
                self.mm(self.ps[7][:, :], Btok[:, g * 128:(g + 1) * 128], Xw[s][:, :, :].rearrange("p h e -> p (h e)"),
                        True, True, ("Btok", f"Xw{s}"), (("ps", 7),))
                Hg = Hst[:, g * 512:(g + 1) * 512]
                Hg3 = Hg.rearrange("p (h e) -> p h e", h=8)
                self.tt("dve", Hg3, Hg3, bc_h(cdec[:, hs]), ALU.mult, ("Hst", "cdec"), ("Hst",))
                self.tt("dve", Hg, Hg, self.ps[7][:, :], ALU.add, ("Hst", ("ps", 7)), ("Hst",))
                if not prev:
                    self.cp("act", prevT[:, g * 512:(g + 1) * 512], Hg, ("Hst",), ("prevT",))
        if not prev:
            vn, vkey = self.v_ssd_norm
            for g in range(G):
                self.act(sq[:, :, :], gbuf[:, 4 * g:4 * g + 4, :], AF.Square, ("gbuf",), ("sq",))
                for j in range(4):
                    self.mm(self.ps[1][:, :BLK], self.onesb[:, :], sq[:, j, :], j == 0, j == 3,
                            ("onesb", "sq"), (("ps", 1),))
                self.act(rstd[:, :], self.ps[1][:, :BLK], AF.Ln, (("ps", 1),), ("rstd",), bias=EPS, scale=1.0 / 512)
                self.act(rstd[:, :], rstd[:, :], AF.Exp, ("rstd",), ("rstd",), scale=-0.5)
                for j in range(4):
                    ct = 4 * g + j
                    self.stt(ynb[:, ct, :], gbuf[:, ct, :], vn[:, ct:ct + 1], rstd[:, :], ALU.mult, ALU.mult,
                             ("gbuf", "rstd", vkey), ("ynb",))
            self.dma_tiles("act", ynb, self.s_yn, 0, NDI, slice(t0, t0 + BLK), ("ynb",), ("s_yn",), ("st0", "st1"),
                           store=True)
    if prev:
        self.dma("act", self.s_state[:, :], Hst[:, :], ("Hst",), ("s_state",), "st1")


Builder.phase_ssd = phase_ssd


def phase_attn(self):
    cfg = self.cfg
    D, TT, NH = cfg.D, cfg.TT, cfg.NH
    self.phase_reset()
    sb = self.sb
    NQG = TT // 512
    NPB = TT // 128
    m01f = sb("m01f", [128, 4, 512], F32)
    mneg = sb("mneg", [128, 4, 512], F32)
    m01 = sb("m01", [128, 4, 512], BF16)
    self.dma("sp", m01f[:, :, :], self.I["cmask"][:, 0:2048].rearrange("p (k t) -> p k t", k=4), (), ("m01f",), "ld0")
    self.dma("sp", mneg[:, :, :], self.I["cmask"][:, 2048:4096].rearrange("p (k t) -> p k t", k=4), (), ("mneg",), "ld1")
    self.cp("dve", m01[:, :, :], m01f[:, :, :], ("m01f",), ("m01",))
    Kh = [sb(f"Kh{i}", [128, 2 * TT], BF16) for i in range(2)]
    Qh = [sb(f"Qh{i}", [128, TT], BF16) for i in range(2)]
    Vs = [sb(f"Vs{i}", [128, 2 * TT], BF16) for i in range(2)]
    Vh = [sb(f"Vh{i}", [128, 2 * TT // 128, 128], BF16) for i in range(2)]
    oT = [sb(f"oT{i}", [128, TT], BF16) for i in range(2)]
    et = [sb(f"et{i}", [128, 512], F32) for i in range(2)]
    sp_ = [sb(f"sp{i}", [128, 512], BF16) for i in range(3)]
    T2 = [sb(f"T2{i}", [128, 512], F32) for i in range(2)]
    Wt = [sb(f"Wt{i}", [128, 512], BF16) for i in range(2)]
    At = [sb(f"At{i}", [128, 512], BF16) for i in range(2)]
    it = 0
    og = 0
    for h in range(NH):
        s = h % 2
        self.dma("sp", Kh[s][:, :], self.s_k[h, :, :], ("s_k",), (f"Kh{s}",), f"ld{s}")
        self.dma("sp", Qh[s][:, :], self.s_q[h, :, :], ("s_q",), (f"Qh{s}",), f"ld{2}")
        self.dma("sp", Vs[s][:, :], self.s_vT[h, :, :], ("s_vT",), (f"Vs{s}",), f"ld{s}")
        nkb = 2 * TT // 128
        for k8 in range(0, nkb, 8):
            for j in range(8):
                kb = k8 + j
                self.tr(self.psb[4][:, j * 128:(j + 1) * 128], Vs[s][:, kb * 128:(kb + 1) * 128], self.identb[:, :],
                        (f"Vs{s}", "identb"), (("ps", 4),))
            self.cp("dve", Vh[s][:, k8:k8 + 8, :], self.psb[4][:, :].rearrange("p (j d) -> p j d", j=8),
                    (("ps", 4),), (f"Vh{s}",))
        for qg in range(NQG):
            nk = NPB + 4 * (qg + 1)
            ob = 2 + (og % 2)
            og += 1
            acur = None
            for idx, kb in enumerate(range(nk - 1, -1, -1)):
                first = idx == 0
                last = kb == 0
                kk = kb - (NPB + 4 * qg)
                z = it % 2
                sp_i = it % 3
                it += 1
                zb = self.ps[z]
                self.mm(zb[:, :], Kh[s][:, kb * 128:(kb + 1) * 128], Qh[s][:, qg * 512:(qg + 1) * 512], True, False,
                        (f"Kh{s}", f"Qh{s}"), (("ps", z),))
                self.act(et[z][:, :], zb[:, :], AF.Exp, (("ps", z),), (f"et{z}",))
                self.act(sp_[sp_i][:, :], et[z][:, :], AF.Ln, (f"et{z}",), (f"sp{sp_i}",), bias=1.0)
                if kk >= 0:
                    self.tt("pool", sp_[sp_i][:, :], sp_[sp_i][:, :], m01[:, kk, :], ALU.mult,
                            (f"sp{sp_i}", "m01"), (f"sp{sp_i}",))
                self.mm(zb[:, :], self.nstrictb[:, :], sp_[sp_i][:, :], False, first,
                        ("nstrictb", f"sp{sp_i}"), (("ps", z),))
                if not first:
                    self.mm(zb[:, :], self.nonesb[:, :], acur[0][:, :], False, True,
                            ("nonesb", acur[1]), (("ps", z),))
                self.tt("dve", T2[z][:, :], zb[:, :], sp_[sp_i][:, :], ALU.subtract,
                        (("ps", z), f"sp{sp_i}"), (f"T2{z}",))
                if kk >= 0:
                    self.tt("pool", T2[z][:, :], T2[z][:, :], mneg[:, kk, :], ALU.add, (f"T2{z}", "mneg"), (f"T2{z}",))
                self.act(Wt[z][:, :], T2[z][:, :], AF.Exp, (f"T2{z}",), (f"Wt{z}",))
                self.mm(self.ps[ob][:, :], Vh[s][:, kb, :], Wt[z][:, :], first, last,
                        (f"Vh{s}", f"Wt{z}"), (("ps", ob),))
                if not last:
                    if first:
                        acur = (sp_[sp_i], f"sp{sp_i}")
                    else:
                        a = (it) % 2
                        self.tt("pool", At[a][:, :], acur[0][:, :], sp_[sp_i][:, :], ALU.add,
                                (acur[1], f"sp{sp_i}"), (f"At{a}",))
                        acur = (At[a], f"At{a}")
            self.act(oT[s][:, qg * 512:(qg + 1) * 512], self.ps[ob][:, :], AF.Copy, (("ps", ob),), (f"oT{s}",))
        self.dma("act", self.s_o[h, :, :], oT[s][:, :], (f"oT{s}",), (("s_o", h),), f"st{s}")


Builder.phase_attn = phase_attn


def res_epi(self, xo, t0, TTg):
    nb = (TTg + 511) // 512
    cnt = [0]

    def epi(i, banks, wd):
        s = cnt[0] % 2
        cnt[0] += 1
        self.dma("sp", xo[s][:, :TTg], self.s_xT[i, :, t0:t0 + TTg], (("s_xT", i, t0),), (f"xo{s}",), f"ld{s}")
        for b in range(nb):
            w = min(512, TTg - b * 512)
            self.tt("dve", xo[s][:, b * 512:b * 512 + w], xo[s][:, b * 512:b * 512 + w], self.ps[banks[b]][:, :w],
                    ALU.add, (f"xo{s}", ("ps", banks[b])), (f"xo{s}",))
        self.dma("act", self.s_xT[i, :, t0:t0 + TTg], xo[s][:, :TTg], (f"xo{s}",), (("s_xT", i, t0),), f"st{s}")
    return epi


def phase_merge(self):
    cfg = self.cfg
    D, TT, DI = cfg.D, cfg.TT, cfg.DI
    KT = D // 128
    I = self.I
    assert DI == D
    nb = (TT + 511) // 512
    tiles = [(j * 128, 128) for j in range(KT)]
    cnt = [0]
    self.phase_reset()
    gemm_setup(self, D, TT)
    ob = [self.sb(f"ob{i}", [128, TT], BF16) for i in range(2)]
    gemm_load_panel(self, self.s_yn, "s_yn")

    def epi1(i, banks, wd):
        s = cnt[0] % 2
        cnt[0] += 1
        for b in range(nb):
            w = min(512, TT - b * 512)
            self.act(ob[s][:, b * 512:b * 512 + w], self.ps[banks[b]][:, :w], AF.Copy, (("ps", banks[b]),), (f"ob{s}",))
        self.dma("act", self.s_bs[i, :, :], ob[s][:, :], (f"ob{s}",), (("s_bs", i),), f"st{s}")
    gemm(self, I["w_ssd_out"], tiles, epi1)
    self.phase_reset()
    gemm_setup(self, D, TT)
    ob2 = [self.sb(f"ob{i}", [128, TT], BF16) for i in range(2)]
    g1 = self.sb("g1", [128, TT], BF16)
    g2 = self.sb("g2", [128, TT], BF16)
    bsb = self.sb("bsb", [128, TT], BF16)
    t1 = self.sb("t1", [128, TT], F32)
    gemm_load_panel(self, self.s_o, "s_o")

    def epi2(i, banks, wd):
        s = cnt[0] % 2
        cnt[0] += 1
        self.dma("sp", g1[:, :], self.s_gate[i, :, :], ("s_gate",), ("g1",), "ld0")
        self.dma("sp", g2[:, :], self.s_gate[KT + i, :, :], ("s_gate",), ("g2",), "ld1")
        self.dma("sp", bsb[:, :], self.s_bs[i, :, :], ("s_bs",), ("bsb",), "ld2")
        self.tt("pool", t1[:, :], g1[:, :], bsb[:, :], ALU.mult, ("g1", "bsb"), ("t1",))
        for b in range(nb):
            w = min(512, TT - b * 512)
            sl = slice(b * 512, b * 512 + w)
            self.tt("dve", g2[:, sl], g2[:, sl], self.ps[banks[b]][:, :w], ALU.mult, ("g2", ("ps", banks[b])), ("g2",))
        self.tt("dve", ob2[s][:, :], g2[:, :], t1[:, :], ALU.add, ("g2", "t1"), (f"ob{s}",))
        self.dma("act", self.s_mg[i, :, :], ob2[s][:, :], (f"ob{s}",), (("s_mg", i),), f"st{s}")
    gemm(self, I["w_sb_out"], tiles, epi2)
    self.phase_reset()
    gemm_setup(self, D, TT)
    xo = [self.sb(f"xo{i}", [128, TT], F32) for i in range(2)]
    gemm_load_panel(self, self.s_mg, "s_mg")
    gemm(self, I["w_out"], tiles, res_epi(self, xo, 0, TT))


Builder.phase_merge = phase_merge


def phase_xattn(self):
    cfg = self.cfg
    D, TT, NM = cfg.D, cfg.TT, cfg.NMEM
    KT = D // 128
    XH, XD = cfg.XH, cfg.XD
    NDT = XD // 128
    I = self.I
    self.phase_transpose_in(I["mem"], "mem", TT=NM, dst=self.s_mT)
    self.phase_norm(self.v_norm_mem, "mem", src=self.s_mT, dst=self.s_hm, ntok=NM)
    self.phase_reset()
    gemm_setup(self, D, NM)
    obm = [self.sb(f"obm{i}", [128, NM], BF16) for i in range(2)]
    gemm_load_panel(self, self.s_hm, "s_hm")
    cnt = [0]

    def epikv(i, banks, wd):
        s = cnt[0] % 2
        cnt[0] += 1
        self.act(obm[s][:, :], self.ps[banks[0]][:, :NM], AF.Copy, (("ps", banks[0]),), (f"obm{s}",))
        dst = self.s_mk if i < KT else self.s_mv
        self.dma("act", dst[i % KT, :, :], obm[s][:, :], (f"obm{s}",), ((dst.name, i),), f"st{s}")
    gemm(self, I["w_xa_kv"], [(j * 128, 128) for j in range(2 * KT)], epikv)
    self.phase_norm(self.v_norm_xa, "xa")
    self.phase_reset()
    gemm_setup(self, D, TT)
    nb = (TT + 511) // 512
    ob = [self.sb(f"ob{i}", [128, TT], BF16) for i in range(2)]
    xo = [self.sb(f"xo{i}", [128, TT], F32) for i in range(2)]
    gemm_load_panel(self, self.s_h, "s_h")

    def epiq(i, banks, wd):
        s = cnt[0] % 2
        cnt[0] += 1
        for b in range(nb):
            w = min(512, TT - b * 512)
            self.act(ob[s][:, b * 512:b * 512 + w], self.ps[banks[b]][:, :w], AF.Copy, (("ps", banks[b]),), (f"ob{s}",),
                     scale=float(XD ** -0.5))
        self.dma("act", self.s_xq[i, :, :], ob[s][:, :], (f"ob{s}",), (("s_xq", i),), f"st{s}")
    tiles = [(j * 128, 128) for j in range(KT)]
    gemm(self, I["w_xa_q"], tiles, epiq)
    self.phase_reset()
    sb = self.sb
    NMT = NM // 128
    mk = sb("mk", [128, NDT, NM], BF16)
    mvs = sb("mvs", [128, NDT, NM], BF16)
    mvt = sb("mvt", [128, NMT, XD], BF16)
    xq = sb("xq", [128, NDT, TT], BF16)
    sc = [sb(f"sc{i}", [128, NM], F32) for i in range(2)]
    mx = [sb(f"mx{i}", [128, 2], F32) for i in range(2)]
    pb = [sb(f"pb{i}", [128, NM], BF16) for i in range(2)]
    pT = [sb(f"pT{i}", [128, NMT, 512], BF16) for i in range(2)]
    oo = [sb(f"oo{i}", [128, 512], BF16) for i in range(2)]
    it = 0
    for xh in range(XH):
        k0 = xh * NDT
        self.dma("sp", mk[:, :, :], self.s_mk[k0:k0 + NDT, :, :].rearrange("k p t -> p k t"), ("s_mk",), ("mk",), "ld0")
        self.dma("sp", mvs[:, :, :], self.s_mv[k0:k0 + NDT, :, :].rearrange("k p t -> p k t"), ("s_mv",), ("mvs",), "ld1")
        self.dma("sp", xq[:, :, :], self.s_xq[k0:k0 + NDT, :, :].rearrange("k p t -> p k t"), ("s_xq",), ("xq",), "ld2")
        for dvt in range(NDT):
            for mt in range(NMT):
                self.tr(self.psb[7][:, mt * 128:(mt + 1) * 128], mvs[:, dvt, mt * 128:(mt + 1) * 128], self.identb[:, :],
                        ("mvs", "identb"), (("ps", 7),))
            self.cp("dve", mvt[:, :, dvt * 128:(dvt + 1) * 128],
                    self.psb[7][:, :NMT * 128].rearrange("p (m d) -> p m d", m=NMT), (("ps", 7),), ("mvt",))
        for tg in range(TT // 512):
            pg = tg % 2
            for t4 in range(4):
                tq = tg * 4 + t4
                z = it % 2
                it += 1
                for dt_ in range(NDT):
                    self.mm(self.ps[z][:, :NM], xq[:, dt_, tq * 128:(tq + 1) * 128], mk[:, dt_, :],
                            dt_ == 0, dt_ == NDT - 1, ("xq", "mk"), (("ps", z),))
                self.P.add("dve", (lambda zz: (lambda e: e.tensor_reduce(out=mx[zz][:, 0:1], in_=self.ps[zz][:, :NM],
                                                                          axis=AX.X, op=ALU.max)))(z),
                           (("ps", z),), (f"mx{z}",))
                self.ts("dve", mx[z][:, 0:1], mx[z][:, 0:1], -1.0, None, ALU.mult, None, (f"mx{z}",), (f"mx{z}",))
                self.act(sc[z][:, :], self.ps[z][:, :NM], AF.Exp, (("ps", z), f"mx{z}"), (f"sc{z}", f"mx{z}"),
                         bias=mx[z][:, 0:1], accum_out=mx[z][:, 1:2])
                self.P.add("dve", (lambda zz: (lambda e: e.reciprocal(out=mx[zz][:, 1:2], in_=mx[zz][:, 1:2])))(z),
                           (f"mx{z}",), (f"mx{z}",))
                self.ts("dve", pb[z][:, :], sc[z][:, :], mx[z][:, 1:2], None, ALU.mult, None,
                        (f"sc{z}", f"mx{z}"), (f"pb{z}",))
                for mt in range(NMT):
                    self.tr(self.psb[4 + z][:, mt * 128:(mt + 1) * 128], pb[z][:, mt * 128:(mt + 1) * 128],
                            self.identb[:, :], (f"pb{z}", "identb"), (("ps", 4 + z),))
                self.cp("dve" if z else "act", pT[pg][:, :, t4 * 128:(t4 + 1) * 128],
                        self.psb[4 + z][:, :NMT * 128].rearrange("p (m t) -> p m t", m=NMT),
                        (("ps", 4 + z),), (f"pT{pg}",))
            for dvt in range(NDT):
                o = it % 2
                it += 1
                for mt in range(NMT):
                    self.mm(self.ps[2 + o][:, :], mvt[:, mt, dvt * 128:(dvt + 1) * 128], pT[pg][:, mt, :],
                            mt == 0, mt == NMT - 1, ("mvt", f"pT{pg}"), (("ps", 2 + o),))
                self.act(oo[o][:, :], self.ps[2 + o][:, :], AF.Copy, (("ps", 2 + o),), (f"oo{o}",))
                self.dma("act", self.s_xo[k0 + dvt, :, tg * 512:(tg + 1) * 512], oo[o][:, :], (f"oo{o}",),
                         (("s_xo", k0 + dvt, tg),), f"st{o}")
    self.phase_reset()
    gemm_setup(self, D, TT)
    xo = [self.sb(f"xo{i}", [128, TT], F32) for i in range(2)]
    gemm_load_panel(self, self.s_xo, "s_xo")
    gemm(self, I["w_xa_o"], tiles, res_epi(self, xo, 0, TT))


Builder.phase_xattn = phase_xattn


def phase_ffn(self):
    cfg = self.cfg
    D, TT, DFF = cfg.D, cfg.TT, cfg.DFF
    KT = D // 128
    NF = DFF // 128
    I = self.I
    self.phase_norm(self.v_norm_ffn, "ffn")
    self.phase_reset()
    gemm_setup(self, D, TT)
    nb = (TT + 511) // 512
    sg = self.sb("sg", [128, TT], BF16)
    ob = [self.sb(f"ob{i}", [128, TT], BF16) for i in range(2)]
    gemm_load_panel(self, self.s_h, "s_h")
    tiles = []
    for j in range(NF):
        tiles.append((j * 128, 128))
        tiles.append((DFF + j * 128, 128))
    cnt = [0]

    def epi(i, banks, wd):
        j = i // 2
        if i % 2 == 0:
            for b in range(nb):
                w = min(512, TT - b * 512)
                self.act(sg[:, b * 512:b * 512 + w], self.ps[banks[b]][:, :w], AF.Silu, (("ps", banks[b]),), ("sg",))
        else:
            s = cnt[0] % 2
            cnt[0] += 1
            for b in range(nb):
                w = min(512, TT - b * 512)
                sl = slice(b * 512, b * 512 + w)
                self.tt("dve", ob[s][:, sl], sg[:, sl], self.ps[banks[b]][:, :w], ALU.mult,
                        ("sg", ("ps", banks[b])), (f"ob{s}",))
            self.dma("act", self.s_act[j, :, :], ob[s][:, :], (f"ob{s}",), (("s_act", j),), f"st{s}")
    gemm(self, I["w_ffn_in"], tiles, epi)
    self.phase_reset()
    TS = min(512, TT)
    gemm_setup(self, DFF, TS)
    xo = [self.sb(f"xo{i}", [128, TS], F32) for i in range(2)]
    otiles = [(j * 128, 128) for j in range(KT)]
    for t0 in range(0, TT, TS):
        gemm_load_panel(self, self.s_act, "s_act", t0=t0)
        gemm(self, I["w_ffn_out"], otiles, res_epi(self, xo, t0, TS))


Builder.phase_ffn = phase_ffn
```

```python
import numpy as np
import ml_dtypes
import concourse.bass as bass
import concourse.mybir as mybir
from concourse.bass_utils import run_bass_kernel_spmd
from contextlib import ExitStack

F32 = mybir.dt.float32
BF16 = mybir.dt.bfloat16
AF = mybir.ActivationFunctionType
ALU = mybir.AluOpType
AX = mybir.AxisListType

COMPUTE = ("pe", "act", "dve", "pool")
ALLENG = ("pe", "act", "dve", "pool", "sp")
EPS = 1e-6
NEG = -30000.0
SAME_ENGINE_SYNC = True


class Op:
    __slots__ = ("eng", "fn", "reads", "writes", "chan", "idx", "deps", "sig",
                 "waits", "ordinal", "needs_sig")

    def __init__(self, eng, fn, reads, writes, chan, idx):
        self.eng = eng
        self.fn = fn
        self.reads = reads
        self.writes = writes
        self.chan = chan
        self.idx = idx
        self.deps = set()
        self.sig = None
        self.waits = []
        self.ordinal = 0
        self.needs_sig = False


class Prog:
    EPOCH = 12000

    def __init__(self, nc, same_engine_sync=True):
        self.nc = nc
        self.ops = []
        self.same_engine_sync = same_engine_sync
        self.barriers = []

    def add(self, eng, fn, reads=(), writes=(), chan=None):
        op = Op(eng, fn, tuple(reads), tuple(writes), chan, len(self.ops))
        self.ops.append(op)
        return op

    def barrier(self):
        self.barriers.append(len(self.ops))

    def analyze(self):
        ops = self.ops
        last_writer = {}
        readers = {}
        chan_last = {}
        chan_count = {}
        eng_last = {}
        bset = set(self.barriers)
        pending = {}
        for op in ops:
            if op.idx in bset:
                deps = set(eng_last.values()) | set(chan_last.values())
                for e in ALLENG:
                    pending.setdefault(e, set()).update(deps)
            d = op.deps
            if op.eng in pending:
                d |= pending.pop(op.eng)
            for k in op.reads:
                if k in last_writer:
                    d.add(last_writer[k])
            for k in op.writes:
                if k in last_writer:
                    d.add(last_writer[k])
                r = readers.get(k)
                if r:
                    for kk, v in r.items():
                        if kk == "dma":
                            d.update(v)
                        else:
                            d.add(v)
            if op.chan is not None:
                if op.chan in chan_last:
                    d.add(chan_last[op.chan])
                chan_last[op.chan] = op.idx
                chan_count[op.chan] = chan_count.get(op.chan, 0) + 1
                op.ordinal = chan_count[op.chan]
            d.discard(op.idx)
            for k in op.reads:
                r = readers.setdefault(k, {})
                if op.chan is not None:
                    r.setdefault("dma", []).append(op.idx)
                else:
                    r[op.eng] = op.idx
            for k in op.writes:
                last_writer[k] = op.idx
                readers[k] = {}
            if op.chan is None:
                eng_last[op.eng] = op.idx
        for op in ops:
            for j in op.deps:
                dj = ops[j]
                if dj.chan is None:
                    if dj.eng != op.eng or op.chan is not None:
                        dj.needs_sig = True
                    elif self.same_engine_sync and dj.eng != "pe":
                        dj.needs_sig = True
        cnt = {e: 0 for e in COMPUTE}
        for op in ops:
            if op.chan is None and op.needs_sig:
                assert op.fn is not None
                cnt[op.eng] += 1
                op.sig = cnt[op.eng]
        self.sig_counts = cnt
        self.chans = sorted(chan_count.keys())
        waited = {e: {} for e in ALLENG}
        for op in ops:
            w = waited[op.eng]
            need = {}
            for j in op.deps:
                dj = ops[j]
                if dj.chan is not None:
                    key = ("c", dj.chan)
                    val = 16 * dj.ordinal
                else:
                    if dj.sig is None:
                        continue
                    if dj.eng == op.eng and op.chan is None and (
                            not self.same_engine_sync or dj.eng == "pe"):
                        continue
                    ep = (dj.sig - 1) // self.EPOCH
                    key = ("e", dj.eng, ep)
                    val = dj.sig - ep * self.EPOCH
                if need.get(key, 0) < val:
                    need[key] = val
            for key, val in need.items():
                if w.get(key, 0) < val:
                    w[key] = val
                    op.waits.append((key, val))

    def emit(self):
        nc = self.nc
        self.analyze()
        sems = {}
        with ExitStack() as es:
            for e in COMPUTE:
                nep = (self.sig_counts[e] + self.EPOCH - 1) // self.EPOCH
                for ep in range(max(nep, 1)):
                    sems[("e", e, ep)] = es.enter_context(nc.semaphore(f"s_{e}_{ep}"))
            for c in self.chans:
                sems[("c", c)] = es.enter_context(nc.semaphore(f"c_{c}"))
            self.nsems = len(sems)
            block = es.enter_context(nc.Block())
            per_eng = {e: [op for op in self.ops if op.eng == e] for e in ALLENG}
            EP = self.EPOCH

            def run(engobj, lst):
                for op in lst:
                    for key, val in op.waits:
                        engobj.wait_ge(sems[key], val)
                    if op.fn is None:
                        continue
                    ins = op.fn(engobj)
                    if op.chan is not None:
                        ins.then_inc(sems[("c", op.chan)], 16)
                    elif op.sig is not None:
                        ep = (op.sig - 1) // EP
                        ins.then_inc(sems[("e", op.eng, ep)], 1)

            @block.tensor
            def _(e):
                run(e, per_eng["pe"])

            @block.scalar
            def _(e):
                run(e, per_eng["act"])

            @block.vector
            def _(e):
                run(e, per_eng["dve"])

            @block.gpsimd
            def _(e):
                run(e, per_eng["pool"])

            @block.sync
            def _(e):
                run(e, per_eng["sp"])


class Cfg:
    def __init__(self, D=4096, TT=2048, G=8, NMEM=256, DFF=11008, stop_after=None):
        self.D = D
        self.TT = TT
        self.G = G
        self.H = 8 * G
        self.DI = 512 * G
        self.CD = self.DI + 2 * G * 128
        self.NH = D // 128
        self.NMEM = NMEM
        self.DFF = DFF
        self.XH = 4
        self.XD = D // 4
        sizes = (self.DI, self.CD, self.H, D, D, D, 2 * D)
        self.off = np.concatenate([[0], np.cumsum(sizes)]).astype(int)
        self.NIN = int(self.off[-1])
        self.stop_after = stop_after


def make_consts():
    c = {}
    c["ident"] = np.eye(128, dtype=np.float32)
    i = np.arange(128)
    c["ones"] = np.ones((128, 128), np.float32)
    c["triinc"] = (i[:, None] <= i[None, :]).astype(np.float32)
    c["negm"] = np.where(i[None, :] < i[:, None], NEG, 0.0).astype(np.float32)
    c["nstrict"] = -(i[:, None] > i[None, :]).astype(np.float32)
    t = np.arange(512)
    m = np.stack([((kb * 128 + i)[:, None] < t[None, :]).astype(np.float32) for kb in range(4)])
    c["m01"] = m.transpose(1, 0, 2).reshape(128, 4 * 512)
    c["mneg"] = ((m - 1.0) * (-NEG)).transpose(1, 0, 2).reshape(128, 4 * 512)
    names = ["ident", "ones", "triinc", "negm", "nstrict"]
    offs = {}
    o = 0
    for n in names:
        offs[n] = (o, c[n].shape[1])
        o += c[n].shape[1]
    packed = np.concatenate([c[n] for n in names], axis=1).astype(np.float32)
    masks = np.concatenate([c["m01"], c["mneg"]], axis=1).astype(np.float32)
    return packed, offs, masks


class Builder:
    def __init__(self, cfg):
        self.cfg = cfg
        self.nc = bass.Bass("TRN2", target_bir_lowering=False)
        self.P = Prog(self.nc, same_engine_sync=SAME_ENGINE_SYNC)
        self.arena_top = 0
        self.arena_base = 0
        self.uid = 0
        self.dram = {}
        self.psum = []
        self.done = False

    def sb(self, name, shape, dt):
        nbytes = int(np.prod(shape[1:])) * (4 if dt == F32 else 2)
        nbytes = (nbytes + 63) // 64 * 64
        self.uid += 1
        t = self.nc.alloc_sbuf_tensor_at(f"{name}_{self.uid}", list(shape), dt, offset=self.arena_top)
        self.arena_top += nbytes
        assert self.arena_top <= self.sb_limit, (name, self.arena_top, self.sb_limit)
        return t

    def phase_reset(self):
        self.P.barrier()
        self.arena_top = self.arena_base

    def dr(self, name, shape, dt):
        t = self.nc.dram_tensor(name, list(shape), dt, kind="Internal")
        self.dram[name] = t
        return t

    def dma(self, q, out, in_, reads, writes, chan, slow=False):
        if slow:
            self.P.add(q, lambda e: e.dma_start(out=out, in_=in_, allow_slow_non_contiguous=True),
                       reads, writes, chan=chan)
        else:
            self.P.add(q, lambda e: e.dma_start(out=out, in_=in_), reads, writes, chan=chan)

    def dma_tiles(self, q, sb_tile, dram, k0, kn, tsl, reads, writes, chans, store=False, step=8):
        for i, a in enumerate(range(0, kn, step)):
            n = min(step, kn - a)
            d = dram[k0 + a:k0 + a + n, :, tsl].rearrange("k p t -> p k t")
            t_ = sb_tile[:, a:a + n, :]
            ch = chans[i % len(chans)]
            if store:
                self.dma(q, d, t_, reads, writes, ch)
            else:
                self.dma(q, t_, d, reads, writes, ch)

    def mm(self, out, lhsT, rhs, start, stop, reads, writes, **kw):
        self.P.add("pe", lambda e: e.matmul(out, lhsT, rhs, start=start, stop=stop, **kw), reads, writes)

    def tr(self, out, in_, ident, reads, writes):
        self.P.add("pe", lambda e: e.transpose(out, in_, ident), reads, writes)

    def act(self, out, in_, func, reads, writes, bias=None, scale=None, accum_out=None, eng="act"):
        kw = {}
        if bias is not None:
            kw["bias"] = bias
        if scale is not None:
            kw["scale"] = scale
        if accum_out is not None:
            kw["accum_out"] = accum_out
        self.P.add("act", lambda e: e.activation(out=out, in_=in_, func=func, **kw), reads, writes)

    def tt(self, eng, out, in0, in1, op, reads, writes):
        self.P.add(eng, lambda e: e.tensor_tensor(out=out, in0=in0, in1=in1, op=op), reads, writes)

    def ts(self, eng, out, in0, s1, s2, op0, op1, reads, writes):
        if op1 is None:
            self.P.add(eng, lambda e: e.tensor_scalar(out=out, in0=in0, scalar1=s1, scalar2=None, op0=op0),
                       reads, writes)
        else:
            self.P.add(eng, lambda e: e.tensor_scalar(out=out, in0=in0, scalar1=s1, scalar2=s2, op0=op0, op1=op1),
                       reads, writes)

    def stt(self, out, in0, scalar, in1, op0, op1, reads, writes):
        self.P.add("dve", lambda e: e.scalar_tensor_tensor(out=out, in0=in0, scalar=scalar, in1=in1,
                                                           op0=op0, op1=op1), reads, writes)

    def cp(self, eng, out, in_, reads, writes):
        if eng == "act":
            self.P.add("act", lambda e: e.copy(out=out, in_=in_), reads, writes)
        else:
            self.P.add(eng, lambda e: e.tensor_copy(out=out, in_=in_), reads, writes)

    def memset(self, eng, ap, val, writes):
        self.P.add(eng, lambda e: e.memset(ap, val), (), writes)

    def build(self):
        cfg = self.cfg
        nc = self.nc
        D, TT, G, H, DI, CD, NH = cfg.D, cfg.TT, cfg.G, cfg.H, cfg.DI, cfg.CD, cfg.NH
        KT = D // 128

        early = cfg.stop_after in ("tin", "pinproj", "pssd", "inproj", "ssd", "attn", "ssdonly")
        self.tiny = set()

        def ein(name, shape):
            if early and name in ("w_ffn_in", "w_ffn_out", "w_xa_kv", "w_xa_q", "w_xa_o", "w_out", "w_ssd_out",
                                  "w_sb_out") or (cfg.stop_after == "ssdonly" and name == "w_in"):
                shape = [128, 128]
                self.tiny.add(name)
            return nc.dram_tensor(name, list(shape), F32, kind="ExternalInput")

        self.packed, self.coffs, self.masks = make_consts()
        I = {}
        I["x_prev"] = ein("x_prev", [TT, D])
        I["x_own"] = ein("x_own", [TT, D])
        I["mem"] = ein("mem", [cfg.NMEM, D])
        I["flag"] = ein("flag", [128, 1])
        I["consts"] = ein("consts", list(self.packed.shape))
        I["cmask"] = ein("cmask", list(self.masks.shape))
        I["norm_mix"] = ein("norm_mix", [D])
        I["w_in"] = ein("w_in", [D, cfg.NIN])
        I["b_gate"] = ein("b_gate", [2 * D])
        I["conv_w"] = ein("conv_w", [4, CD])
        I["conv_b"] = ein("conv_b", [CD])
        I["dt_bias"] = ein("dt_bias", [H])
        I["a_log"] = ein("a_log", [H])
        I["d_skip"] = ein("d_skip", [H])
        I["ssd_norm"] = ein("ssd_norm", [DI])
        I["w_ssd_out"] = ein("w_ssd_out", [DI, D])
        I["w_sb_out"] = ein("w_sb_out", [D, D])
        I["w_out"] = ein("w_out", [D, D])
        I["norm_xa"] = ein("norm_xa", [D])
        I["norm_mem"] = ein("norm_mem", [D])
        I["w_xa_q"] = ein("w_xa_q", [D, D])
        I["w_xa_kv"] = ein("w_xa_kv", [D, 2 * D])
        I["w_xa_o"] = ein("w_xa_o", [D, D])
        I["norm_ffn"] = ein("norm_ffn", [D])
        I["w_ffn_in"] = ein("w_ffn_in", [D, 2 * cfg.DFF])
        I["w_ffn_out"] = ein("w_ffn_out", [cfg.DFF, D])
        I["norm_final"] = ein("norm_final", [D])
        self.I = I
        self.out = nc.dram_tensor("out", [TT, D], F32, kind="ExternalOutput")
        self.dbg = None

        self.ps = [nc.alloc_psum_tensor(f"ps{i}", [128, 512], F32) for i in range(8)]
        self.psb = [p.bitcast(BF16) for p in self.ps]

        self.arena_top = (nc.sbuf_base + 63) // 64 * 64
        self.sb_limit = nc.sbuf_top - 64
        C = {}
        ncol = self.packed.shape[1]
        cst = self.sb("cst", [128, ncol], F32)
        self.dma("sp", cst[:, :], I["consts"][:, :], (), ("cst",), "ld0")

        def cs(name):
            o, n = self.coffs[name]
            return cst[:, o:o + n]
        self.cs = cs
        identb = self.sb("identb", [128, 128], BF16)
        self.cp("dve", identb[:, :], cs("ident"), ("cst",), ("identb",))
        onesb = self.sb("onesb", [128, 128], BF16)
        self.cp("dve", onesb[:, :], cs("ones"), ("cst",), ("onesb",))
        self.identb, self.onesb = identb, onesb
        nstrictb = self.sb("nstrictb", [128, 128], BF16)
        self.cp("dve", nstrictb[:, :], cs("nstrict"), ("cst",), ("nstrictb",))
        self.nstrictb = nstrictb
        nonesb = self.sb("nonesb", [128, 128], BF16)
        self.ts("dve", nonesb[:, :], cs("ones"), -1.0, None, ALU.mult, None, ("cst",), ("nonesb",))
        self.nonesb = nonesb
        flag = self.sb("flag", [128, 1], F32)
        self.dma("sp", flag[:, :], I["flag"][:, :], (), ("flag",), "ld1")
        self.flag = flag

        vstage = self.sb("vstage", [128, 128], F32)
        self.vcount = 0

        def colvec_into(dst_ap, src1d, n, key):
            nt = n // 128
            b = self.vcount % 2
            self.vcount += 1
            self.dma("sp", vstage[:nt, :], src1d.rearrange("(t p) -> t p", p=128), (), ("vstage",), "ld0")
            self.tr(self.ps[b][:, :nt], vstage[:nt, :], self.cs("ident")[:nt, :nt], ("vstage", "cst"), (("ps", b),))
            self.cp("dve", dst_ap, self.ps[b][:, :nt], (("ps", b),), (key,))

        def colvec(name, src, n, chan):
            t_ = self.sb(name, [128, n // 128], F32)
            colvec_into(t_[:, :], src[:], n, name)
            return (t_, name)
        self.colvec = colvec
        self.v_norm_mix = colvec("v_norm_mix", I["norm_mix"], D, "ld0")
        self.v_norm_xa = colvec("v_norm_xa", I["norm_xa"], D, "ld1")
        self.v_norm_mem = colvec("v_norm_mem", I["norm_mem"], D, "ld0")
        self.v_norm_ffn = colvec("v_norm_ffn", I["norm_ffn"], D, "ld1")
        self.v_norm_final = colvec("v_norm_final", I["norm_final"], D, "ld0")
        self.v_ssd_norm = colvec("v_ssd_norm", I["ssd_norm"], DI, "ld1")
        self.v_b_gate = colvec("v_b_gate", I["b_gate"], 2 * D, "ld0")
        self.v_conv_b = colvec("v_conv_b", I["conv_b"], CD, "ld1")
        v_conv_w = self.sb("v_conv_w", [128, 4, CD // 128], F32)
        for j in range(4):
            colvec_into(v_conv_w[:, j, :], I["conv_w"][j, :], CD, "v_conv_w")
        self.v_conv_w = v_conv_w
        hv = self.sb("hv", [H, 4], F32)
        for j, nm in enumerate(["dt_bias", "a_log", "d_skip"]):
            self.dma("sp", hv[:, j:j + 1], I[nm].rearrange("(h o) -> h o", o=1), (), ("hv",), "ld1", slow=True)
        self.hv = hv
        negA = self.sb("negA", [H, 1], F32)
        self.act(negA[:, :], hv[:, 1:2], AF.Exp, ("hv",), ("negA",))
        self.ts("dve", negA[:, :], negA[:, :], -1.0, None, ALU.mult, None, ("negA",), ("negA",))
        self.negA = negA
        self.halo = self.sb("halo", [128, CD // 128, 3], F32)
        self.memset("pool", self.halo[:, :, :], 0.0, ("halo",))
        self.arena_base = self.arena_top

        self.s_xT = self.dr("s_xT", [KT, 128, TT], F32)
        self.s_h = self.dr("s_h", [KT, 128, TT], BF16)
        self.s_sz = self.dr("s_sz", [DI // 128, 128, TT], BF16)
        self.s_xbc = self.dr("s_xbc", [CD // 128, 128, TT], BF16)
        self.s_q = self.dr("s_q", [NH, 128, TT], BF16)
        self.s_k = self.dr("s_k", [NH, 128, 2 * TT], BF16)
        self.s_vT = self.dr("s_vT", [NH, 128, 2 * TT], BF16)
        self.s_gate = self.dr("s_gate", [2 * KT, 128, TT], BF16)
        self.s_yn = self.dr("s_yn", [DI // 128, 128, TT], BF16)
        self.s_o = self.dr("s_o", [NH, 128, TT], BF16)
        self.s_bs = self.dr("s_bs", [KT, 128, TT], BF16)
        self.s_mg = self.dr("s_mg", [KT, 128, TT], BF16)
        self.s_dt = self.dr("s_dt", [2, H, TT], F32)
        self.s_state = self.dr("s_state", [128, H * 64], F32)
        self.s_act = self.dr("s_act", [cfg.DFF // 128, 128, TT], BF16)
        self.s_xq = self.dr("s_xq", [KT, 128, TT], BF16)
        self.s_xo = self.dr("s_xo", [KT, 128, TT], BF16)
        self.s_hm = self.dr("s_hm", [KT, 128, cfg.NMEM], BF16)
        self.s_mT = self.dr("s_mT", [KT, 128, cfg.NMEM], F32)
        self.s_mk = self.dr("s_mk", [KT, 128, cfg.NMEM], BF16)
        self.s_mv = self.dr("s_mv", [KT, 128, cfg.NMEM], BF16)

        self.main()
        self.P.emit()
        return nc

    def stop(self, name):
        if self.cfg.stop_after == name:
            self.done = True
        return self.done

    def main(self):
        cfg = self.cfg
        I = self.I
        if cfg.stop_after == "ssdonly":
            self.phase_transpose_in(I["x_own"], "own")
            self.phase_reset()
            zf = self.sb("zf", [128, cfg.TT], F32)
            zb = self.sb("zb", [128, cfg.TT], BF16)
            self.memset("pool", zf[:, :], 0.01, ("zf",))
            self.memset("pool", zb[:, :], 0.01, ("zb",))
            self.dma("sp", self.s_dt[0, :, :], zf[:cfg.H, :], ("zf",), ("s_dt",), "st0")
            self.ts("dve", zf[:, :], zf[:, :], -1.0, None, ALU.mult, None, ("zf",), ("zf",))
            self.dma("sp", self.s_dt[1, :, :], zf[:cfg.H, :], ("zf",), ("s_dt",), "st0")
            for k in range(cfg.CD // 128):
                self.dma("sp", self.s_xbc[k, :, :], zb[:, :], ("zb",), ("s_xbc",), "st1")
            import os
            for k in range(cfg.DI // 128):
                self.dma("sp", self.s_sz[k, :, :], zb[:, :], ("zb",), ("s_sz",), "st1")
            self.phase_ssd(prev=True)
            if os.environ.get("SSD_OWN"):
                self.phase_ssd(prev=False)
            self.phase_final(norm=False)
            return
        if cfg.stop_after == "tin":
            self.phase_transpose_in(I["x_own"], "own")
            self.phase_norm(self.v_norm_mix, "mix")
            self.phase_final(norm=True)
            return
        self.phase_transpose_in(I["x_prev"], "prev")
        self.phase_norm(self.v_norm_mix, "mix")
        self.phase_inproj(prev=True)
        if self.stop("pinproj"):
            return self.finish_debug()
        self.phase_ssd(prev=True)
        if self.stop("pssd"):
            return self.finish_debug()
        self.phase_transpose_in(I["x_own"], "own")
        self.phase_norm(self.v_norm_mix, "mix")
        self.phase_inproj(prev=False)
        if self.stop("inproj"):
            return self.finish_debug()
        self.phase_ssd(prev=False)
        if self.stop("ssd"):
            return self.finish_debug()
        self.phase_attn()
        if self.stop("attn"):
            return self.finish_debug()
        self.phase_merge()
        if self.stop("merge"):
            return self.finish_debug()
        self.phase_xattn()
        if self.stop("xattn"):
            return self.finish_debug()
        self.phase_ffn()
        self.phase_final()

    def finish_debug(self):
        self.phase_final(norm=False)

    def phase_transpose_in(self, x, tag, TT=None, dst=None):
        cfg = self.cfg
        D = cfg.D
        TT = TT or cfg.TT
        dst = dst if dst is not None else self.s_xT
        KT = D // 128
        self.phase_reset()
        xt = [self.sb(f"xt{i}", [128, D], F32) for i in range(2)]
        ot = [self.sb(f"xo{i}", [128, KT, 128], F32) for i in range(2)]
        ident = self.cs("ident")
        for tt in range(TT // 128):
            s = tt % 2
            self.dma("sp", xt[s][:, :], x[tt * 128:(tt + 1) * 128, :], (), (f"xt{s}",), f"ld{s}")
            for g in range(KT // 4):
                bank = (tt * (KT // 4) + g) % 8
                for j in range(4):
                    ft = g * 4 + j
                    self.tr(self.ps[bank][:, j * 128:(j + 1) * 128], xt[s][:, ft * 128:(ft + 1) * 128],
                            ident, (f"xt{s}", "cst"), (("ps", bank),))
                eng = "act" if g % 2 == 0 else "dve"
                self.cp(eng, ot[s][:, g * 4:(g + 1) * 4, :],
                        self.ps[bank][:, :].rearrange("p (j t) -> p j t", j=4),
                        (("ps", bank),), (f"xo{s}",))
            self.dma_tiles("sp", ot[s], dst, 0, KT, slice(tt * 128, (tt + 1) * 128), (f"xo{s}",), (dst.name,),
                           (f"st{s}",), store=True)

    def phase_norm(self, wv, tag, src=None, dst=None, ntok=None, D=None):
        cfg = self.cfg
        wvec, wkey = wv
        D = D or cfg.D
        TT = ntok or cfg.TT
        KT = D // 128
        src = src if src is not None else self.s_xT
        dst = dst if dst is not None else self.s_h
        skey = src.name
        dkey = dst.name
        self.phase_reset()
        xin = [self.sb(f"nx{i}", [128, TT], F32) for i in range(3)]
        sq = [self.sb(f"nsq{i}", [128, TT], F32) for i in range(2)]
        acc = self.sb("nacc", [128, TT], F32)
        rstd = self.sb("nrstd", [128, TT], F32)
        ho = [self.sb(f"nho{i}", [128, TT], BF16) for i in range(2)]
        for kt in range(KT):
            s = kt % 3
            self.dma("sp", xin[s][:, :], src[kt, :, :], (skey,), (f"nx{s}",), f"ld{s}")
            if kt == 0:
                self.act(acc[:, :], xin[s][:, :], AF.Square, (f"nx{s}",), ("nacc",))
            else:
                s2 = kt % 2
                self.act(sq[s2][:, :], xin[s][:, :], AF.Square, (f"nx{s}",), (f"nsq{s2}",))
                self.tt("pool", acc[:, :], acc[:, :], sq[s2][:, :], ALU.add, ("nacc", f"nsq{s2}"), ("nacc",))
        ones = self.cs("ones")
        nb = (TT + 511) // 512
        for b in range(nb):
            w = min(512, TT - b * 512)
            self.mm(self.ps[b][:, :w], ones, acc[:, b * 512:b * 512 + w], True, True,
                    ("cst", "nacc"), (("ps", b),))
            self.act(rstd[:, b * 512:b * 512 + w], self.ps[b][:, :w], AF.Ln, (("ps", b),), ("nrstd",),
                     bias=EPS, scale=1.0 / D)
        self.act(rstd[:, :], rstd[:, :], AF.Exp, ("nrstd",), ("nrstd",), scale=-0.5)
        for kt in range(KT):
            s = kt % 3
            s2 = kt % 2
            self.dma("sp", xin[s][:, :], src[kt, :, :], (skey,), (f"nx{s}",), f"ld{s}")
            self.stt(ho[s2][:, :], xin[s][:, :], wvec[:, kt:kt + 1], rstd[:, :], ALU.mult, ALU.mult,
                     (f"nx{s}", "nrstd", wkey), (f"nho{s2}",))
            self.dma("sp", dst[kt, :, :], ho[s2][:, :], (f"nho{s2}",), (dkey,), f"st{s2}")


    def phase_final(self, norm=True):
        cfg = self.cfg
        D, TT = cfg.D, cfg.TT
        KT = D // 128
        NTT = TT // 128
        src = self.s_xT
        self.phase_reset()
        xin = [self.sb(f"fx{i}", [128, TT], F32) for i in range(2)]
        sq = [self.sb(f"fsq{i}", [128, TT], F32) for i in range(2)]
        acc = self.sb("facc", [128, TT], F32)
        rstd = self.sb("frstd", [128, TT], F32)
        yk = [self.sb(f"fy{i}", [128, TT], F32) for i in range(2)]
        ot = [self.sb(f"fo{i}", [128, NTT, 128], F32) for i in range(2)]
        wvec, wkey = self.v_norm_final
        if norm:
            for kt in range(KT):
                s = kt % 2
                self.dma("sp", xin[s][:, :], src[kt, :, :], ("s_xT",), (f"fx{s}",), f"ld{s}")
                if kt == 0:
                    self.act(acc[:, :], xin[s][:, :], AF.Square, (f"fx{s}",), ("facc",))
                else:
                    self.act(sq[s][:, :], xin[s][:, :], AF.Square, (f"fx{s}",), (f"fsq{s}",))
                    self.tt("pool", acc[:, :], acc[:, :], sq[s][:, :], ALU.add, ("facc", f"fsq{s}"), ("facc",))
            ones = self.cs("ones")
            for b in range((TT + 511) // 512):
                w = min(512, TT - b * 512)
                self.mm(self.ps[b][:, :w], ones, acc[:, b * 512:b * 512 + w], True, True,
                        ("cst", "facc"), (("ps", b),))
                self.act(rstd[:, b * 512:b * 512 + w], self.ps[b][:, :w], AF.Ln, (("ps", b),), ("frstd",),
                         bias=EPS, scale=1.0 / D)
            self.act(rstd[:, :], rstd[:, :], AF.Exp, ("frstd",), ("frstd",), scale=-0.5)
        ident = self.cs("ident")
        for kt in range(KT):
            s = kt % 2
            self.dma("sp", xin[s][:, :], src[kt, :, :], ("s_xT",), (f"fx{s}",), f"ld{s}")
            if norm:
                self.stt(yk[s][:, :], xin[s][:, :], wvec[:, kt:kt + 1], rstd[:, :], ALU.mult, ALU.mult,
                         (f"fx{s}", "frstd", wkey), (f"fy{s}",))
                y, ykey = yk[s], f"fy{s}"
            else:
                y, ykey = xin[s], f"fx{s}"
            for g in range(NTT // 4):
                bank = (kt * (NTT // 4) + g) % 8
                for j in range(4):
                    tt = g * 4 + j
                    self.tr(self.ps[bank][:, j * 128:(j + 1) * 128], y[:, tt * 128:(tt + 1) * 128],
                            ident, (ykey, "cst"), (("ps", bank),))
                eng = "act" if g % 2 == 0 else "dve"
                self.cp(eng, ot[s][:, g * 4:(g + 1) * 4, :],
                        self.ps[bank][:, :].rearrange("p (j t) -> p j t", j=4),
                        (("ps", bank),), (f"fo{s}",))
            self.dma("sp", self.out[:, kt * 128:(kt + 1) * 128].rearrange("(t p) f -> p t f", p=128),
                     ot[s][:, :, :], (f"fo{s}",), ("out",), f"st{s}")
        self.P.add("sp", None, ("out",), ())


_CACHE = {}


def _get_nc(cfg_key, cfg):
    if cfg_key not in _CACHE:
        b = Builder(cfg)
        nc = b.build()
        _CACHE[cfg_key] = (nc, b)
    return _CACHE[cfg_key]


def run_cfg(cfg, inputs, n_batch, cfg_key):
    nc, b = _get_nc(cfg_key, cfg)
    TT, D = cfg.TT, cfg.D
    f32 = np.float32
    x = np.ascontiguousarray(inputs["x"], dtype=f32)
    mem = np.ascontiguousarray(inputs["mem"], dtype=f32)
    shared = {}
    for k in ["norm_mix", "w_in", "b_gate", "conv_w", "conv_b", "dt_bias", "a_log", "d_skip", "ssd_norm",
              "w_ssd_out", "w_sb_out", "w_out", "norm_xa", "norm_mem", "w_xa_q", "w_xa_kv", "w_xa_o",
              "norm_ffn", "w_ffn_in", "w_ffn_out"]:
        shared[k] = np.ascontiguousarray(np.asarray(inputs[k], dtype=f32)[0])
    shared["norm_final"] = np.ascontiguousarray(inputs["norm_final"], dtype=f32)
    for k in b.tiny:
        shared[k] = np.zeros((128, 128), f32)
    shared["consts"] = b.packed
    shared["cmask"] = b.masks
    zeros = np.zeros((TT, D), f32)
    in_maps = []
    ncores = 2 * n_batch
    for c in range(ncores):
        bi, half = c // 2, c % 2
        m = dict(shared)
        m["x_own"] = np.ascontiguousarray(x[bi, half * TT:(half + 1) * TT])
        m["x_prev"] = np.ascontiguousarray(x[bi, 0:TT]) if half == 1 else zeros
        m["mem"] = np.ascontiguousarray(mem[bi])
        m["flag"] = np.full((128, 1), float(half), f32)
        in_maps.append(m)
    res = run_bass_kernel_spmd(nc, in_maps, core_ids=list(range(ncores)))
    out = np.zeros((n_batch, 2 * TT, D), f32)
    for c in range(ncores):
        bi, half = c // 2, c % 2
        out[bi, half * TT:(half + 1) * TT] = np.asarray(res.results[c]["out"], dtype=f32)
    return out


def kernel(**inputs):
    cfg = Cfg()
    return run_cfg(cfg, inputs, 4, "full")


KCH = 16


def gemm_setup(self, K, TTg):
    KT = K // 128
    self.g_panel = self.sb("panel", [128, KT, TTg], BF16)
    self.g_st = [self.sb(f"gst{i}", [128, KCH, 128], F32) for i in range(3)]
    self.g_wb = [self.sb(f"gwb{i}", [128, KCH, 128], BF16) for i in range(3)]
    self.g_u = 0
    self.g_nt = 0
    self.g_TT = TTg
    self.g_KT = KT


def gemm_load_panel(self, src, skey, t0=0):
    KT, TTg = self.g_KT, self.g_TT
    step = 8
    for k0 in range(0, KT, step):
        kn = min(step, KT - k0)
        self.dma("sp", self.g_panel[:, k0:k0 + kn, :],
                 src[k0:k0 + kn, :, t0:t0 + TTg].rearrange("k p t -> p k t"),
                 (skey,), ("panel",), f"ld{(k0 // step) % 3}")


def gemm(self, W, tiles, epi):
    KT, TTg = self.g_KT, self.g_TT
    nb = (TTg + 511) // 512
    nsets = 8 // nb
    units = []
    for i, (c0, wd) in enumerate(tiles):
        nk = (KT + KCH - 1) // KCH
        for kc in range(nk):
            units.append((i, c0, wd, kc, kc == nk - 1))

    def load(u):
        i, c0, wd, kc, last = units[u]
        s = (self.g_u + u) % 3
        k0 = kc * KCH
        kn = min(KCH, KT - k0)
        self.dma("sp", self.g_st[s][:, :kn, :wd],
                 W[k0 * 128:(k0 + kn) * 128, c0:c0 + wd].rearrange("(kt p) n -> p kt n", p=128),
                 (), (f"gst{s}",), f"w{s}")
        self.cp("act", self.g_wb[s][:, :kn, :wd], self.g_st[s][:, :kn, :wd], (f"gst{s}",), (f"gwb{s}",))

    LA = 2
    for u in range(min(LA, len(units))):
        load(u)
    for u in range(len(units)):
        if u + LA < len(units):
            load(u + LA)
        i, c0, wd, kc, last = units[u]
        s = (self.g_u + u) % 3
        k0 = kc * KCH
        kn = min(KCH, KT - k0)
        setn = (self.g_nt + i) % nsets
        banks = [setn * nb + b for b in range(nb)]
        for kt in range(kn):
            for b in range(nb):
                w = min(512, TTg - b * 512)
                self.mm(self.ps[banks[b]][:wd, :w], self.g_wb[s][:, kt, :wd],
                        self.g_panel[:, k0 + kt, b * 512:b * 512 + w],
                        (k0 + kt == 0), (k0 + kt == KT - 1),
                        (f"gwb{s}", "panel"), (("ps", banks[b]),))
        if last:
            epi(i, banks, wd)
    self.g_u += len(units)
    self.g_nt += len(tiles)


def phase_inproj(self, prev):
    cfg = self.cfg
    D, TT, G, H, DI, CD, NH = cfg.D, cfg.TT, cfg.G, cfg.H, cfg.DI, cfg.CD, cfg.NH
    off = cfg.off
    I = self.I
    self.phase_reset()
    gemm_setup(self, D, TT)
    ob = [self.sb(f"ob{i}", [128, TT], BF16) for i in range(2)]
    u_t = self.sb("cu", [128, TT + 3], F32)
    acc = self.sb("cacc", [128, TT], F32)
    gemm_load_panel(self, self.s_h, "s_h")
    koff = 0 if prev else TT
    tiles = []
    kinds = []

    def addseg(kind, seg, n):
        for j in range(0, n, 128):
            tiles.append((int(off[seg]) + j, min(128, n - j)))
            kinds.append((kind, j // 128))
    if not prev:
        addseg("z", 0, DI)
    addseg("xbc", 1, CD)
    addseg("dt", 2, H)
    if not prev:
        addseg("q", 3, D)
    addseg("k", 4, D)
    addseg("v", 5, D)
    if not prev:
        addseg("gate", 6, 2 * D)
    nb = (TT + 511) // 512
    cnt = [0]

    def evac(dst, banks, func, okey, wd=128, bias=None, scale=None):
        for b in range(nb):
            w = min(512, TT - b * 512)
            self.act(dst[:wd, b * 512:b * 512 + w], self.ps[banks[b]][:wd, :w], func,
                     (("ps", banks[b]),), (okey,), bias=bias, scale=scale)

    def epi(i, banks, wd):
        kind, idx = kinds[i]
        s = cnt[0] % 2
        cnt[0] += 1
        okey = f"ob{s}"
        o = ob[s]
        if kind == "z":
            evac(o, banks, AF.Silu, okey)
            self.dma("act", self.s_sz[idx, :, :], o[:, :], (okey,), ("s_sz",), f"st{s}")
        elif kind == "q":
            evac(o, banks, AF.Copy, okey, scale=float(128 ** -0.5))
            self.dma("act", self.s_q[idx, :, :], o[:, :], (okey,), ("s_q",), f"st{s}")
        elif kind == "k":
            evac(o, banks, AF.Copy, okey)
            self.dma("act", self.s_k[idx, :, koff:koff + TT], o[:, :], (okey,), ("s_k",), f"st{s}")
        elif kind == "v":
            if prev:
                evac(o, banks, AF.Copy, okey, scale=self.flag[:, 0:1])
            else:
                evac(o, banks, AF.Copy, okey)
            self.dma("act", self.s_vT[idx, :, koff:koff + TT], o[:, :], (okey,), ("s_vT",), f"st{s}")
        elif kind == "gate":
            evac(o, banks, AF.Sigmoid, okey, bias=self.v_b_gate[0][:, idx:idx + 1])
            self.dma("act", self.s_gate[idx, :, :], o[:, :], (okey,), ("s_gate",), f"st{s}")
        elif kind == "dt":
            evac(acc, banks, AF.Exp, "cacc", wd=H, bias=self.hv[:, 0:1])
            self.act(acc[:H, :], acc[:H, :], AF.Ln, ("cacc",), ("cacc",), bias=1.0)
            self.ts("dve", u_t[:H, :TT], acc[:H, :], self.negA[:, 0:1], None, ALU.mult, None,
                    ("cacc", "negA"), ("cu",))
            self.dma("act", self.s_dt[0, :, :], acc[:H, :], ("cacc",), ("s_dt",), "st0")
            self.dma("act", self.s_dt[1, :, :], u_t[:H, :TT], ("cu",), ("s_dt",), "st1")
        elif kind == "xbc":
            cw = self.v_conv_w
            self.cp("dve", u_t[:, 0:3], self.halo[:, idx, :], ("halo",), ("cu",))
            evac(u_t[:, 3:], banks, AF.Copy, "cu")
            self.cp("dve", self.halo[:, idx, :], u_t[:, TT:TT + 3], ("cu",), ("halo",))
            self.ts("dve", acc[:, :], u_t[:, 3:3 + TT], cw[:, 3, idx:idx + 1], self.v_conv_b[0][:, idx:idx + 1],
                    ALU.mult, ALU.add, ("cu", "v_conv_w", "v_conv_b"), ("cacc",))
            for j in (2, 1, 0):
                self.stt(acc[:, :], u_t[:, j:j + TT], cw[:, j, idx:idx + 1], acc[:, :], ALU.mult, ALU.add,
                         ("cu", "cacc", "v_conv_w"), ("cacc",))
            self.act(o[:, :], acc[:, :], AF.Silu, ("cacc",), (okey,))
            self.dma("act", self.s_xbc[idx, :, :], o[:, :], (okey,), ("s_xbc",), f"st{s}")

    gemm(self, I["w_in"], tiles, epi)


Builder.phase_inproj = phase_inproj


def phase_ssd(self, prev):
    cfg = self.cfg
    D, TT, G, H, DI, CD = cfg.D, cfg.TT, cfg.G, cfg.H, cfg.DI, cfg.CD
    NDI = DI // 128
    NCD = CD // 128
    BLK = min(TT, 256)
    NCH = BLK // 64
    self.phase_reset()
    sb = self.sb
    ident = self.cs("ident")
    ones = self.cs("ones")
    triinc = self.cs("triinc")
    negm = self.cs("negm")
    xbc = sb("xbc", [128, NCD, BLK], BF16)
    dtb = sb("dtb", [H, 2, BLK], F32)
    Hst = sb("Hst", [128, H * 64], F32)
    prevT = sb("prevT", [128, H * 64], BF16)
    X = sb("X", [64, DI], BF16)
    Btok = sb("Btok", [64, G * 128], BF16)
    dts = sb("dts", [64, 2 * H], F32)
    acum = sb("acum", [64, H], F32)
    cdec = sb("cdec", [128, H], F32)
    w2 = sb("w2", [64, H], F32)
    Xw = [sb(f"Xw{i}", [64, 8, 64], BF16) for i in range(2)]
    if not prev:
        szb = sb("szb", [128, NDI, BLK], BF16)
        gbuf = sb("gbuf", [128, NDI, BLK], F32)
        ynb = sb("ynb", [128, NDI, BLK], BF16)
        sq = sb("sq", [128, 4, BLK], BF16)
        rstd = sb("rstd", [128, BLK], F32)
        Dg = [sb(f"Dg{i}", [64, 8, 64], F32) for i in range(2)]
        Eg = [sb(f"Eg{i}", [64, 8, 64], F32) for i in range(2)]
        EA = [sb(f"EA{i}", [128, 8, 64], BF16) for i in range(2)]
        LT = [sb(f"LT{i}", [64, 8, 64], BF16) for i in range(2)]
        MT = [sb(f"MT{i}", [64, 8, 64], BF16) for i in range(2)]
        Cs = [sb(f"Cs{i}", [128, 8, 64], BF16) for i in range(2)]
        Xd = [sb(f"Xd{i}", [64, 8, 64], BF16) for i in range(2)]
        cb = [sb(f"cb{i}", [64, 64], F32) for i in range(2)]
        ysb = [sb(f"ysb{i}", [64, 512], F32) for i in range(2)]
        DSI = sb("DSI", [64, H, 64], BF16)
        dsr = sb("dsr", [64, H], F32)
        d2 = sb("d2", [H, H], F32)
        self.ts("dve", d2[:, :], ident[:H, :H], self.hv[:, 2:3], None, ALU.mult, None, ("cst", "hv"), ("d2",))
        self.mm(self.ps[0][:64, :H], ones[:H, :64], d2[:, :], True, True, ("cst", "d2"), (("ps", 0),))
        self.cp("dve", dsr[:, :], self.ps[0][:64, :H], (("ps", 0),), ("dsr",))
        self.tt("dve", DSI[:, :, :], ident[:64, :64].unsqueeze(1).to_broadcast([64, H, 64]),
                dsr[:, :].unsqueeze(2).to_broadcast([64, H, 64]), ALU.mult, ("cst", "dsr"), ("DSI",))
    if prev:
        self.memset("pool", Hst[:, :], 0.0, ("Hst",))
    else:
        self.dma("sp", Hst[:, :], self.s_state[:, :], ("s_state",), ("Hst",), "ld0")
        self.ts("dve", Hst[:, :], Hst[:, :], self.flag[:, 0:1], None, ALU.mult, None, ("Hst", "flag"), ("Hst",))
        self.cp("act", prevT[:, :], Hst[:, :], ("Hst",), ("prevT",))

    def bc_h(ap2, n=8):
        return ap2.unsqueeze(2).to_broadcast([ap2.shape[0], n, 64])

    def bc_m(ap2, n=8):
        return ap2.unsqueeze(1).to_broadcast([ap2.shape[0], n, 64])

    it = 0
    import os
    NBLK_ = int(os.environ.get("SSD_NBLK", TT // BLK))
    SEC_ = int(os.environ.get("SSD_SEC", 9))
    for blk in range(min(NBLK_, TT // BLK)):
        t0 = blk * BLK
        tiles_needed = list(range(NCD)) if not prev else list(range(NDI + G))
        nld = len(tiles_needed)
        self.dma_tiles("sp", xbc, self.s_xbc, 0, nld, slice(t0, t0 + BLK), ("s_xbc",), ("xbc",), ("ld0", "ld2"))
        self.dma("sp", dtb[:, :, :], self.s_dt[:, :, t0:t0 + BLK].rearrange("a h t -> h a t"),
                 ("s_dt",), ("dtb",), "ld1")
        if not prev:
            self.dma_tiles("sp", szb, self.s_sz, 0, NDI, slice(t0, t0 + BLK), ("s_sz",), ("szb",), ("ld2", "ld1"))
        for ch in range(NCH if SEC_ > 0 else 0):
            lo = ch * 64
            self.tr(self.ps[0][:64, 0:H], dtb[:, 0, lo:lo + 64], ident[:H, :H], ("dtb", "cst"), (("ps", 0),))
            self.tr(self.ps[0][:64, H:2 * H], dtb[:, 1, lo:lo + 64], ident[:H, :H], ("dtb", "cst"), (("ps", 0),))
            self.cp("dve", dts[:, :], self.ps[0][:64, :2 * H], (("ps", 0),), ("dts",))
            if SEC_ < 2:
                continue
            self.mm(self.ps[1][:64, 0:H], triinc[:64, :64], dts[:, H:2 * H], True, True, ("cst", "dts"), (("ps", 1),))
            self.cp("act", acum[:, :], self.ps[1][:64, 0:H], (("ps", 1),), ("acum",))
            if SEC_ < 3:
                continue
            self.mm(self.ps[3][:, 0:2 * H], ones[:64, :], dts[:, 0:2 * H], True, True, ("cst", "dts"), (("ps", 3),))
            if SEC_ < 4:
                continue
            self.act(cdec[:, :], self.ps[3][:, H:2 * H], AF.Exp, (("ps", 3),), ("cdec",))
            if SEC_ < 5:
                continue
            self.cp("act", w2[:, :], self.ps[3][:64, H:2 * H], (("ps", 3),), ("w2",))
            self.tt("dve", w2[:, :], w2[:, :], acum[:, :], ALU.subtract, ("w2", "acum"), ("w2",))
            if SEC_ < 6:
                continue
            self.act(w2[:, :], w2[:, :], AF.Exp, ("w2",), ("w2",))
            if SEC_ < 7:
                continue
            self.tt("dve", w2[:, :], w2[:, :], dts[:, 0:H], ALU.mult, ("w2", "dts"), ("w2",))
            if SEC_ < 8:
                continue
            for c8 in range(0, NDI, 8):
                n8 = min(8, NDI - c8)
                for j in range(n8):
                    self.tr(self.psb[2][:64, j * 128:(j + 1) * 128], xbc[:, c8 + j, lo:lo + 64], self.identb[:, :],
                            ("xbc", "identb"), (("ps", 2),))
                self.cp("act" if (c8 // 8) % 2 == 0 else "dve", X[:, c8 * 128:(c8 + n8) * 128],
                        self.psb[2][:64, :n8 * 128], (("ps", 2),), ("X",))
            for g in range(G):
                self.tr(self.psb[2][:64, g * 128:(g + 1) * 128], xbc[:, NDI + g, lo:lo + 64], self.identb[:, :],
                        ("xbc", "identb"), (("ps", 2),))
            self.cp("dve", Btok[:, :], self.psb[2][:64, :G * 128], (("ps", 2),), ("Btok",))
            for g in range(G if SEC_ > 8 else 0):
                s = it % 2
                it += 1
                hs = slice(8 * g, 8 * g + 8)
                Xg = X[:, g * 512:(g + 1) * 512].rearrange("p (h e) -> p h e", h=8)
                if not prev:
                    Bt = xbc[:, NDI + g, lo:lo + 64]
                    Ct = xbc[:, NDI + G + g, lo:lo + 64]
                    self.tt("dve", Dg[s][:, :, :], bc_h(acum[:, hs]), bc_m(ident[:64, :64]), ALU.mult,
                            ("acum", "cst"), (f"Dg{s}",))
                    self.tt("dve", Eg[s][:, :, :], bc_m(negm[:64, :64]), bc_h(acum[:, hs]), ALU.subtract,
                            ("acum", "cst"), (f"Eg{s}",))
                    Dg2 = Dg[s][:, :, :].rearrange("p h l -> p (h l)")
                    Eg2 = Eg[s][:, :, :].rearrange("p h l -> p (h l)")
                    self.mm(self.ps[3][:, :], ones[:64, :], Dg2, True, True, ("cst", f"Dg{s}"), (("ps", 3),))
                    self.mm(self.ps[4][:64, :], ones[:64, :64], Dg2, True, False, ("cst", f"Dg{s}"), (("ps", 4),))
                    self.mm(self.ps[4][:64, :], ident[:64, :64], Eg2, False, True, ("cst", f"Eg{s}"), (("ps", 4),))
                    self.act(EA[s][:, :, :].rearrange("p h l -> p (h l)"), self.ps[3][:, :], AF.Exp,
                             (("ps", 3),), (f"EA{s}",))
                    self.act(LT[s][:, :, :].rearrange("p h l -> p (h l)"), self.ps[4][:64, :], AF.Exp,
                             (("ps", 4),), (f"LT{s}",))
                    self.mm(self.ps[5][:64, 0:64], Bt, Ct, True, True, ("xbc",), (("ps", 5),))
                    self.cp("act", cb[s][:, :], self.ps[5][:64, 0:64], (("ps", 5),), (f"cb{s}",))
                    self.tt("dve", MT[s][:, :, :], LT[s][:, :, :], bc_m(cb[s][:, :]), ALU.mult,
                            (f"LT{s}", f"cb{s}"), (f"MT{s}",))
                    self.tt("pool", Cs[s][:, :, :], EA[s][:, :, :], bc_m(Ct), ALU.mult,
                            (f"EA{s}", "xbc"), (f"Cs{s}",))
                    self.tt("pool", Xd[s][:, :, :], Xg, bc_h(dts[:, hs]), ALU.mult, ("X", "dts"), (f"Xd{s}",))
                self.tt("dve", Xw[s][:, :, :], Xg, bc_h(w2[:, hs]), ALU.mult, ("X", "w2"), (f"Xw{s}",))
                if not prev:
                    for h in range(8):
                        hg = 8 * g + h
                        yo = self.ps[6][:64, h * 64:(h + 1) * 64]
                        self.mm(yo, MT[s][:, h, :], Xd[s][:, h, :], True, False, (f"MT{s}", f"Xd{s}"), (("ps", 6),))
                        self.mm(yo, DSI[:, hg, :], Xg[:, h, :], False, False, ("DSI", "X"), (("ps", 6),))
                        self.mm(yo, Cs[s][:, h, :], prevT[:, hg * 64:(hg + 1) * 64], False, True,
                                (f"Cs{s}", "prevT"), (("ps", 6),))
                    self.cp("act", ysb[s][:, :], self.ps[6][:64, :], (("ps", 6),), (f"ysb{s}",))
                    for j in range(4):
                        self.tr(self.ps[5][:, 128 + j * 64:128 + (j + 1) * 64], ysb[s][:, j * 128:(j + 1) * 128],
                                ident[:64, :64], (f"ysb{s}", "cst"), (("ps", 5),))
                    self.tt("dve", gbuf[:, 4 * g:4 * g + 4, lo:lo + 64],
                            self.ps[5][:, 128:384].rearrange("p (j l) -> p j l", j=4),
                            szb[:, 4 * g:4 * g + 4, lo:lo + 64], ALU.mult, (("ps", 5), "szb"), ("gbuf",))
                self.mm(self.ps[7][:, :], Btok[:, g * 128:(g + 1) * 128], Xw[s][:, :, :].rearrange("p h e -> p (h e)"),
                        True, True, ("Btok", f"Xw{s}"), (("ps", 7),))
                Hg = Hst[:, g * 512:(g + 1) * 512]
                Hg3 = Hg.rearrange("p (h e) -> p h e", h=8)
                self.tt("dve", Hg3, Hg3, bc_h(cdec[:, hs]), ALU.mult, ("Hst", "cdec"), ("Hst",))
                self.tt("dve", Hg, Hg, self.ps[7][:, :], ALU.add, ("Hst", ("ps", 7)), ("Hst",))
                if not prev:
                    self.cp("act", prevT[:, g * 512:(g + 1) * 512], Hg, ("Hst",), ("prevT",))
        if not prev:
            vn, vkey = self.v_ssd_norm
            for g in range(G):
                self.act(sq[:, :, :], gbuf[:, 4 * g:4 * g + 4, :], AF.Square, ("gbuf",), ("sq",))
                for j in range(4):
                    self.mm(self.ps[1][:, :BLK], self.onesb[:, :], sq[:, j, :], j == 0, j == 3,
                            ("onesb", "sq"), (("ps", 1),))
                self.act(rstd[:, :], self.ps[1][:, :BLK], AF.Ln, (("ps", 1),), ("rstd",), bias=EPS, scale=1.0 / 512)
                self.act(rstd[:, :], rstd[:, :], AF.Exp, ("rstd",), ("rstd",), scale=-0.5)
                for j in range(4):
                    ct = 4 * g + j
                    self.stt(ynb[:, ct, :], gbuf[:, ct, :], vn[:, ct:ct + 1], rstd[:, :], ALU.mult, ALU.mult,
                             ("gbuf", "rstd", vkey), ("ynb",))
            self.dma_tiles("act", ynb, self.s_yn, 0, NDI, slice(t0, t0 + BLK), ("ynb",), ("s_yn",), ("st0", "st1"),
                           store=True)
    if prev:
        self.dma("act", self.s_state[:, :], Hst[:, :], ("Hst",), ("s_state",), "st1")


Builder.phase_ssd = phase_ssd


def phase_attn(self):
    cfg = self.cfg
    D, TT, NH = cfg.D, cfg.TT, cfg.NH
    self.phase_reset()
    sb = self.sb
    NQG = TT // 512
    NPB = TT // 128
    m01f = sb("m01f", [128, 4, 512], F32)
    mneg = sb("mneg", [128, 4, 512], F32)
    m01 = sb("m01", [128, 4, 512], BF16)
    self.dma("sp", m01f[:, :, :], self.I["cmask"][:, 0:2048].rearrange("p (k t) -> p k t", k=4), (), ("m01f",), "ld0")
    self.dma("sp", mneg[:, :, :], self.I["cmask"][:, 2048:4096].rearrange("p (k t) -> p k t", k=4), (), ("mneg",), "ld1")
    self.cp("dve", m01[:, :, :], m01f[:, :, :], ("m01f",), ("m01",))
    Kh = [sb(f"Kh{i}", [128, 2 * TT], BF16) for i in range(2)]
    Qh = [sb(f"Qh{i}", [128, TT], BF16) for i in range(2)]
    Vs = [sb(f"Vs{i}", [128, 2 * TT], BF16) for i in range(2)]
    Vh = [sb(f"Vh{i}", [128, 2 * TT // 128, 128], BF16) for i in range(2)]
    oT = [sb(f"oT{i}", [128, TT], BF16) for i in range(2)]
    NZ = 4
    zbanks = [0, 1, 5, 6]
    et = [sb(f"et{i}", [128, 512], F32) for i in range(NZ)]
    sp_ = [sb(f"sp{i}", [128, 512], BF16) for i in range(NZ + 1)]
    T2 = [sb(f"T2{i}", [128, 512], F32) for i in range(NZ)]
    Wt = [sb(f"Wt{i}", [128, 512], BF16) for i in range(NZ)]
    At = [sb(f"At{i}", [128, 512], BF16) for i in range(3)]
    nkb_all = 2 * TT // 128

    def head_load(h):
        s = h % 2
        self.dma("sp", Kh[s][:, :], self.s_k[h, :, :], ("s_k",), (f"Kh{s}",), f"ld{s}")
        self.dma("sp", Qh[s][:, :], self.s_q[h, :, :], ("s_q",), (f"Qh{s}",), "ld2")
        self.dma("sp", Vs[s][:, :], self.s_vT[h, :, :], ("s_vT",), (f"Vs{s}",), f"ld{s}")
        for k8 in range(0, nkb_all, 8):
            for j in range(8):
                kb = k8 + j
                self.tr(self.psb[4][:, j * 128:(j + 1) * 128], Vs[s][:, kb * 128:(kb + 1) * 128], self.identb[:, :],
                        (f"Vs{s}", "identb"), (("ps", 4),))
            self.cp("dve", Vh[s][:, k8:k8 + 8, :], self.psb[4][:, :].rearrange("p (j d) -> p j d", j=8),
                    (("ps", 4),), (f"Vh{s}",))

    its = []
    og = 0
    for h in range(NH):
        for qg in range(NQG):
            nk = NPB + 4 * (qg + 1)
            ob = 2 + (og % 2)
            og += 1
            for idx, kb in enumerate(range(nk - 1, -1, -1)):
                its.append(dict(h=h, qg=qg, kb=kb, first=(idx == 0), last=(kb == 0),
                                kk=kb - (NPB + 4 * qg), ob=ob, newhead=(qg == 0 and idx == 0)))
    acur = [None]
    acnt = [0]

    def stageA(i):
        c = its[i]
        if c["newhead"]:
            head_load(c["h"])
        s = c["h"] % 2
        z = i % NZ
        spi = i % (NZ + 1)
        zb = self.ps[zbanks[z]]
        kb, qg, kk = c["kb"], c["qg"], c["kk"]
        self.mm(zb[:, :], Kh[s][:, kb * 128:(kb + 1) * 128], Qh[s][:, qg * 512:(qg + 1) * 512], True, False,
                (f"Kh{s}", f"Qh{s}"), (("ps", zbanks[z]),))
        self.act(et[z][:, :], zb[:, :], AF.Exp, (("ps", zbanks[z]),), (f"et{z}",))
        self.act(sp_[spi][:, :], et[z][:, :], AF.Ln, (f"et{z}",), (f"sp{spi}",), bias=1.0)
        if kk >= 0:
            self.tt("pool", sp_[spi][:, :], sp_[spi][:, :], m01[:, kk, :], ALU.mult,
                    (f"sp{spi}", "m01"), (f"sp{spi}",))

    def stageB(i):
        c = its[i]
        s = c["h"] % 2
        z = i % NZ
        spi = i % (NZ + 1)
        zbk = zbanks[z]
        zb = self.ps[zbk]
        kb, qg, kk, first, last, ob = c["kb"], c["qg"], c["kk"], c["first"], c["last"], c["ob"]
        self.mm(zb[:, :], self.nstrictb[:, :], sp_[spi][:, :], False, first,
                ("nstrictb", f"sp{spi}"), (("ps", zbk),))
        if not first:
            self.mm(zb[:, :], self.nonesb[:, :], acur[0][0][:, :], False, True,
                    ("nonesb", acur[0][1]), (("ps", zbk),))
        self.tt("dve", T2[z][:, :], zb[:, :], sp_[spi][:, :], ALU.subtract,
                (("ps", zbk), f"sp{spi}"), (f"T2{z}",))
        if kk >= 0:
            self.tt("pool", T2[z][:, :], T2[z][:, :], mneg[:, kk, :], ALU.add, (f"T2{z}", "mneg"), (f"T2{z}",))
        self.act(Wt[z][:, :], T2[z][:, :], AF.Exp, (f"T2{z}",), (f"Wt{z}",))
        self.mm(self.ps[ob][:, :], Vh[s][:, kb, :], Wt[z][:, :], first, last,
                (f"Vh{s}", f"Wt{z}"), (("ps", ob),))
        if not last:
            if first:
                acur[0] = (sp_[spi], f"sp{spi}")
            else:
                a = acnt[0] % 3
                acnt[0] += 1
                self.tt("pool", At[a][:, :], acur[0][0][:, :], sp_[spi][:, :], ALU.add,
                        (acur[0][1], f"sp{spi}"), (f"At{a}",))
                acur[0] = (At[a], f"At{a}")
        else:
            self.act(oT[s][:, qg * 512:(qg + 1) * 512], self.ps[ob][:, :], AF.Copy, (("ps", ob),), (f"oT{s}",))
            if qg == NQG - 1:
                h = c["h"]
                self.dma("act", self.s_o[h, :, :], oT[s][:, :], (f"oT{s}",), (("s_o", h),), f"st{s}")

    n = len(its)
    stageA(0)
    for i in range(n):
        if i + 1 < n:
            stageA(i + 1)
        stageB(i)


Builder.phase_attn = phase_attn


def res_epi(self, xo, t0, TTg):
    nb = (TTg + 511) // 512
    cnt = [0]

    def epi(i, banks, wd):
        s = cnt[0] % 2
        cnt[0] += 1
        self.dma("sp", xo[s][:, :TTg], self.s_xT[i, :, t0:t0 + TTg], (("s_xT", i, t0),), (f"xo{s}",), f"ld{s}")
        for b in range(nb):
            w = min(512, TTg - b * 512)
            self.tt("dve", xo[s][:, b * 512:b * 512 + w], xo[s][:, b * 512:b * 512 + w], self.ps[banks[b]][:, :w],
                    ALU.add, (f"xo{s}", ("ps", banks[b])), (f"xo{s}",))
        self.dma("act", self.s_xT[i, :, t0:t0 + TTg], xo[s][:, :TTg], (f"xo{s}",), (("s_xT", i, t0),), f"st{s}")
    return epi


def phase_merge(self):
    cfg = self.cfg
    D, TT, DI = cfg.D, cfg.TT, cfg.DI
    KT = D // 128
    I = self.I
    assert DI == D
    nb = (TT + 511) // 512
    tiles = [(j * 128, 128) for j in range(KT)]
    cnt = [0]
    self.phase_reset()
    gemm_setup(self, D, TT)
    ob = [self.sb(f"ob{i}", [128, TT], BF16) for i in range(2)]
    gemm_load_panel(self, self.s_yn, "s_yn")

    def epi1(i, banks, wd):
        s = cnt[0] % 2
        cnt[0] += 1
        for b in range(nb):
            w = min(512, TT - b * 512)
            self.act(ob[s][:, b * 512:b * 512 + w], self.ps[banks[b]][:, :w], AF.Copy, (("ps", banks[b]),), (f"ob{s}",))
        self.dma("act", self.s_bs[i, :, :], ob[s][:, :], (f"ob{s}",), (("s_bs", i),), f"st{s}")
    gemm(self, I["w_ssd_out"], tiles, epi1)
    self.phase_reset()
    gemm_setup(self, D, TT)
    ob2 = [self.sb(f"ob{i}", [128, TT], BF16) for i in range(2)]
    g1 = self.sb("g1", [128, TT], BF16)
    g2 = self.sb("g2", [128, TT], BF16)
    bsb = self.sb("bsb", [128, TT], BF16)
    t1 = self.sb("t1", [128, TT], F32)
    gemm_load_panel(self, self.s_o, "s_o")

    def epi2(i, banks, wd):
        s = cnt[0] % 2
        cnt[0] += 1
        self.dma("sp", g1[:, :], self.s_gate[i, :, :], ("s_gate",), ("g1",), "ld0")
        self.dma("sp", g2[:, :], self.s_gate[KT + i, :, :], ("s_gate",), ("g2",), "ld1")
        self.dma("sp", bsb[:, :], self.s_bs[i, :, :], ("s_bs",), ("bsb",), "ld2")
        self.tt("pool", t1[:, :], g1[:, :], bsb[:, :], ALU.mult, ("g1", "bsb"), ("t1",))
        for b in range(nb):
            w = min(512, TT - b * 512)
            sl = slice(b * 512, b * 512 + w)
            self.tt("dve", g2[:, sl], g2[:, sl], self.ps[banks[b]][:, :w], ALU.mult, ("g2", ("ps", banks[b])), ("g2",))
        self.tt("dve", ob2[s][:, :], g2[:, :], t1[:, :], ALU.add, ("g2", "t1"), (f"ob{s}",))
        self.dma("act", self.s_mg[i, :, :], ob2[s][:, :], (f"ob{s}",), (("s_mg", i),), f"st{s}")
    gemm(self, I["w_sb_out"], tiles, epi2)
    self.phase_reset()
    gemm_setup(self, D, TT)
    xo = [self.sb(f"xo{i}", [128, TT], F32) for i in range(2)]
    gemm_load_panel(self, self.s_mg, "s_mg")
    gemm(self, I["w_out"], tiles, res_epi(self, xo, 0, TT))


Builder.phase_merge = phase_merge


def phase_xattn(self):
    cfg = self.cfg
    D, TT, NM = cfg.D, cfg.TT, cfg.NMEM
    KT = D // 128
    XH, XD = cfg.XH, cfg.XD
    NDT = XD // 128
    I = self.I
    self.phase_transpose_in(I["mem"], "mem", TT=NM, dst=self.s_mT)
    self.phase_norm(self.v_norm_mem, "mem", src=self.s_mT, dst=self.s_hm, ntok=NM)
    self.phase_reset()
    gemm_setup(self, D, NM)
    obm = [self.sb(f"obm{i}", [128, NM], BF16) for i in range(2)]
    gemm_load_panel(self, self.s_hm, "s_hm")
    cnt = [0]

    def epikv(i, banks, wd):
        s = cnt[0] % 2
        cnt[0] += 1
        self.act(obm[s][:, :], self.ps[banks[0]][:, :NM], AF.Copy, (("ps", banks[0]),), (f"obm{s}",))
        dst = self.s_mk if i < KT else self.s_mv
        self.dma("act", dst[i % KT, :, :], obm[s][:, :], (f"obm{s}",), ((dst.name, i),), f"st{s}")
    gemm(self, I["w_xa_kv"], [(j * 128, 128) for j in range(2 * KT)], epikv)
    self.phase_norm(self.v_norm_xa, "xa")
    self.phase_reset()
    gemm_setup(self, D, TT)
    nb = (TT + 511) // 512
    ob = [self.sb(f"ob{i}", [128, TT], BF16) for i in range(2)]
    xo = [self.sb(f"xo{i}", [128, TT], F32) for i in range(2)]
    gemm_load_panel(self, self.s_h, "s_h")

    def epiq(i, banks, wd):
        s = cnt[0] % 2
        cnt[0] += 1
        for b in range(nb):
            w = min(512, TT - b * 512)
            self.act(ob[s][:, b * 512:b * 512 + w], self.ps[banks[b]][:, :w], AF.Copy, (("ps", banks[b]),), (f"ob{s}",),
                     scale=float(XD ** -0.5))
        self.dma("act", self.s_xq[i, :, :], ob[s][:, :], (f"ob{s}",), (("s_xq", i),), f"st{s}")
    tiles = [(j * 128, 128) for j in range(KT)]
    gemm(self, I["w_xa_q"], tiles, epiq)
    self.phase_reset()
    sb = self.sb
    NMT = NM // 128
    mk = sb("mk", [128, NDT, NM], BF16)
    mvs = sb("mvs", [128, NDT, NM], BF16)
    mvt = sb("mvt", [128, NMT, XD], BF16)
    xq = sb("xq", [128, NDT, TT], BF16)
    sc = [sb(f"sc{i}", [128, NM], F32) for i in range(2)]
    mx = [sb(f"mx{i}", [128, 2], F32) for i in range(2)]
    pb = [sb(f"pb{i}", [128, NM], BF16) for i in range(2)]
    pT = [sb(f"pT{i}", [128, NMT, 512], BF16) for i in range(2)]
    oo = [sb(f"oo{i}", [128, 512], BF16) for i in range(2)]
    it = 0
    for xh in range(XH):
        k0 = xh * NDT
        self.dma("sp", mk[:, :, :], self.s_mk[k0:k0 + NDT, :, :].rearrange("k p t -> p k t"), ("s_mk",), ("mk",), "ld0")
        self.dma("sp", mvs[:, :, :], self.s_mv[k0:k0 + NDT, :, :].rearrange("k p t -> p k t"), ("s_mv",), ("mvs",), "ld1")
        self.dma("sp", xq[:, :, :], self.s_xq[k0:k0 + NDT, :, :].rearrange("k p t -> p k t"), ("s_xq",), ("xq",), "ld2")
        for dvt in range(NDT):
            for mt in range(NMT):
                self.tr(self.psb[7][:, mt * 128:(mt + 1) * 128], mvs[:, dvt, mt * 128:(mt + 1) * 128], self.identb[:, :],
                        ("mvs", "identb"), (("ps", 7),))
            self.cp("dve", mvt[:, :, dvt * 128:(dvt + 1) * 128],
                    self.psb[7][:, :NMT * 128].rearrange("p (m d) -> p m d", m=NMT), (("ps", 7),), ("mvt",))
        for tg in range(TT // 512):
            pg = tg % 2
            for t4 in range(4):
                tq = tg * 4 + t4
                z = it % 2
                it += 1
                for dt_ in range(NDT):
                    self.mm(self.ps[z][:, :NM], xq[:, dt_, tq * 128:(tq + 1) * 128], mk[:, dt_, :],
                            dt_ == 0, dt_ == NDT - 1, ("xq", "mk"), (("ps", z),))
                self.P.add("dve", (lambda zz: (lambda e: e.tensor_reduce(out=mx[zz][:, 0:1], in_=self.ps[zz][:, :NM],
                                                                          axis=AX.X, op=ALU.max)))(z),
                           (("ps", z),), (f"mx{z}",))
                self.ts("dve", mx[z][:, 0:1], mx[z][:, 0:1], -1.0, None, ALU.mult, None, (f"mx{z}",), (f"mx{z}",))
                self.act(sc[z][:, :], self.ps[z][:, :NM], AF.Exp, (("ps", z), f"mx{z}"), (f"sc{z}", f"mx{z}"),
                         bias=mx[z][:, 0:1], accum_out=mx[z][:, 1:2])
                self.P.add("dve", (lambda zz: (lambda e: e.reciprocal(out=mx[zz][:, 1:2], in_=mx[zz][:, 1:2])))(z),
                           (f"mx{z}",), (f"mx{z}",))
                self.ts("dve", pb[z][:, :], sc[z][:, :], mx[z][:, 1:2], None, ALU.mult, None,
                        (f"sc{z}", f"mx{z}"), (f"pb{z}",))
                for mt in range(NMT):
                    self.tr(self.psb[4 + z][:, mt * 128:(mt + 1) * 128], pb[z][:, mt * 128:(mt + 1) * 128],
                            self.identb[:, :], (f"pb{z}", "identb"), (("ps", 4 + z),))
                self.cp("dve" if z else "act", pT[pg][:, :, t4 * 128:(t4 + 1) * 128],
                        self.psb[4 + z][:, :NMT * 128].rearrange("p (m t) -> p m t", m=NMT),
                        (("ps", 4 + z),), (f"pT{pg}",))
            for dvt in range(NDT):
                o = it % 2
                it += 1
                for mt in range(NMT):
                    self.mm(self.ps[2 + o][:, :], mvt[:, mt, dvt * 128:(dvt + 1) * 128], pT[pg][:, mt, :],
                            mt == 0, mt == NMT - 1, ("mvt", f"pT{pg}"), (("ps", 2 + o),))
                self.act(oo[o][:, :], self.ps[2 + o][:, :], AF.Copy, (("ps", 2 + o),), (f"oo{o}",))
                self.dma("act", self.s_xo[k0 + dvt, :, tg * 512:(tg + 1) * 512], oo[o][:, :], (f"oo{o}",),
                         (("s_xo", k0 + dvt, tg),), f"st{o}")
    self.phase_reset()
    gemm_setup(self, D, TT)
    xo = [self.sb(f"xo{i}", [128, TT], F32) for i in range(2)]
    gemm_load_panel(self, self.s_xo, "s_xo")
    gemm(self, I["w_xa_o"], tiles, res_epi(self, xo, 0, TT))


Builder.phase_xattn = phase_xattn


def phase_ffn(self):
    cfg = self.cfg
    D, TT, DFF = cfg.D, cfg.TT, cfg.DFF
    KT = D // 128
    NF = DFF // 128
    I = self.I
    self.phase_norm(self.v_norm_ffn, "ffn")
    self.phase_reset()
    gemm_setup(self, D, TT)
    nb = (TT + 511) // 512
    sg = self.sb("sg", [128, TT], BF16)
    ob = [self.sb(f"ob{i}", [128, TT], BF16) for i in range(2)]
    gemm_load_panel(self, self.s_h, "s_h")
    tiles = []
    for j in range(NF):
        tiles.append((j * 128, 128))
        tiles.append((DFF + j * 128, 128))
    cnt = [0]

    def epi(i, banks, wd):
        j = i // 2
        if i % 2 == 0:
            for b in range(nb):
                w = min(512, TT - b * 512)
                self.act(sg[:, b * 512:b * 512 + w], self.ps[banks[b]][:, :w], AF.Silu, (("ps", banks[b]),), ("sg",))
        else:
            s = cnt[0] % 2
            cnt[0] += 1
            for b in range(nb):
                w = min(512, TT - b * 512)
                sl = slice(b * 512, b * 512 + w)
                self.tt("dve", ob[s][:, sl], sg[:, sl], self.ps[banks[b]][:, :w], ALU.mult,
                        ("sg", ("ps", banks[b])), (f"ob{s}",))
            self.dma("act", self.s_act[j, :, :], ob[s][:, :], (f"ob{s}",), (("s_act", j),), f"st{s}")
    gemm(self, I["w_ffn_in"], tiles, epi)
    self.phase_reset()
    TS = min(512, TT)
    gemm_setup(self, DFF, TS)
    xo = [self.sb(f"xo{i}", [128, TS], F32) for i in range(2)]
    otiles = [(j * 128, 128) for j in range(KT)]
    for t0 in range(0, TT, TS):
        gemm_load_panel(self, self.s_act, "s_act", t0=t0)
        gemm(self, I["w_ffn_out"], otiles, res_epi(self, xo, t0, TS))


Builder.phase_ffn = phase_ffn
```

```python
import numpy as np
import ml_dtypes
import concourse.bass as bass
import concourse.mybir as mybir
from concourse.bass_utils import run_bass_kernel_spmd
from contextlib import ExitStack

F32 = mybir.dt.float32
BF16 = mybir.dt.bfloat16
AF = mybir.ActivationFunctionType
ALU = mybir.AluOpType
AX = mybir.AxisListType

COMPUTE = ("pe", "act", "dve", "pool")
ALLENG = ("pe", "act", "dve", "pool", "sp")
EPS = 1e-6
NEG = -30000.0
SAME_ENGINE_SYNC = True


class Op:
    __slots__ = ("eng", "fn", "reads", "writes", "chan", "idx", "deps", "sig",
                 "waits", "ordinal", "needs_sig")

    def __init__(self, eng, fn, reads, writes, chan, idx):
        self.eng = eng
        self.fn = fn
        self.reads = reads
        self.writes = writes
        self.chan = chan
        self.idx = idx
        self.deps = set()
        self.sig = None
        self.waits = []
        self.ordinal = 0
        self.needs_sig = False


class Prog:
    EPOCH = 12000

    def __init__(self, nc, same_engine_sync=True):
        self.nc = nc
        self.ops = []
        self.same_engine_sync = same_engine_sync
        self.barriers = []

    def add(self, eng, fn, reads=(), writes=(), chan=None):
        op = Op(eng, fn, tuple(reads), tuple(writes), chan, len(self.ops))
        self.ops.append(op)
        return op

    def barrier(self):
        self.barriers.append(len(self.ops))

    def analyze(self):
        ops = self.ops
        last_writer = {}
        readers = {}
        chan_last = {}
        chan_count = {}
        eng_last = {}
        bset = set(self.barriers)
        pending = {}
        for op in ops:
            if op.idx in bset:
                deps = set(eng_last.values()) | set(chan_last.values())
                for e in ALLENG:
                    pending.setdefault(e, set()).update(deps)
            d = op.deps
            if op.eng in pending:
                d |= pending.pop(op.eng)
            for k in op.reads:
                if k in last_writer:
                    d.add(last_writer[k])
            for k in op.writes:
                if k in last_writer:
                    d.add(last_writer[k])
                r = readers.get(k)
                if r:
                    for kk, v in r.items():
                        if kk == "dma":
                            d.update(v)
                        else:
                            d.add(v)
            if op.chan is not None:
                if op.chan in chan_last:
                    d.add(chan_last[op.chan])
                chan_last[op.chan] = op.idx
                chan_count[op.chan] = chan_count.get(op.chan, 0) + 1
                op.ordinal = chan_count[op.chan]
            d.discard(op.idx)
            for k in op.reads:
                r = readers.setdefault(k, {})
                if op.chan is not None:
                    r.setdefault("dma", []).append(op.idx)
                else:
                    r[op.eng] = op.idx
            for k in op.writes:
                last_writer[k] = op.idx
                readers[k] = {}
            if op.chan is None:
                eng_last[op.eng] = op.idx
        for op in ops:
            for j in op.deps:
                dj = ops[j]
                if dj.chan is None:
                    if dj.eng != op.eng or op.chan is not None:
                        dj.needs_sig = True
                    elif self.same_engine_sync and dj.eng != "pe":
                        dj.needs_sig = True
        cnt = {e: 0 for e in COMPUTE}
        for op in ops:
            if op.chan is None and op.needs_sig:
                assert op.fn is not None
                cnt[op.eng] += 1
                op.sig = cnt[op.eng]
        self.sig_counts = cnt
        self.chans = sorted(chan_count.keys())
        waited = {e: {} for e in ALLENG}
        for op in ops:
            w = waited[op.eng]
            need = {}
            for j in op.deps:
                dj = ops[j]
                if dj.chan is not None:
                    key = ("c", dj.chan)
                    val = 16 * dj.ordinal
                else:
                    if dj.sig is None:
                        continue
                    if dj.eng == op.eng and op.chan is None and (
                            not self.same_engine_sync or dj.eng == "pe"):
                        continue
                    ep = (dj.sig - 1) // self.EPOCH
                    key = ("e", dj.eng, ep)
                    val = dj.sig - ep * self.EPOCH
                if need.get(key, 0) < val:
                    need[key] = val
            for key, val in need.items():
                if w.get(key, 0) < val:
                    w[key] = val
                    op.waits.append((key, val))

    def emit(self):
        nc = self.nc
        self.analyze()
        sems = {}
        with ExitStack() as es:
            for e in COMPUTE:
                nep = (self.sig_counts[e] + self.EPOCH - 1) // self.EPOCH
                for ep in range(max(nep, 1)):
                    sems[("e", e, ep)] = es.enter_context(nc.semaphore(f"s_{e}_{ep}"))
            for c in self.chans:
                sems[("c", c)] = es.enter_context(nc.semaphore(f"c_{c}"))
            self.nsems = len(sems)
            block = es.enter_context(nc.Block())
            per_eng = {e: [op for op in self.ops if op.eng == e] for e in ALLENG}
            EP = self.EPOCH

            def run(engobj, lst):
                for op in lst:
                    for key, val in op.waits:
                        engobj.wait_ge(sems[key], val)
                    if op.fn is None:
                        continue
                    ins = op.fn(engobj)
                    if op.chan is not None:
                        ins.then_inc(sems[("c", op.chan)], 16)
                    elif op.sig is not None:
                        ep = (op.sig - 1) // EP
                        ins.then_inc(sems[("e", op.eng, ep)], 1)

            @block.tensor
            def _(e):
                run(e, per_eng["pe"])

            @block.scalar
            def _(e):
                run(e, per_eng["act"])

            @block.vector
            def _(e):
                run(e, per_eng["dve"])

            @block.gpsimd
            def _(e):
                run(e, per_eng["pool"])

            @block.sync
            def _(e):
                run(e, per_eng["sp"])


class Cfg:
    def __init__(self, D=4096, TT=2048, G=8, NMEM=256, DFF=11008, stop_after=None):
        self.D = D
        self.TT = TT
        self.G = G
        self.H = 8 * G
        self.DI = 512 * G
        self.CD = self.DI + 2 * G * 128
        self.NH = D // 128
        self.NMEM = NMEM
        self.DFF = DFF
        self.XH = 4
        self.XD = D // 4
        sizes = (self.DI, self.CD, self.H, D, D, D, 2 * D)
        self.off = np.concatenate([[0], np.cumsum(sizes)]).astype(int)
        self.NIN = int(self.off[-1])
        self.stop_after = stop_after


def make_consts():
    c = {}
    c["ident"] = np.eye(128, dtype=np.float32)
    i = np.arange(128)
    c["ones"] = np.ones((128, 128), np.float32)
    c["triinc"] = (i[:, None] <= i[None, :]).astype(np.float32)
    c["negm"] = np.where(i[None, :] < i[:, None], NEG, 0.0).astype(np.float32)
    c["nstrict"] = -(i[:, None] > i[None, :]).astype(np.float32)
    t = np.arange(512)
    m = np.stack([((kb * 128 + i)[:, None] < t[None, :]).astype(np.float32) for kb in range(4)])
    c["m01"] = m.transpose(1, 0, 2).reshape(128, 4 * 512)
    c["mneg"] = ((m - 1.0) * (-NEG)).transpose(1, 0, 2).reshape(128, 4 * 512)
    names = ["ident", "ones", "triinc", "negm", "nstrict"]
    offs = {}
    o = 0
    for n in names:
        offs[n] = (o, c[n].shape[1])
        o += c[n].shape[1]
    packed = np.concatenate([c[n] for n in names], axis=1).astype(np.float32)
    masks = np.concatenate([c["m01"], c["mneg"]], axis=1).astype(np.float32)
    return packed, offs, masks


class Builder:
    def __init__(self, cfg):
        self.cfg = cfg
        self.nc = bass.Bass("TRN2", target_bir_lowering=False)
        self.P = Prog(self.nc, same_engine_sync=SAME_ENGINE_SYNC)
        self.arena_top = 0
        self.arena_base = 0
        self.uid = 0
        self.dram = {}
        self.psum = []
        self.done = False

    def sb(self, name, shape, dt):
        nbytes = int(np.prod(shape[1:])) * (4 if dt == F32 else 2)
        nbytes = (nbytes + 63) // 64 * 64
        self.uid += 1
        t = self.nc.alloc_sbuf_tensor_at(f"{name}_{self.uid}", list(shape), dt, offset=self.arena_top)
        self.arena_top += nbytes
        assert self.arena_top <= self.sb_limit, (name, self.arena_top, self.sb_limit)
        return t

    def phase_reset(self):
        self.P.barrier()
        self.arena_top = self.arena_base

    def dr(self, name, shape, dt):
        t = self.nc.dram_tensor(name, list(shape), dt, kind="Internal")
        self.dram[name] = t
        return t

    def dma(self, q, out, in_, reads, writes, chan, slow=False):
        if slow:
            self.P.add(q, lambda e: e.dma_start(out=out, in_=in_, allow_slow_non_contiguous=True),
                       reads, writes, chan=chan)
        else:
            self.P.add(q, lambda e: e.dma_start(out=out, in_=in_), reads, writes, chan=chan)

    def dma_tiles(self, q, sb_tile, dram, k0, kn, tsl, reads, writes, chans, store=False, step=8):
        for i, a in enumerate(range(0, kn, step)):
            n = min(step, kn - a)
            d = dram[k0 + a:k0 + a + n, :, tsl].rearrange("k p t -> p k t")
            t_ = sb_tile[:, a:a + n, :]
            ch = chans[i % len(chans)]
            if store:
                self.dma(q, d, t_, reads, writes, ch)
            else:
                self.dma(q, t_, d, reads, writes, ch)

    def mm(self, out, lhsT, rhs, start, stop, reads, writes, **kw):
        self.P.add("pe", lambda e: e.matmul(out, lhsT, rhs, start=start, stop=stop, **kw), reads, writes)

    def tr(self, out, in_, ident, reads, writes):
        self.P.add("pe", lambda e: e.transpose(out, in_, ident), reads, writes)

    def act(self, out, in_, func, reads, writes, bias=None, scale=None, accum_out=None, eng="act"):
        kw = {}
        if bias is not None:
            kw["bias"] = bias
        if scale is not None:
            kw["scale"] = scale
        if accum_out is not None:
            kw["accum_out"] = accum_out
        self.P.add("act", lambda e: e.activation(out=out, in_=in_, func=func, **kw), reads, writes)

    def tt(self, eng, out, in0, in1, op, reads, writes):
        self.P.add(eng, lambda e: e.tensor_tensor(out=out, in0=in0, in1=in1, op=op), reads, writes)

    def ts(self, eng, out, in0, s1, s2, op0, op1, reads, writes):
        if op1 is None:
            self.P.add(eng, lambda e: e.tensor_scalar(out=out, in0=in0, scalar1=s1, scalar2=None, op0=op0),
                       reads, writes)
        else:
            self.P.add(eng, lambda e: e.tensor_scalar(out=out, in0=in0, scalar1=s1, scalar2=s2, op0=op0, op1=op1),
                       reads, writes)

    def stt(self, out, in0, scalar, in1, op0, op1, reads, writes):
        self.P.add("dve", lambda e: e.scalar_tensor_tensor(out=out, in0=in0, scalar=scalar, in1=in1,
                                                           op0=op0, op1=op1), reads, writes)

    def cp(self, eng, out, in_, reads, writes):
        if eng == "act":
            self.P.add("act", lambda e: e.copy(out=out, in_=in_), reads, writes)
        else:
            self.P.add(eng, lambda e: e.tensor_copy(out=out, in_=in_), reads, writes)

    def memset(self, eng, ap, val, writes):
        self.P.add(eng, lambda e: e.memset(ap, val), (), writes)

    def build(self):
        cfg = self.cfg
        nc = self.nc
        D, TT, G, H, DI, CD, NH = cfg.D, cfg.TT, cfg.G, cfg.H, cfg.DI, cfg.CD, cfg.NH
        KT = D // 128

        early = cfg.stop_after in ("tin", "pinproj", "pssd", "inproj", "ssd", "attn", "ssdonly")
        self.tiny = set()

        def ein(name, shape):
            if early and name in ("w_ffn_in", "w_ffn_out", "w_xa_kv", "w_xa_q", "w_xa_o", "w_out", "w_ssd_out",
                                  "w_sb_out") or (cfg.stop_after == "ssdonly" and name == "w_in"):
                shape = [128, 128]
                self.tiny.add(name)
            return nc.dram_tensor(name, list(shape), F32, kind="ExternalInput")

        self.packed, self.coffs, self.masks = make_consts()
        I = {}
        I["x_prev"] = ein("x_prev", [TT, D])
        I["x_own"] = ein("x_own", [TT, D])
        I["mem"] = ein("mem", [cfg.NMEM, D])
        I["flag"] = ein("flag", [128, 1])
        I["consts"] = ein("consts", list(self.packed.shape))
        I["cmask"] = ein("cmask", list(self.masks.shape))
        I["norm_mix"] = ein("norm_mix", [D])
        I["w_in"] = ein("w_in", [D, cfg.NIN])
        I["b_gate"] = ein("b_gate", [2 * D])
        I["conv_w"] = ein("conv_w", [4, CD])
        I["conv_b"] = ein("conv_b", [CD])
        I["dt_bias"] = ein("dt_bias", [H])
        I["a_log"] = ein("a_log", [H])
        I["d_skip"] = ein("d_skip", [H])
        I["ssd_norm"] = ein("ssd_norm", [DI])
        I["w_ssd_out"] = ein("w_ssd_out", [DI, D])
        I["w_sb_out"] = ein("w_sb_out", [D, D])
        I["w_out"] = ein("w_out", [D, D])
        I["norm_xa"] = ein("norm_xa", [D])
        I["norm_mem"] = ein("norm_mem", [D])
        I["w_xa_q"] = ein("w_xa_q", [D, D])
        I["w_xa_kv"] = ein("w_xa_kv", [D, 2 * D])
        I["w_xa_o"] = ein("w_xa_o", [D, D])
        I["norm_ffn"] = ein("norm_ffn", [D])
        I["w_ffn_in"] = ein("w_ffn_in", [D, 2 * cfg.DFF])
        I["w_ffn_out"] = ein("w_ffn_out", [cfg.DFF, D])
        I["norm_final"] = ein("norm_final", [D])
        self.I = I
        self.out = nc.dram_tensor("out", [TT, D], F32, kind="ExternalOutput")
        self.dbg = None

        self.ps = [nc.alloc_psum_tensor(f"ps{i}", [128, 512], F32) for i in range(8)]
        self.psb = [p.bitcast(BF16) for p in self.ps]

        self.arena_top = (nc.sbuf_base + 63) // 64 * 64
        self.sb_limit = nc.sbuf_top - 64
        C = {}
        ncol = self.packed.shape[1]
        cst = self.sb("cst", [128, ncol], F32)
        self.dma("sp", cst[:, :], I["consts"][:, :], (), ("cst",), "ld0")

        def cs(name):
            o, n = self.coffs[name]
            return cst[:, o:o + n]
        self.cs = cs
        identb = self.sb("identb", [128, 128], BF16)
        self.cp("dve", identb[:, :], cs("ident"), ("cst",), ("identb",))
        onesb = self.sb("onesb", [128, 128], BF16)
        self.cp("dve", onesb[:, :], cs("ones"), ("cst",), ("onesb",))
        self.identb, self.onesb = identb, onesb
        nstrictb = self.sb("nstrictb", [128, 128], BF16)
        self.cp("dve", nstrictb[:, :], cs("nstrict"), ("cst",), ("nstrictb",))
        self.nstrictb = nstrictb
        nonesb = self.sb("nonesb", [128, 128], BF16)
        self.ts("dve", nonesb[:, :], cs("ones"), -1.0, None, ALU.mult, None, ("cst",), ("nonesb",))
        self.nonesb = nonesb
        flag = self.sb("flag", [128, 1], F32)
        self.dma("sp", flag[:, :], I["flag"][:, :], (), ("flag",), "ld1")
        self.flag = flag

        vstage = self.sb("vstage", [128, 128], F32)
        self.vcount = 0

        def colvec_into(dst_ap, src1d, n, key):
            nt = n // 128
            b = self.vcount % 2
            self.vcount += 1
            self.dma("sp", vstage[:nt, :], src1d.rearrange("(t p) -> t p", p=128), (), ("vstage",), "ld0")
            self.tr(self.ps[b][:, :nt], vstage[:nt, :], self.cs("ident")[:nt, :nt], ("vstage", "cst"), (("ps", b),))
            self.cp("dve", dst_ap, self.ps[b][:, :nt], (("ps", b),), (key,))

        def colvec(name, src, n, chan):
            t_ = self.sb(name, [128, n // 128], F32)
            colvec_into(t_[:, :], src[:], n, name)
            return (t_, name)
        self.colvec = colvec
        self.v_norm_mix = colvec("v_norm_mix", I["norm_mix"], D, "ld0")
        self.v_norm_xa = colvec("v_norm_xa", I["norm_xa"], D, "ld1")
        self.v_norm_mem = colvec("v_norm_mem", I["norm_mem"], D, "ld0")
        self.v_norm_ffn = colvec("v_norm_ffn", I["norm_ffn"], D, "ld1")
        self.v_norm_final = colvec("v_norm_final", I["norm_final"], D, "ld0")
        self.v_ssd_norm = colvec("v_ssd_norm", I["ssd_norm"], DI, "ld1")
        self.v_b_gate = colvec("v_b_gate", I["b_gate"], 2 * D, "ld0")
        self.v_conv_b = colvec("v_conv_b", I["conv_b"], CD, "ld1")
        v_conv_w = self.sb("v_conv_w", [128, 4, CD // 128], F32)
        for j in range(4):
            colvec_into(v_conv_w[:, j, :], I["conv_w"][j, :], CD, "v_conv_w")
        self.v_conv_w = v_conv_w
        hv = self.sb("hv", [H, 4], F32)
        for j, nm in enumerate(["dt_bias", "a_log", "d_skip"]):
            self.dma("sp", hv[:, j:j + 1], I[nm].rearrange("(h o) -> h o", o=1), (), ("hv",), "ld1", slow=True)
        self.hv = hv
        negA = self.sb("negA", [H, 1], F32)
        self.act(negA[:, :], hv[:, 1:2], AF.Exp, ("hv",), ("negA",))
        self.ts("dve", negA[:, :], negA[:, :], -1.0, None, ALU.mult, None, ("negA",), ("negA",))
        self.negA = negA
        self.halo = self.sb("halo", [128, CD // 128, 3], F32)
        self.memset("pool", self.halo[:, :, :], 0.0, ("halo",))
        self.arena_base = self.arena_top

        self.s_xT = self.dr("s_xT", [KT, 128, TT], F32)
        self.s_h = self.dr("s_h", [KT, 128, TT], BF16)
        self.s_sz = self.dr("s_sz", [DI // 128, 128, TT], BF16)
        self.s_xbc = self.dr("s_xbc", [CD // 128, 128, TT], BF16)
        self.s_q = self.dr("s_q", [NH, 128, TT], BF16)
        self.s_k = self.dr("s_k", [NH, 128, 2 * TT], BF16)
        self.s_vT = self.dr("s_vT", [NH, 128, 2 * TT], BF16)
        self.s_gate = self.dr("s_gate", [2 * KT, 128, TT], BF16)
        self.s_yn = self.dr("s_yn", [DI // 128, 128, TT], BF16)
        self.s_o = self.dr("s_o", [NH, 128, TT], BF16)
        self.s_bs = self.dr("s_bs", [KT, 128, TT], BF16)
        self.s_mg = self.dr("s_mg", [KT, 128, TT], BF16)
        self.s_dt = self.dr("s_dt", [2, H, TT], F32)
        self.s_state = self.dr("s_state", [128, H * 64], F32)
        self.s_act = self.dr("s_act", [cfg.DFF // 128, 128, TT], BF16)
        self.s_xq = self.dr("s_xq", [KT, 128, TT], BF16)
        self.s_xo = self.dr("s_xo", [KT, 128, TT], BF16)
        self.s_hm = self.dr("s_hm", [KT, 128, cfg.NMEM], BF16)
        self.s_mT = self.dr("s_mT", [KT, 128, cfg.NMEM], F32)
        self.s_mk = self.dr("s_mk", [KT, 128, cfg.NMEM], BF16)
        self.s_mv = self.dr("s_mv", [KT, 128, cfg.NMEM], BF16)

        self.main()
        self.P.emit()
        return nc

    def stop(self, name):
        if self.cfg.stop_after == name:
            self.done = True
        return self.done

    def main(self):
        cfg = self.cfg
        I = self.I
        if cfg.stop_after == "ssdonly":
            self.phase_transpose_in(I["x_own"], "own")
            self.phase_reset()
            zf = self.sb("zf", [128, cfg.TT], F32)
            zb = self.sb("zb", [128, cfg.TT], BF16)
            self.memset("pool", zf[:, :], 0.01, ("zf",))
            self.memset("pool", zb[:, :], 0.01, ("zb",))
            self.dma("sp", self.s_dt[0, :, :], zf[:cfg.H, :], ("zf",), ("s_dt",), "st0")
            self.ts("dve", zf[:, :], zf[:, :], -1.0, None, ALU.mult, None, ("zf",), ("zf",))
            self.dma("sp", self.s_dt[1, :, :], zf[:cfg.H, :], ("zf",), ("s_dt",), "st0")
            for k in range(cfg.CD // 128):
                self.dma("sp", self.s_xbc[k, :, :], zb[:, :], ("zb",), ("s_xbc",), "st1")
            import os
            for k in range(cfg.DI // 128):
                self.dma("sp", self.s_sz[k, :, :], zb[:, :], ("zb",), ("s_sz",), "st1")
            self.phase_ssd(prev=True)
            if os.environ.get("SSD_OWN"):
                self.phase_ssd(prev=False)
            self.phase_final(norm=False)
            return
        if cfg.stop_after == "tin":
            self.phase_transpose_in(I["x_own"], "own")
            self.phase_norm(self.v_norm_mix, "mix")
            self.phase_final(norm=True)
            return
        self.phase_transpose_in(I["x_prev"], "prev")
        self.phase_norm(self.v_norm_mix, "mix")
        self.phase_inproj(prev=True)
        if self.stop("pinproj"):
            return self.finish_debug()
        self.phase_ssd(prev=True)
        if self.stop("pssd"):
            return self.finish_debug()
        self.phase_transpose_in(I["x_own"], "own")
        self.phase_norm(self.v_norm_mix, "mix")
        self.phase_inproj(prev=False)
        if self.stop("inproj"):
            return self.finish_debug()
        self.phase_ssd(prev=False)
        if self.stop("ssd"):
            return self.finish_debug()
        self.phase_attn()
        if self.stop("attn"):
            return self.finish_debug()
        self.phase_merge()
        if self.stop("merge"):
            return self.finish_debug()
        self.phase_xattn()
        if self.stop("xattn"):
            return self.finish_debug()
        self.phase_ffn()
        self.phase_final()

    def finish_debug(self):
        self.phase_final(norm=False)

    def phase_transpose_in(self, x, tag, TT=None, dst=None):
        cfg = self.cfg
        D = cfg.D
        TT = TT or cfg.TT
        dst = dst if dst is not None else self.s_xT
        KT = D // 128
        self.phase_reset()
        xt = [self.sb(f"xt{i}", [128, D], F32) for i in range(2)]
        ot = [self.sb(f"xo{i}", [128, KT, 128], F32) for i in range(2)]
        ident = self.cs("ident")
        for tt in range(TT // 128):
            s = tt % 2
            self.dma("sp", xt[s][:, :], x[tt * 128:(tt + 1) * 128, :], (), (f"xt{s}",), f"ld{s}")
            for g in range(KT // 4):
                bank = (tt * (KT // 4) + g) % 8
                for j in range(4):
                    ft = g * 4 + j
                    self.tr(self.ps[bank][:, j * 128:(j + 1) * 128], xt[s][:, ft * 128:(ft + 1) * 128],
                            ident, (f"xt{s}", "cst"), (("ps", bank),))
                eng = "act" if g % 2 == 0 else "dve"
                self.cp(eng, ot[s][:, g * 4:(g + 1) * 4, :],
                        self.ps[bank][:, :].rearrange("p (j t) -> p j t", j=4),
                        (("ps", bank),), (f"xo{s}",))
            self.dma_tiles("sp", ot[s], dst, 0, KT, slice(tt * 128, (tt + 1) * 128), (f"xo{s}",), (dst.name,),
                           (f"st{s}",), store=True)

    def phase_norm(self, wv, tag, src=None, dst=None, ntok=None, D=None):
        cfg = self.cfg
        wvec, wkey = wv
        D = D or cfg.D
        TT = ntok or cfg.TT
        KT = D // 128
        src = src if src is not None else self.s_xT
        dst = dst if dst is not None else self.s_h
        skey = src.name
        dkey = dst.name
        self.phase_reset()
        xin = [self.sb(f"nx{i}", [128, TT], F32) for i in range(3)]
        sq = [self.sb(f"nsq{i}", [128, TT], F32) for i in range(2)]
        acc = self.sb("nacc", [128, TT], F32)
        rstd = self.sb("nrstd", [128, TT], F32)
        ho = [self.sb(f"nho{i}", [128, TT], BF16) for i in range(2)]
        for kt in range(KT):
            s = kt % 3
            self.dma("sp", xin[s][:, :], src[kt, :, :], (skey,), (f"nx{s}",), f"ld{s}")
            if kt == 0:
                self.act(acc[:, :], xin[s][:, :], AF.Square, (f"nx{s}",), ("nacc",))
            else:
                s2 = kt % 2
                self.act(sq[s2][:, :], xin[s][:, :], AF.Square, (f"nx{s}",), (f"nsq{s2}",))
                self.tt("pool", acc[:, :], acc[:, :], sq[s2][:, :], ALU.add, ("nacc", f"nsq{s2}"), ("nacc",))
        ones = self.cs("ones")
        nb = (TT + 511) // 512
        for b in range(nb):
            w = min(512, TT - b * 512)
            self.mm(self.ps[b][:, :w], ones, acc[:, b * 512:b * 512 + w], True, True,
                    ("cst", "nacc"), (("ps", b),))
            self.act(rstd[:, b * 512:b * 512 + w], self.ps[b][:, :w], AF.Ln, (("ps", b),), ("nrstd",),
                     bias=EPS, scale=1.0 / D)
        self.act(rstd[:, :], rstd[:, :], AF.Exp, ("nrstd",), ("nrstd",), scale=-0.5)
        for kt in range(KT):
            s = kt % 3
            s2 = kt % 2
            self.dma("sp", xin[s][:, :], src[kt, :, :], (skey,), (f"nx{s}",), f"ld{s}")
            self.stt(ho[s2][:, :], xin[s][:, :], wvec[:, kt:kt + 1], rstd[:, :], ALU.mult, ALU.mult,
                     (f"nx{s}", "nrstd", wkey), (f"nho{s2}",))
            self.dma("sp", dst[kt, :, :], ho[s2][:, :], (f"nho{s2}",), (dkey,), f"st{s2}")


    def phase_final(self, norm=True):
        cfg = self.cfg
        D, TT = cfg.D, cfg.TT
        KT = D // 128
        NTT = TT // 128
        src = self.s_xT
        self.phase_reset()
        xin = [self.sb(f"fx{i}", [128, TT], F32) for i in range(2)]
        sq = [self.sb(f"fsq{i}", [128, TT], F32) for i in range(2)]
        acc = self.sb("facc", [128, TT], F32)
        rstd = self.sb("frstd", [128, TT], F32)
        yk = [self.sb(f"fy{i}", [128, TT], F32) for i in range(2)]
        ot = [self.sb(f"fo{i}", [128, NTT, 128], F32) for i in range(2)]
        wvec, wkey = self.v_norm_final
        if norm:
            for kt in range(KT):
                s = kt % 2
                self.dma("sp", xin[s][:, :], src[kt, :, :], ("s_xT",), (f"fx{s}",), f"ld{s}")
                if kt == 0:
                    self.act(acc[:, :], xin[s][:, :], AF.Square, (f"fx{s}",), ("facc",))
                else:
                    self.act(sq[s][:, :], xin[s][:, :], AF.Square, (f"fx{s}",), (f"fsq{s}",))
                    self.tt("pool", acc[:, :], acc[:, :], sq[s][:, :], ALU.add, ("facc", f"fsq{s}"), ("facc",))
            ones = self.cs("ones")
            for b in range((TT + 511) // 512):
                w = min(512, TT - b * 512)
                self.mm(self.ps[b][:, :w], ones, acc[:, b * 512:b * 512 + w], True, True,
                        ("cst", "facc"), (("ps", b),))
                self.act(rstd[:, b * 512:b * 512 + w], self.ps[b][:, :w], AF.Ln, (("ps", b),), ("frstd",),
                         bias=EPS, scale=1.0 / D)
            self.act(rstd[:, :], rstd[:, :], AF.Exp, ("frstd",), ("frstd",), scale=-0.5)
        ident = self.cs("ident")
        for kt in range(KT):
            s = kt % 2
            self.dma("sp", xin[s][:, :], src[kt, :, :], ("s_xT",), (f"fx{s}",), f"ld{s}")
            if norm:
                self.stt(yk[s][:, :], xin[s][:, :], wvec[:, kt:kt + 1], rstd[:, :], ALU.mult, ALU.mult,
                         (f"fx{s}", "frstd", wkey), (f"fy{s}",))
                y, ykey = yk[s], f"fy{s}"
            else:
                y, ykey = xin[s], f"fx{s}"
            for g in range(NTT // 4):
                bank = (kt * (NTT // 4) + g) % 8
                for j in range(4):
                    tt = g * 4 + j
                    self.tr(self.ps[bank][:, j * 128:(j + 1) * 128], y[:, tt * 128:(tt + 1) * 128],
                            ident, (ykey, "cst"), (("ps", bank),))
                eng = "act" if g % 2 == 0 else "dve"
                self.cp(eng, ot[s][:, g * 4:(g + 1) * 4, :],
                        self.ps[bank][:, :].rearrange("p (j t) -> p j t", j=4),
                        (("ps", bank),), (f"fo{s}",))
            self.dma("sp", self.out[:, kt * 128:(kt + 1) * 128].rearrange("(t p) f -> p t f", p=128),
                     ot[s][:, :, :], (f"fo{s}",), ("out",), f"st{s}")
        self.P.add("sp", None, ("out",), ())


_CACHE = {}


def _get_nc(cfg_key, cfg):
    if cfg_key not in _CACHE:
        b = Builder(cfg)
        nc = b.build()
        _CACHE[cfg_key] = (nc, b)
    return _CACHE[cfg_key]


def run_cfg(cfg, inputs, n_batch, cfg_key):
    nc, b = _get_nc(cfg_key, cfg)
    TT, D = cfg.TT, cfg.D
    f32 = np.float32
    x = np.ascontiguousarray(inputs["x"], dtype=f32)
    mem = np.ascontiguousarray(inputs["mem"], dtype=f32)
    shared = {}
    for k in ["norm_mix", "w_in", "b_gate", "conv_w", "conv_b", "dt_bias", "a_log", "d_skip", "ssd_norm",
              "w_ssd_out", "w_sb_out", "w_out", "norm_xa", "norm_mem", "w_xa_q", "w_xa_kv", "w_xa_o",
              "norm_ffn", "w_ffn_in", "w_ffn_out"]:
        shared[k] = np.ascontiguousarray(np.asarray(inputs[k], dtype=f32)[0])
    shared["norm_final"] = np.ascontiguousarray(inputs["norm_final"], dtype=f32)
    for k in b.tiny:
        shared[k] = np.zeros((128, 128), f32)
    shared["consts"] = b.packed
    shared["cmask"] = b.masks
    zeros = np.zeros((TT, D), f32)
    in_maps = []
    ncores = 2 * n_batch
    for c in range(ncores):
        bi, half = c // 2, c % 2
        m = dict(shared)
        m["x_own"] = np.ascontiguousarray(x[bi, half * TT:(half + 1) * TT])
        m["x_prev"] = np.ascontiguousarray(x[bi, 0:TT]) if half == 1 else zeros
        m["mem"] = np.ascontiguousarray(mem[bi])
        m["flag"] = np.full((128, 1), float(half), f32)
        in_maps.append(m)
    res = run_bass_kernel_spmd(nc, in_maps, core_ids=list(range(ncores)))
    out = np.zeros((n_batch, 2 * TT, D), f32)
    for c in range(ncores):
        bi, half = c // 2, c % 2
        out[bi, half * TT:(half + 1) * TT] = np.asarray(res.results[c]["out"], dtype=f32)
    return out


def kernel(**inputs):
    cfg = Cfg()
    return run_cfg(cfg, inputs, 4, "full")


KCH = 16


def gemm_setup(self, K, TTg):
    KT = K // 128
    self.g_panel = self.sb("panel", [128, KT, TTg], BF16)
    self.g_st = [self.sb(f"gst{i}", [128, KCH, 128], F32) for i in range(3)]
    self.g_wb = [self.sb(f"gwb{i}", [128, KCH, 128], BF16) for i in range(3)]
    self.g_u = 0
    self.g_nt = 0
    self.g_TT = TTg
    self.g_KT = KT


def gemm_load_panel(self, src, skey, t0=0):
    KT, TTg = self.g_KT, self.g_TT
    step = 8
    for k0 in range(0, KT, step):
        kn = min(step, KT - k0)
        self.dma("sp", self.g_panel[:, k0:k0 + kn, :],
                 src[k0:k0 + kn, :, t0:t0 + TTg].rearrange("k p t -> p k t"),
                 (skey,), ("panel",), f"ld{(k0 // step) % 3}")


def gemm(self, W, tiles, epi):
    KT, TTg = self.g_KT, self.g_TT
    nb = (TTg + 511) // 512
    nsets = 8 // nb
    units = []
    for i, (c0, wd) in enumerate(tiles):
        nk = (KT + KCH - 1) // KCH
        for kc in range(nk):
            units.append((i, c0, wd, kc, kc == nk - 1))

    def load(u):
        i, c0, wd, kc, last = units[u]
        s = (self.g_u + u) % 3
        k0 = kc * KCH
        kn = min(KCH, KT - k0)
        self.dma("sp", self.g_st[s][:, :kn, :wd],
                 W[k0 * 128:(k0 + kn) * 128, c0:c0 + wd].rearrange("(kt p) n -> p kt n", p=128),
                 (), (f"gst{s}",), f"w{s}")
        self.cp("act", self.g_wb[s][:, :kn, :wd], self.g_st[s][:, :kn, :wd], (f"gst{s}",), (f"gwb{s}",))

    LA = 2
    for u in range(min(LA, len(units))):
        load(u)
    for u in range(len(units)):
        if u + LA < len(units):
            load(u + LA)
        i, c0, wd, kc, last = units[u]
        s = (self.g_u + u) % 3
        k0 = kc * KCH
        kn = min(KCH, KT - k0)
        setn = (self.g_nt + i) % nsets
        banks = [setn * nb + b for b in range(nb)]
        for kt in range(kn):
            for b in range(nb):
                w = min(512, TTg - b * 512)
                self.mm(self.ps[banks[b]][:wd, :w], self.g_wb[s][:, kt, :wd],
                        self.g_panel[:, k0 + kt, b * 512:b * 512 + w],
                        (k0 + kt == 0), (k0 + kt == KT - 1),
                        (f"gwb{s}", "panel"), (("ps", banks[b]),))
        if last:
            epi(i, banks, wd)
    self.g_u += len(units)
    self.g_nt += len(tiles)


def phase_inproj(self, prev):
    cfg = self.cfg
    D, TT, G, H, DI, CD, NH = cfg.D, cfg.TT, cfg.G, cfg.H, cfg.DI, cfg.CD, cfg.NH
    off = cfg.off
    I = self.I
    self.phase_reset()
    gemm_setup(self, D, TT)
    ob = [self.sb(f"ob{i}", [128, TT], BF16) for i in range(2)]
    u_t = self.sb("cu", [128, TT + 3], F32)
    acc = self.sb("cacc", [128, TT], F32)
    gemm_load_panel(self, self.s_h, "s_h")
    koff = 0 if prev else TT
    tiles = []
    kinds = []

    def addseg(kind, seg, n):
        for j in range(0, n, 128):
            tiles.append((int(off[seg]) + j, min(128, n - j)))
            kinds.append((kind, j // 128))
    if not prev:
        addseg("z", 0, DI)
    addseg("xbc", 1, CD)
    addseg("dt", 2, H)
    if not prev:
        addseg("q", 3, D)
    addseg("k", 4, D)
    addseg("v", 5, D)
    if not prev:
        addseg("gate", 6, 2 * D)
    nb = (TT + 511) // 512
    cnt = [0]

    def evac(dst, banks, func, okey, wd=128, bias=None, scale=None):
        for b in range(nb):
            w = min(512, TT - b * 512)
            self.act(dst[:wd, b * 512:b * 512 + w], self.ps[banks[b]][:wd, :w], func,
                     (("ps", banks[b]),), (okey,), bias=bias, scale=scale)

    def epi(i, banks, wd):
        kind, idx = kinds[i]
        s = cnt[0] % 2
        cnt[0] += 1
        okey = f"ob{s}"
        o = ob[s]
        if kind == "z":
            evac(o, banks, AF.Silu, okey)
            self.dma("act", self.s_sz[idx, :, :], o[:, :], (okey,), ("s_sz",), f"st{s}")
        elif kind == "q":
            evac(o, banks, AF.Copy, okey, scale=float(128 ** -0.5))
            self.dma("act", self.s_q[idx, :, :], o[:, :], (okey,), ("s_q",), f"st{s}")
        elif kind == "k":
            evac(o, banks, AF.Copy, okey)
            self.dma("act", self.s_k[idx, :, koff:koff + TT], o[:, :], (okey,), ("s_k",), f"st{s}")
        elif kind == "v":
            if prev:
                evac(o, banks, AF.Copy, okey, scale=self.flag[:, 0:1])
            else:
                evac(o, banks, AF.Copy, okey)
            self.dma("act", self.s_vT[idx, :, koff:koff + TT], o[:, :], (okey,), ("s_vT",), f"st{s}")
        elif kind == "gate":
            evac(o, banks, AF.Sigmoid, okey, bias=self.v_b_gate[0][:, idx:idx + 1])
            self.dma("act", self.s_gate[idx, :, :], o[:, :], (okey,), ("s_gate",), f"st{s}")
        elif kind == "dt":
            evac(acc, banks, AF.Exp, "cacc", wd=H, bias=self.hv[:, 0:1])
            self.act(acc[:H, :], acc[:H, :], AF.Ln, ("cacc",), ("cacc",), bias=1.0)
            self.ts("dve", u_t[:H, :TT], acc[:H, :], self.negA[:, 0:1], None, ALU.mult, None,
                    ("cacc", "negA"), ("cu",))
            self.dma("act", self.s_dt[0, :, :], acc[:H, :], ("cacc",), ("s_dt",), "st0")
            self.dma("act", self.s_dt[1, :, :], u_t[:H, :TT], ("cu",), ("s_dt",), "st1")
        elif kind == "xbc":
            cw = self.v_conv_w
            self.cp("dve", u_t[:, 0:3], self.halo[:, idx, :], ("halo",), ("cu",))
            evac(u_t[:, 3:], banks, AF.Copy, "cu")
            self.cp("dve", self.halo[:, idx, :], u_t[:, TT:TT + 3], ("cu",), ("halo",))
            self.ts("dve", acc[:, :], u_t[:, 3:3 + TT], cw[:, 3, idx:idx + 1], self.v_conv_b[0][:, idx:idx + 1],
                    ALU.mult, ALU.add, ("cu", "v_conv_w", "v_conv_b"), ("cacc",))
            for j in (2, 1, 0):
                self.stt(acc[:, :], u_t[:, j:j + TT], cw[:, j, idx:idx + 1], acc[:, :], ALU.mult, ALU.add,
                         ("cu", "cacc", "v_conv_w"), ("cacc",))
            self.act(o[:, :], acc[:, :], AF.Silu, ("cacc",), (okey,))
            self.dma("act", self.s_xbc[idx, :, :], o[:, :], (okey,), ("s_xbc",), f"st{s}")

    gemm(self, I["w_in"], tiles, epi)


Builder.phase_inproj = phase_inproj


def phase_ssd(self, prev):
    cfg = self.cfg
    D, TT, G, H, DI, CD = cfg.D, cfg.TT, cfg.G, cfg.H, cfg.DI, cfg.CD
    NDI = DI // 128
    NCD = CD // 128
    BLK = min(TT, 256)
    NCH = BLK // 64
    self.phase_reset()
    sb = self.sb
    ident = self.cs("ident")
    ones = self.cs("ones")
    triinc = self.cs("triinc")
    negm = self.cs("negm")
    xbc = sb("xbc", [128, NCD, BLK], BF16)
    dtb = sb("dtb", [H, 2, BLK], F32)
    Hst = sb("Hst", [128, H * 64], F32)
    prevT = sb("prevT", [128, H * 64], BF16)
    X = sb("X", [64, DI], BF16)
    Btok = sb("Btok", [64, G * 128], BF16)
    dts = sb("dts", [64, 2 * H], F32)
    acum = sb("acum", [64, H], F32)
    cdec = sb("cdec", [128, H], F32)
    w2 = sb("w2", [64, H], F32)
    Xw = [sb(f"Xw{i}", [64, 8, 64], BF16) for i in range(2)]
    if not prev:
        szb = sb("szb", [128, NDI, BLK], BF16)
        gbuf = sb("gbuf", [128, NDI, BLK], F32)
        ynb = sb("ynb", [128, NDI, BLK], BF16)
        sq = sb("sq", [128, 4, BLK], BF16)
        rstd = sb("rstd", [128, BLK], F32)
        Dg = [sb(f"Dg{i}", [64, 8, 64], F32) for i in range(2)]
        Eg = [sb(f"Eg{i}", [64, 8, 64], F32) for i in range(2)]
        EA = [sb(f"EA{i}", [128, 8, 64], BF16) for i in range(2)]
        LT = [sb(f"LT{i}", [64, 8, 64], BF16) for i in range(2)]
        MT = [sb(f"MT{i}", [64, 8, 64], BF16) for i in range(2)]
        Cs = [sb(f"Cs{i}", [128, 8, 64], BF16) for i in range(2)]
        Xd = [sb(f"Xd{i}", [64, 8, 64], BF16) for i in range(2)]
        cb = [sb(f"cb{i}", [64, 64], F32) for i in range(2)]
        ysb = [sb(f"ysb{i}", [64, 512], F32) for i in range(2)]
        DSI = sb("DSI", [64, H, 64], BF16)
        dsr = sb("dsr", [64, H], F32)
        d2 = sb("d2", [H, H], F32)
        self.ts("dve", d2[:, :], ident[:H, :H], self.hv[:, 2:3], None, ALU.mult, None, ("cst", "hv"), ("d2",))
        self.mm(self.ps[0][:64, :H], ones[:H, :64], d2[:, :], True, True, ("cst", "d2"), (("ps", 0),))
        self.cp("dve", dsr[:, :], self.ps[0][:64, :H], (("ps", 0),), ("dsr",))
        self.tt("dve", DSI[:, :, :], ident[:64, :64].unsqueeze(1).to_broadcast([64, H, 64]),
                dsr[:, :].unsqueeze(2).to_broadcast([64, H, 64]), ALU.mult, ("cst", "dsr"), ("DSI",))
    if prev:
        self.memset("pool", Hst[:, :], 0.0, ("Hst",))
    else:
        self.dma("sp", Hst[:, :], self.s_state[:, :], ("s_state",), ("Hst",), "ld0")
        self.ts("dve", Hst[:, :], Hst[:, :], self.flag[:, 0:1], None, ALU.mult, None, ("Hst", "flag"), ("Hst",))
        self.cp("act", prevT[:, :], Hst[:, :], ("Hst",), ("prevT",))

    def bc_h(ap2, n=8):
        return ap2.unsqueeze(2).to_broadcast([ap2.shape[0], n, 64])

    def bc_m(ap2, n=8):
        return ap2.unsqueeze(1).to_broadcast([ap2.shape[0], n, 64])

    it = 0
    import os
    NBLK_ = int(os.environ.get("SSD_NBLK", TT // BLK))
    SEC_ = int(os.environ.get("SSD_SEC", 9))
    for blk in range(min(NBLK_, TT // BLK)):
        t0 = blk * BLK
        tiles_needed = list(range(NCD)) if not prev else list(range(NDI + G))
        nld = len(tiles_needed)
        self.dma_tiles("sp", xbc, self.s_xbc, 0, nld, slice(t0, t0 + BLK), ("s_xbc",), ("xbc",), ("ld0", "ld2"))
        self.dma("sp", dtb[:, :, :], self.s_dt[:, :, t0:t0 + BLK].rearrange("a h t -> h a t"),
                 ("s_dt",), ("dtb",), "ld1")
        if not prev:
            self.dma_tiles("sp", szb, self.s_sz, 0, NDI, slice(t0, t0 + BLK), ("s_sz",), ("szb",), ("ld2", "ld1"))
        for ch in range(NCH if SEC_ > 0 else 0):
            lo = ch * 64
            self.tr(self.ps[0][:64, 0:H], dtb[:, 0, lo:lo + 64], ident[:H, :H], ("dtb", "cst"), (("ps", 0),))
            self.tr(self.ps[0][:64, H:2 * H], dtb[:, 1, lo:lo + 64], ident[:H, :H], ("dtb", "cst"), (("ps", 0),))
            self.cp("dve", dts[:, :], self.ps[0][:64, :2 * H], (("ps", 0),), ("dts",))
            if SEC_ < 2:
                continue
            self.mm(self.ps[1][:64, 0:H], triinc[:64, :64], dts[:, H:2 * H], True, True, ("cst", "dts"), (("ps", 1),))
            self.cp("act", acum[:, :], self.ps[1][:64, 0:H], (("ps", 1),), ("acum",))
            if SEC_ < 3:
                continue
            self.mm(self.ps[3][:, 0:2 * H], ones[:64, :], dts[:, 0:2 * H], True, True, ("cst", "dts"), (("ps", 3),))
            if SEC_ < 4:
                continue
            self.act(cdec[:, :], self.ps[3][:, H:2 * H], AF.Exp, (("ps", 3),), ("cdec",))
            if SEC_ < 5:
                continue
            self.cp("act", w2[:, :], self.ps[3][:64, H:2 * H], (("ps", 3),), ("w2",))
            self.tt("dve", w2[:, :], w2[:, :], acum[:, :], ALU.subtract, ("w2", "acum"), ("w2",))
            if SEC_ < 6:
                continue
            self.act(w2[:, :], w2[:, :], AF.Exp, ("w2",), ("w2",))
            if SEC_ < 7:
                continue
            self.tt("dve", w2[:, :], w2[:, :], dts[:, 0:H], ALU.mult, ("w2", "dts"), ("w2",))
            if SEC_ < 8:
                continue
            for c8 in range(0, NDI, 8):
                n8 = min(8, NDI - c8)
                for j in range(n8):
                    self.tr(self.psb[2][:64, j * 128:(j + 1) * 128], xbc[:, c8 + j, lo:lo + 64], self.identb[:, :],
                            ("xbc", "identb"), (("ps", 2),))
                self.cp("act" if (c8 // 8) % 2 == 0 else "dve", X[:, c8 * 128:(c8 + n8) * 128],
                        self.psb[2][:64, :n8 * 128], (("ps", 2),), ("X",))
            for g in range(G):
                self.tr(self.psb[2][:64, g * 128:(g + 1) * 128], xbc[:, NDI + g, lo:lo + 64], self.identb[:, :],
                        ("xbc", "identb"), (("ps", 2),))
            self.cp("dve", Btok[:, :], self.psb[2][:64, :G * 128], (("ps", 2),), ("Btok",))
            for g in range(G if SEC_ > 8 else 0):
                s = it % 2
                it += 1
                hs = slice(8 * g, 8 * g + 8)
                Xg = X[:, g * 512:(g + 1) * 512].rearrange("p (h e) -> p h e", h=8)
                if not prev:
                    Bt = xbc[:, NDI + g, lo:lo + 64]
                    Ct = xbc[:, NDI + G + g, lo:lo + 64]
                    self.tt("dve", Dg[s][:, :, :], bc_h(acum[:, hs]), bc_m(ident[:64, :64]), ALU.mult,
                            ("acum", "cst"), (f"Dg{s}",))
                    self.tt("dve", Eg[s][:, :, :], bc_m(negm[:64, :64]), bc_h(acum[:, hs]), ALU.subtract,
                            ("acum", "cst"), (f"Eg{s}",))
                    Dg2 = Dg[s][:, :, :].rearrange("p h l -> p (h l)")
                    Eg2 = Eg[s][:, :, :].rearrange("p h l -> p (h l)")
                    self.mm(self.ps[3][:, :], ones[:64, :], Dg2, True, True, ("cst", f"Dg{s}"), (("ps", 3),))
                    self.mm(self.ps[4][:64, :], ones[:64, :64], Dg2, True, False, ("cst", f"Dg{s}"), (("ps", 4),))
                    self.mm(self.ps[4][:64, :], ident[:64, :64], Eg2, False, True, ("cst", f"Eg{s}"), (("ps", 4),))
                    self.act(EA[s][:, :, :].rearrange("p h l -> p (h l)"), self.ps[3][:, :], AF.Exp,
                             (("ps", 3),), (f"EA{s}",))
                    self.act(LT[s][:, :, :].rearrange("p h l -> p (h l)"), self.ps[4][:64, :], AF.Exp,
                             (("ps", 4),), (f"LT{s}",))
                    self.mm(self.ps[5][:64, 0:64], Bt, Ct, True, True, ("xbc",), (("ps", 5),))
                    self.cp("act", cb[s][:, :], self.ps[5][:64, 0:64], (("ps", 5),), (f"cb{s}",))
                    self.tt("dve", MT[s][:, :, :], LT[s][:, :, :], bc_m(cb[s][:, :]), ALU.mult,
                            (f"LT{s}", f"cb{s}"), (f"MT{s}",))
                    self.tt("pool", Cs[s][:, :, :], EA[s][:, :, :], bc_m(Ct), ALU.mult,
                            (f"EA{s}", "xbc"), (f"Cs{s}",))
                    self.tt("pool", Xd[s][:, :, :], Xg, bc_h(dts[:, hs]), ALU.mult, ("X", "dts"), (f"Xd{s}",))
                self.tt("dve", Xw[s][:, :, :], Xg, bc_h(w2[:, hs]), ALU.mult, ("X", "w2"), (f"Xw{s}",))
                if not prev:
                    for h in range(8):
                        hg = 8 * g + h
                        yo = self.ps[6][:64, h * 64:(h + 1) * 64]
                        self.mm(yo, MT[s][:, h, :], Xd[s][:, h, :], True, False, (f"MT{s}", f"Xd{s}"), (("ps", 6),))
                        self.mm(yo, DSI[:, hg, :], Xg[:, h, :], False, False, ("DSI", "X"), (("ps", 6),))
                        self.mm(yo, Cs[s][:, h, :], prevT[:, hg * 64:(hg + 1) * 64], False, True,
                                (f"Cs{s}", "prevT"), (("ps", 6),))
                    self.cp("act", ysb[s][:, :], self.ps[6][:64, :], (("ps", 6),), (f"ysb{s}",))
                    for j in range(4):
                        self.tr(self.ps[5][:, 128 + j * 64:128 + (j + 1) * 64], ysb[s][:, j * 128:(j + 1) * 128],
                                ident[:64, :64], (f"ysb{s}", "cst"), (("ps", 5),))
                    self.tt("dve", gbuf[:, 4 * g:4 * g + 4, lo:lo + 64],
                            self.ps[5][:, 128:384].rearrange("p (j l) -> p j l", j=4),
                            szb[:, 4 * g:4 * g + 4, lo:lo + 64], ALU.mult, (("ps", 5), "szb"), ("gbuf",))
                self.mm(self.ps[7][:, :], Btok[:, g * 128:(g + 1) * 128], Xw[s][:, :, :].rearrange("p h e -> p (h e)"),
                        True, True, ("Btok", f"Xw{s}"), (("ps", 7),))
                Hg = Hst[:, g * 512:(g + 1) * 512]
                Hg3 = Hg.rearrange("p (h e) -> p h e", h=8)
                self.tt("dve", Hg3, Hg3, bc_h(cdec[:, hs]), ALU.mult, ("Hst", "cdec"), ("Hst",))
                self.tt("dve", Hg, Hg, self.ps[7][:, :], ALU.add, ("Hst", ("ps", 7)), ("Hst",))
                if not prev:
                    self.cp("act", prevT[:, g * 512:(g + 1) * 512], Hg, ("Hst",), ("prevT",))
        if not prev:
            vn, vkey = self.v_ssd_norm
            for g in range(G):
                self.act(sq[:, :, :], gbuf[:, 4 * g:4 * g + 4, :], AF.Square, ("gbuf",), ("sq",))
                for j in range(4):
                    self.mm(self.ps[1][:, :BLK], self.onesb[:, :], sq[:, j, :], j == 0, j == 3,
                            ("onesb", "sq"), (("ps", 1),))
                self.act(rstd[:, :], self.ps[1][:, :BLK], AF.Ln, (("ps", 1),), ("rstd",), bias=EPS, scale=1.0 / 512)
                self.act(rstd[:, :], rstd[:, :], AF.Exp, ("rstd",), ("rstd",), scale=-0.5)
                for j in range(4):
                    ct = 4 * g + j
                    self.stt(ynb[:, ct, :], gbuf[:, ct, :], vn[:, ct:ct + 1], rstd[:, :], ALU.mult, ALU.mult,
                             ("gbuf", "rstd", vkey), ("ynb",))
            self.dma_tiles("act", ynb, self.s_yn, 0, NDI, slice(t0, t0 + BLK), ("ynb",), ("s_yn",), ("st0", "st1"),
                           store=True)
    if prev:
        self.dma("act", self.s_state[:, :], Hst[:, :], ("Hst",), ("s_state",), "st1")


Builder.phase_ssd = phase_ssd


def phase_attn(self):
    cfg = self.cfg
    D, TT, NH = cfg.D, cfg.TT, cfg.NH
    self.phase_reset()
    sb = self.sb
    NQG = TT // 512
    NPB = TT // 128
    m01f = sb("m01f", [128, 4, 512], F32)
    mneg = sb("mneg", [128, 4, 512], F32)
    m01 = sb("m01", [128, 4, 512], BF16)
    self.dma("sp", m01f[:, :, :], self.I["cmask"][:, 0:2048].rearrange("p (k t) -> p k t", k=4), (), ("m01f",), "ld0")
    self.dma("sp", mneg[:, :, :], self.I["cmask"][:, 2048:4096].rearrange("p (k t) -> p k t", k=4), (), ("mneg",), "ld1")
    self.cp("dve", m01[:, :, :], m01f[:, :, :], ("m01f",), ("m01",))
    Kh = [sb(f"Kh{i}", [128, 2 * TT], BF16) for i in range(2)]
    Qh = [sb(f"Qh{i}", [128, TT], BF16) for i in range(2)]
    Vs = [sb(f"Vs{i}", [128, 2 * TT], BF16) for i in range(2)]
    Vh = [sb(f"Vh{i}", [128, 2 * TT // 128, 128], BF16) for i in range(2)]
    oT = [sb(f"oT{i}", [128, TT], BF16) for i in range(2)]
    NZ = 4
    zbanks = [0, 1, 5, 6]
    et = [sb(f"et{i}", [128, 512], F32) for i in range(NZ)]
    sp_ = [sb(f"sp{i}", [128, 512], BF16) for i in range(NZ + 2)]
    T2 = [sb(f"T2{i}", [128, 512], F32) for i in range(NZ)]
    Wt = [sb(f"Wt{i}", [128, 512], BF16) for i in range(NZ)]
    At = [sb(f"At{i}", [128, 512], BF16) for i in range(3)]
    nkb_all = 2 * TT // 128

    def head_load(h):
        s = h % 2
        self.dma("sp", Kh[s][:, :], self.s_k[h, :, :], ("s_k",), (f"Kh{s}",), f"ld{s}")
        self.dma("sp", Qh[s][:, :], self.s_q[h, :, :], ("s_q",), (f"Qh{s}",), "ld2")
        self.dma("sp", Vs[s][:, :], self.s_vT[h, :, :], ("s_vT",), (f"Vs{s}",), f"ld{s}")
        for k8 in range(0, nkb_all, 8):
            for j in range(8):
                kb = k8 + j
                self.tr(self.psb[4][:, j * 128:(j + 1) * 128], Vs[s][:, kb * 128:(kb + 1) * 128], self.identb[:, :],
                        (f"Vs{s}", "identb"), (("ps", 4),))
            self.cp("dve", Vh[s][:, k8:k8 + 8, :], self.psb[4][:, :].rearrange("p (j d) -> p j d", j=8),
                    (("ps", 4),), (f"Vh{s}",))

    its = []
    og = 0
    for h in range(NH):
        for qg in range(NQG):
            nk = NPB + 4 * (qg + 1)
            ob = 2 + (og % 2)
            og += 1
            for idx, kb in enumerate(range(nk - 1, -1, -1)):
                its.append(dict(h=h, qg=qg, kb=kb, first=(idx == 0), last=(kb == 0),
                                kk=kb - (NPB + 4 * qg), ob=ob, newhead=(qg == 0 and idx == 0)))
    acur = [None]
    acnt = [0]
    NS = NZ + 2

    def stageA(i):
        c = its[i]
        if c["newhead"]:
            head_load(c["h"])
        s = c["h"] % 2
        z = i % NZ
        spi = i % NS
        zb = self.ps[zbanks[z]]
        kb, qg, kk = c["kb"], c["qg"], c["kk"]
        self.mm(zb[:, :], Kh[s][:, kb * 128:(kb + 1) * 128], Qh[s][:, qg * 512:(qg + 1) * 512], True, False,
                (f"Kh{s}", f"Qh{s}"), (("ps", zbanks[z]),))
        self.act(et[z][:, :], zb[:, :], AF.Exp, (("ps", zbanks[z]),), (f"et{z}",))
        self.act(sp_[spi][:, :], et[z][:, :], AF.Ln, (f"et{z}",), (f"sp{spi}",), bias=1.0)
        if kk >= 0:
            self.tt("pool", sp_[spi][:, :], sp_[spi][:, :], m01[:, kk, :], ALU.mult,
                    (f"sp{spi}", "m01"), (f"sp{spi}",))

    def stageB(i):
        c = its[i]
        z = i % NZ
        spi = i % NS
        zbk = zbanks[z]
        zb = self.ps[zbk]
        kk, first, last = c["kk"], c["first"], c["last"]
        self.mm(zb[:, :], self.nstrictb[:, :], sp_[spi][:, :], False, first,
                ("nstrictb", f"sp{spi}"), (("ps", zbk),))
        if not first:
            self.mm(zb[:, :], self.nonesb[:, :], acur[0][0][:, :], False, True,
                    ("nonesb", acur[0][1]), (("ps", zbk),))
        self.tt("dve", T2[z][:, :], zb[:, :], sp_[spi][:, :], ALU.subtract,
                (("ps", zbk), f"sp{spi}"), (f"T2{z}",))
        if kk >= 0:
            self.tt("pool", T2[z][:, :], T2[z][:, :], mneg[:, kk, :], ALU.add, (f"T2{z}", "mneg"), (f"T2{z}",))
        if not last:
            if first:
                acur[0] = (sp_[spi], f"sp{spi}")
            else:
                a = acnt[0] % 3
                acnt[0] += 1
                self.tt("dve", At[a][:, :], acur[0][0][:, :], sp_[spi][:, :], ALU.add,
                        (acur[0][1], f"sp{spi}"), (f"At{a}",))
                acur[0] = (At[a], f"At{a}")

    def stageC(i):
        z = i % NZ
        self.act(Wt[z][:, :], T2[z][:, :], AF.Exp, (f"T2{z}",), (f"Wt{z}",))

    def stageD(i):
        c = its[i]
        s = c["h"] % 2
        z = i % NZ
        kb, qg, first, last, ob = c["kb"], c["qg"], c["first"], c["last"], c["ob"]
        self.mm(self.ps[ob][:, :], Vh[s][:, kb, :], Wt[z][:, :], first, last,
                (f"Vh{s}", f"Wt{z}"), (("ps", ob),))
        if last:
            self.act(oT[s][:, qg * 512:(qg + 1) * 512], self.ps[ob][:, :], AF.Copy, (("ps", ob),), (f"oT{s}",))
            if qg == NQG - 1:
                h = c["h"]
                self.dma("act", self.s_o[h, :, :], oT[s][:, :], (f"oT{s}",), (("s_o", h),), f"st{s}")

    n = len(its)
    for j in range(n + 3):
        if j < n:
            stageA(j)
        if 0 <= j - 1 < n:
            stageB(j - 1)
        if 0 <= j - 2 < n:
            stageC(j - 2)
        if 0 <= j - 3 < n:
            stageD(j - 3)


Builder.phase_attn = phase_attn


def res_epi(self, xo, t0, TTg):
    nb = (TTg + 511) // 512
    cnt = [0]

    def epi(i, banks, wd):
        s = cnt[0] % 2
        cnt[0] += 1
        self.dma("sp", xo[s][:, :TTg], self.s_xT[i, :, t0:t0 + TTg], (("s_xT", i, t0),), (f"xo{s}",), f"ld{s}")
        for b in range(nb):
            w = min(512, TTg - b * 512)
            self.tt("dve", xo[s][:, b * 512:b * 512 + w], xo[s][:, b * 512:b * 512 + w], self.ps[banks[b]][:, :w],
                    ALU.add, (f"xo{s}", ("ps", banks[b])), (f"xo{s}",))
        self.dma("act", self.s_xT[i, :, t0:t0 + TTg], xo[s][:, :TTg], (f"xo{s}",), (("s_xT", i, t0),), f"st{s}")
    return epi


def phase_merge(self):
    cfg = self.cfg
    D, TT, DI = cfg.D, cfg.TT, cfg.DI
    KT = D // 128
    I = self.I
    assert DI == D
    nb = (TT + 511) // 512
    tiles = [(j * 128, 128) for j in range(KT)]
    cnt = [0]
    self.phase_reset()
    gemm_setup(self, D, TT)
    ob = [self.sb(f"ob{i}", [128, TT], BF16) for i in range(2)]
    gemm_load_panel(self, self.s_yn, "s_yn")

    def epi1(i, banks, wd):
        s = cnt[0] % 2
        cnt[0] += 1
        for b in range(nb):
            w = min(512, TT - b * 512)
            self.act(ob[s][:, b * 512:b * 512 + w], self.ps[banks[b]][:, :w], AF.Copy, (("ps", banks[b]),), (f"ob{s}",))
        self.dma("act", self.s_bs[i, :, :], ob[s][:, :], (f"ob{s}",), (("s_bs", i),), f"st{s}")
    gemm(self, I["w_ssd_out"], tiles, epi1)
    self.phase_reset()
    gemm_setup(self, D, TT)
    ob2 = [self.sb(f"ob{i}", [128, TT], BF16) for i in range(2)]
    g1 = self.sb("g1", [128, TT], BF16)
    g2 = self.sb("g2", [128, TT], BF16)
    bsb = self.sb("bsb", [128, TT], BF16)
    t1 = self.sb("t1", [128, TT], F32)
    gemm_load_panel(self, self.s_o, "s_o")

    def epi2(i, banks, wd):
        s = cnt[0] % 2
        cnt[0] += 1
        self.dma("sp", g1[:, :], self.s_gate[i, :, :], ("s_gate",), ("g1",), "ld0")
        self.dma("sp", g2[:, :], self.s_gate[KT + i, :, :], ("s_gate",), ("g2",), "ld1")
        self.dma("sp", bsb[:, :], self.s_bs[i, :, :], ("s_bs",), ("bsb",), "ld2")
        self.tt("pool", t1[:, :], g1[:, :], bsb[:, :], ALU.mult, ("g1", "bsb"), ("t1",))
        for b in range(nb):
            w = min(512, TT - b * 512)
            sl = slice(b * 512, b * 512 + w)
            self.tt("dve", g2[:, sl], g2[:, sl], self.ps[banks[b]][:, :w], ALU.mult, ("g2", ("ps", banks[b])), ("g2",))
        self.tt("dve", ob2[s][:, :], g2[:, :], t1[:, :], ALU.add, ("g2", "t1"), (f"ob{s}",))
        self.dma("act", self.s_mg[i, :, :], ob2[s][:, :], (f"ob{s}",), (("s_mg", i),), f"st{s}")
    gemm(self, I["w_sb_out"], tiles, epi2)
    self.phase_reset()
    gemm_setup(self, D, TT)
    xo = [self.sb(f"xo{i}", [128, TT], F32) for i in range(2)]
    gemm_load_panel(self, self.s_mg, "s_mg")
    gemm(self, I["w_out"], tiles, res_epi(self, xo, 0, TT))


Builder.phase_merge = phase_merge


def phase_xattn(self):
    cfg = self.cfg
    D, TT, NM = cfg.D, cfg.TT, cfg.NMEM
    KT = D // 128
    XH, XD = cfg.XH, cfg.XD
    NDT = XD // 128
    I = self.I
    self.phase_transpose_in(I["mem"], "mem", TT=NM, dst=self.s_mT)
    self.phase_norm(self.v_norm_mem, "mem", src=self.s_mT, dst=self.s_hm, ntok=NM)
    self.phase_reset()
    gemm_setup(self, D, NM)
    obm = [self.sb(f"obm{i}", [128, NM], BF16) for i in range(2)]
    gemm_load_panel(self, self.s_hm, "s_hm")
    cnt = [0]

    def epikv(i, banks, wd):
        s = cnt[0] % 2
        cnt[0] += 1
        self.act(obm[s][:, :], self.ps[banks[0]][:, :NM], AF.Copy, (("ps", banks[0]),), (f"obm{s}",))
        dst = self.s_mk if i < KT else self.s_mv
        self.dma("act", dst[i % KT, :, :], obm[s][:, :], (f"obm{s}",), ((dst.name, i),), f"st{s}")
    gemm(self, I["w_xa_kv"], [(j * 128, 128) for j in range(2 * KT)], epikv)
    self.phase_norm(self.v_norm_xa, "xa")
    self.phase_reset()
    gemm_setup(self, D, TT)
    nb = (TT + 511) // 512
    ob = [self.sb(f"ob{i}", [128, TT], BF16) for i in range(2)]
    xo = [self.sb(f"xo{i}", [128, TT], F32) for i in range(2)]
    gemm_load_panel(self, self.s_h, "s_h")

    def epiq(i, banks, wd):
        s = cnt[0] % 2
        cnt[0] += 1
        for b in range(nb):
            w = min(512, TT - b * 512)
            self.act(ob[s][:, b * 512:b * 512 + w], self.ps[banks[b]][:, :w], AF.Copy, (("ps", banks[b]),), (f"ob{s}",),
                     scale=float(XD ** -0.5))
        self.dma("act", self.s_xq[i, :, :], ob[s][:, :], (f"ob{s}",), (("s_xq", i),), f"st{s}")
    tiles = [(j * 128, 128) for j in range(KT)]
    gemm(self, I["w_xa_q"], tiles, epiq)
    self.phase_reset()
    sb = self.sb
    NMT = NM // 128
    mk = sb("mk", [128, NDT, NM], BF16)
    mvs = sb("mvs", [128, NDT, NM], BF16)
    mvt = sb("mvt", [128, NMT, XD], BF16)
    xq = sb("xq", [128, NDT, TT], BF16)
    sc = [sb(f"sc{i}", [128, NM], F32) for i in range(2)]
    mx = [sb(f"mx{i}", [128, 2], F32) for i in range(2)]
    pb = [sb(f"pb{i}", [128, NM], BF16) for i in range(2)]
    pT = [sb(f"pT{i}", [128, NMT, 512], BF16) for i in range(2)]
    oo = [sb(f"oo{i}", [128, 512], BF16) for i in range(2)]
    it = 0
    for xh in range(XH):
        k0 = xh * NDT
        self.dma("sp", mk[:, :, :], self.s_mk[k0:k0 + NDT, :, :].rearrange("k p t -> p k t"), ("s_mk",), ("mk",), "ld0")
        self.dma("sp", mvs[:, :, :], self.s_mv[k0:k0 + NDT, :, :].rearrange("k p t -> p k t"), ("s_mv",), ("mvs",), "ld1")
        self.dma("sp", xq[:, :, :], self.s_xq[k0:k0 + NDT, :, :].rearrange("k p t -> p k t"), ("s_xq",), ("xq",), "ld2")
        for dvt in range(NDT):
            for mt in range(NMT):
                self.tr(self.psb[7][:, mt * 128:(mt + 1) * 128], mvs[:, dvt, mt * 128:(mt + 1) * 128], self.identb[:, :],
                        ("mvs", "identb"), (("ps", 7),))
            self.cp("dve", mvt[:, :, dvt * 128:(dvt + 1) * 128],
                    self.psb[7][:, :NMT * 128].rearrange("p (m d) -> p m d", m=NMT), (("ps", 7),), ("mvt",))
        for tg in range(TT // 512):
            pg = tg % 2
            for t4 in range(4):
                tq = tg * 4 + t4
                z = it % 2
                it += 1
                for dt_ in range(NDT):
                    self.mm(self.ps[z][:, :NM], xq[:, dt_, tq * 128:(tq + 1) * 128], mk[:, dt_, :],
                            dt_ == 0, dt_ == NDT - 1, ("xq", "mk"), (("ps", z),))
                self.P.add("dve", (lambda zz: (lambda e: e.tensor_reduce(out=mx[zz][:, 0:1], in_=self.ps[zz][:, :NM],
                                                                          axis=AX.X, op=ALU.max)))(z),
                           (("ps", z),), (f"mx{z}",))
                self.ts("dve", mx[z][:, 0:1], mx[z][:, 0:1], -1.0, None, ALU.mult, None, (f"mx{z}",), (f"mx{z}",))
                self.act(sc[z][:, :], self.ps[z][:, :NM], AF.Exp, (("ps", z), f"mx{z}"), (f"sc{z}", f"mx{z}"),
                         bias=mx[z][:, 0:1], accum_out=mx[z][:, 1:2])
                self.P.add("dve", (lambda zz: (lambda e: e.reciprocal(out=mx[zz][:, 1:2], in_=mx[zz][:, 1:2])))(z),
                           (f"mx{z}",), (f"mx{z}",))
                self.ts("dve", pb[z][:, :], sc[z][:, :], mx[z][:, 1:2], None, ALU.mult, None,
                        (f"sc{z}", f"mx{z}"), (f"pb{z}",))
                for mt in range(NMT):
                    self.tr(self.psb[4 + z][:, mt * 128:(mt + 1) * 128], pb[z][:, mt * 128:(mt + 1) * 128],
                            self.identb[:, :], (f"pb{z}", "identb"), (("ps", 4 + z),))
                self.cp("dve" if z else "act", pT[pg][:, :, t4 * 128:(t4 + 1) * 128],
                        self.psb[4 + z][:, :NMT * 128].rearrange("p (m t) -> p m t", m=NMT),
                        (("ps", 4 + z),), (f"pT{pg}",))
            for dvt in range(NDT):
                o = it % 2
                it += 1
                for mt in range(NMT):
                    self.mm(self.ps[2 + o][:, :], mvt[:, mt, dvt * 128:(dvt + 1) * 128], pT[pg][:, mt, :],
                            mt == 0, mt == NMT - 1, ("mvt", f"pT{pg}"), (("ps", 2 + o),))
                self.act(oo[o][:, :], self.ps[2 + o][:, :], AF.Copy, (("ps", 2 + o),), (f"oo{o}",))
                self.dma("act", self.s_xo[k0 + dvt, :, tg * 512:(tg + 1) * 512], oo[o][:, :], (f"oo{o}",),
                         (("s_xo", k0 + dvt, tg),), f"st{o}")
    self.phase_reset()
    gemm_setup(self, D, TT)
    xo = [self.sb(f"xo{i}", [128, TT], F32) for i in range(2)]
    gemm_load_panel(self, self.s_xo, "s_xo")
    gemm(self, I["w_xa_o"], tiles, res_epi(self, xo, 0, TT))


Builder.phase_xattn = phase_xattn


def phase_ffn(self):
    cfg = self.cfg
    D, TT, DFF = cfg.D, cfg.TT, cfg.DFF
    KT = D // 128
    NF = DFF // 128
    I = self.I
    self.phase_norm(self.v_norm_ffn, "ffn")
    self.phase_reset()
    gemm_setup(self, D, TT)
    nb = (TT + 511) // 512
    sg = self.sb("sg", [128, TT], BF16)
    ob = [self.sb(f"ob{i}", [128, TT], BF16) for i in range(2)]
    gemm_load_panel(self, self.s_h, "s_h")
    tiles = []
    for j in range(NF):
        tiles.append((j * 128, 128))
        tiles.append((DFF + j * 128, 128))
    cnt = [0]

    def epi(i, banks, wd):
        j = i // 2
        if i % 2 == 0:
            for b in range(nb):
                w = min(512, TT - b * 512)
                self.act(sg[:, b * 512:b * 512 + w], self.ps[banks[b]][:, :w], AF.Silu, (("ps", banks[b]),), ("sg",))
        else:
            s = cnt[0] % 2
            cnt[0] += 1
            for b in range(nb):
                w = min(512, TT - b * 512)
                sl = slice(b * 512, b * 512 + w)
                self.tt("dve", ob[s][:, sl], sg[:, sl], self.ps[banks[b]][:, :w], ALU.mult,
                        ("sg", ("ps", banks[b])), (f"ob{s}",))
            self.dma("act", self.s_act[j, :, :], ob[s][:, :], (f"ob{s}",), (("s_act", j),), f"st{s}")
    gemm(self, I["w_ffn_in"], tiles, epi)
    self.phase_reset()
    TS = min(512, TT)
    gemm_setup(self, DFF, TS)
    xo = [self.sb(f"xo{i}", [128, TS], F32) for i in range(2)]
    otiles = [(j * 128, 128) for j in range(KT)]
    for t0 in range(0, TT, TS):
        gemm_load_panel(self, self.s_act, "s_act", t0=t0)
        gemm(self, I["w_ffn_out"], otiles, res_epi(self, xo, t0, TS))


Builder.phase_ffn = phase_ffn
```

```python
import numpy as np
import ml_dtypes
import concourse.bass as bass
import concourse.mybir as mybir
from concourse.bass_utils import run_bass_kernel_spmd
from contextlib import ExitStack

F32 = mybir.dt.float32
BF16 = mybir.dt.bfloat16
AF = mybir.ActivationFunctionType
ALU = mybir.AluOpType
AX = mybir.AxisListType

COMPUTE = ("pe", "act", "dve", "pool")
ALLENG = ("pe", "act", "dve", "pool", "sp")
EPS = 1e-6
NEG = -30000.0
SAME_ENGINE_SYNC = True


class Op:
    __slots__ = ("eng", "fn", "reads", "writes", "chan", "idx", "deps", "sig",
                 "waits", "ordinal", "needs_sig")

    def __init__(self, eng, fn, reads, writes, chan, idx):
        self.eng = eng
        self.fn = fn
        self.reads = reads
        self.writes = writes
        self.chan = chan
        self.idx = idx
        self.deps = set()
        self.sig = None
        self.waits = []
        self.ordinal = 0
        self.needs_sig = False


class Prog:
    EPOCH = 12000

    def __init__(self, nc, same_engine_sync=True):
        self.nc = nc
        self.ops = []
        self.same_engine_sync = same_engine_sync
        self.barriers = []

    def add(self, eng, fn, reads=(), writes=(), chan=None):
        op = Op(eng, fn, tuple(reads), tuple(writes), chan, len(self.ops))
        self.ops.append(op)
        return op

    def barrier(self):
        self.barriers.append(len(self.ops))

    def analyze(self):
        ops = self.ops
        last_writer = {}
        readers = {}
        chan_last = {}
        chan_count = {}
        eng_last = {}
        bset = set(self.barriers)
        pending = {}
        for op in ops:
            if op.idx in bset:
                deps = set(eng_last.values()) | set(chan_last.values())
                for e in ALLENG:
                    pending.setdefault(e, set()).update(deps)
            d = op.deps
            if op.eng in pending:
                d |= pending.pop(op.eng)
            for k in op.reads:
                if k in last_writer:
                    d.add(last_writer[k])
            for k in op.writes:
                if k in last_writer:
                    d.add(last_writer[k])
                r = readers.get(k)
                if r:
                    for kk, v in r.items():
                        if kk == "dma":
                            d.update(v)
                        else:
                            d.add(v)
            if op.chan is not None:
                if op.chan in chan_last:
                    d.add(chan_last[op.chan])
                chan_last[op.chan] = op.idx
                chan_count[op.chan] = chan_count.get(op.chan, 0) + 1
                op.ordinal = chan_count[op.chan]
            d.discard(op.idx)
            for k in op.reads:
                r = readers.setdefault(k, {})
                if op.chan is not None:
                    r.setdefault("dma", []).append(op.idx)
                else:
                    r[op.eng] = op.idx
            for k in op.writes:
                last_writer[k] = op.idx
                readers[k] = {}
            if op.chan is None:
                eng_last[op.eng] = op.idx
        for op in ops:
            for j in op.deps:
                dj = ops[j]
                if dj.chan is None:
                    if dj.eng != op.eng or op.chan is not None:
                        dj.needs_sig = True
                    elif self.same_engine_sync and dj.eng != "pe":
                        dj.needs_sig = True
        cnt = {e: 0 for e in COMPUTE}
        for op in ops:
            if op.chan is None and op.needs_sig:
                assert op.fn is not None
                cnt[op.eng] += 1
                op.sig = cnt[op.eng]
        self.sig_counts = cnt
        self.chans = sorted(chan_count.keys())
        waited = {e: {} for e in ALLENG}
        for op in ops:
            w = waited[op.eng]
            need = {}
            for j in op.deps:
                dj = ops[j]
                if dj.chan is not None:
                    key = ("c", dj.chan)
                    val = 16 * dj.ordinal
                else:
                    if dj.sig is None:
                        continue
                    if dj.eng == op.eng and op.chan is None and (
                            not self.same_engine_sync or dj.eng == "pe"):
                        continue
                    ep = (dj.sig - 1) // self.EPOCH
                    key = ("e", dj.eng, ep)
                    val = dj.sig - ep * self.EPOCH
                if need.get(key, 0) < val:
                    need[key] = val
            for key, val in need.items():
                if w.get(key, 0) < val:
                    w[key] = val
                    op.waits.append((key, val))

    def emit(self):
        nc = self.nc
        self.analyze()
        sems = {}
        with ExitStack() as es:
            for e in COMPUTE:
                nep = (self.sig_counts[e] + self.EPOCH - 1) // self.EPOCH
                for ep in range(max(nep, 1)):
                    sems[("e", e, ep)] = es.enter_context(nc.semaphore(f"s_{e}_{ep}"))
            for c in self.chans:
                sems[("c", c)] = es.enter_context(nc.semaphore(f"c_{c}"))
            self.nsems = len(sems)
            block = es.enter_context(nc.Block())
            per_eng = {e: [op for op in self.ops if op.eng == e] for e in ALLENG}
            EP = self.EPOCH

            def run(engobj, lst):
                for op in lst:
                    for key, val in op.waits:
                        engobj.wait_ge(sems[key], val)
                    if op.fn is None:
                        continue
                    ins = op.fn(engobj)
                    if op.chan is not None:
                        ins.then_inc(sems[("c", op.chan)], 16)
                    elif op.sig is not None:
                        ep = (op.sig - 1) // EP
                        ins.then_inc(sems[("e", op.eng, ep)], 1)

            @block.tensor
            def _(e):
                run(e, per_eng["pe"])

            @block.scalar
            def _(e):
                run(e, per_eng["act"])

            @block.vector
            def _(e):
                run(e, per_eng["dve"])

            @block.gpsimd
            def _(e):
                run(e, per_eng["pool"])

            @block.sync
            def _(e):
                run(e, per_eng["sp"])


class Cfg:
    def __init__(self, D=4096, TT=2048, G=8, NMEM=256, DFF=11008, stop_after=None):
        self.D = D
        self.TT = TT
        self.G = G
        self.H = 8 * G
        self.DI = 512 * G
        self.CD = self.DI + 2 * G * 128
        self.NH = D // 128
        self.NMEM = NMEM
        self.DFF = DFF
        self.XH = 4
        self.XD = D // 4
        sizes = (self.DI, self.CD, self.H, D, D, D, 2 * D)
        self.off = np.concatenate([[0], np.cumsum(sizes)]).astype(int)
        self.NIN = int(self.off[-1])
        self.stop_after = stop_after


def make_consts():
    c = {}
    c["ident"] = np.eye(128, dtype=np.float32)
    i = np.arange(128)
    c["ones"] = np.ones((128, 128), np.float32)
    c["triinc"] = (i[:, None] <= i[None, :]).astype(np.float32)
    c["negm"] = np.where(i[None, :] < i[:, None], NEG, 0.0).astype(np.float32)
    c["nstrict"] = -(i[:, None] > i[None, :]).astype(np.float32)
    t = np.arange(512)
    m = np.stack([((kb * 128 + i)[:, None] < t[None, :]).astype(np.float32) for kb in range(4)])
    c["m01"] = m.transpose(1, 0, 2).reshape(128, 4 * 512)
    c["mneg"] = ((m - 1.0) * (-NEG)).transpose(1, 0, 2).reshape(128, 4 * 512)
    names = ["ident", "ones", "triinc", "negm", "nstrict"]
    offs = {}
    o = 0
    for n in names:
        offs[n] = (o, c[n].shape[1])
        o += c[n].shape[1]
    packed = np.concatenate([c[n] for n in names], axis=1).astype(np.float32)
    masks = np.concatenate([c["m01"], c["mneg"]], axis=1).astype(np.float32)
    return packed, offs, masks


class Builder:
    def __init__(self, cfg):
        self.cfg = cfg
        self.nc = bass.Bass("TRN2", target_bir_lowering=False)
        self.P = Prog(self.nc, same_engine_sync=SAME_ENGINE_SYNC)
        self.arena_top = 0
        self.arena_base = 0
        self.uid = 0
        self.dram = {}
        self.psum = []
        self.done = False

    def sb(self, name, shape, dt):
        nbytes = int(np.prod(shape[1:])) * (4 if dt == F32 else 2)
        nbytes = (nbytes + 63) // 64 * 64
        self.uid += 1
        t = self.nc.alloc_sbuf_tensor_at(f"{name}_{self.uid}", list(shape), dt, offset=self.arena_top)
        self.arena_top += nbytes
        assert self.arena_top <= self.sb_limit, (name, self.arena_top, self.sb_limit)
        return t

    def phase_reset(self):
        self.P.barrier()
        self.arena_top = self.arena_base

    def dr(self, name, shape, dt):
        t = self.nc.dram_tensor(name, list(shape), dt, kind="Internal")
        self.dram[name] = t
        return t

    def dma(self, q, out, in_, reads, writes, chan, slow=False):
        if slow:
            self.P.add(q, lambda e: e.dma_start(out=out, in_=in_, allow_slow_non_contiguous=True),
                       reads, writes, chan=chan)
        else:
            self.P.add(q, lambda e: e.dma_start(out=out, in_=in_), reads, writes, chan=chan)

    def dma_tiles(self, q, sb_tile, dram, k0, kn, tsl, reads, writes, chans, store=False, step=8):
        for i, a in enumerate(range(0, kn, step)):
            n = min(step, kn - a)
            d = dram[k0 + a:k0 + a + n, :, tsl].rearrange("k p t -> p k t")
            t_ = sb_tile[:, a:a + n, :]
            ch = chans[i % len(chans)]
            if store:
                self.dma(q, d, t_, reads, writes, ch)
            else:
                self.dma(q, t_, d, reads, writes, ch)

    def mm(self, out, lhsT, rhs, start, stop, reads, writes, **kw):
        self.P.add("pe", lambda e: e.matmul(out, lhsT, rhs, start=start, stop=stop, **kw), reads, writes)

    def tr(self, out, in_, ident, reads, writes):
        self.P.add("pe", lambda e: e.transpose(out, in_, ident), reads, writes)

    def act(self, out, in_, func, reads, writes, bias=None, scale=None, accum_out=None, eng="act"):
        kw = {}
        if bias is not None:
            kw["bias"] = bias
        if scale is not None:
            kw["scale"] = scale
        if accum_out is not None:
            kw["accum_out"] = accum_out
        self.P.add("act", lambda e: e.activation(out=out, in_=in_, func=func, **kw), reads, writes)

    def tt(self, eng, out, in0, in1, op, reads, writes):
        self.P.add(eng, lambda e: e.tensor_tensor(out=out, in0=in0, in1=in1, op=op), reads, writes)

    def ts(self, eng, out, in0, s1, s2, op0, op1, reads, writes):
        if op1 is None:
            self.P.add(eng, lambda e: e.tensor_scalar(out=out, in0=in0, scalar1=s1, scalar2=None, op0=op0),
                       reads, writes)
        else:
            self.P.add(eng, lambda e: e.tensor_scalar(out=out, in0=in0, scalar1=s1, scalar2=s2, op0=op0, op1=op1),
                       reads, writes)

    def stt(self, out, in0, scalar, in1, op0, op1, reads, writes):
        self.P.add("dve", lambda e: e.scalar_tensor_tensor(out=out, in0=in0, scalar=scalar, in1=in1,
                                                           op0=op0, op1=op1), reads, writes)

    def cp(self, eng, out, in_, reads, writes):
        if eng == "act":
            self.P.add("act", lambda e: e.copy(out=out, in_=in_), reads, writes)
        else:
            self.P.add(eng, lambda e: e.tensor_copy(out=out, in_=in_), reads, writes)

    def memset(self, eng, ap, val, writes):
        self.P.add(eng, lambda e: e.memset(ap, val), (), writes)

    def build(self):
        cfg = self.cfg
        nc = self.nc
        D, TT, G, H, DI, CD, NH = cfg.D, cfg.TT, cfg.G, cfg.H, cfg.DI, cfg.CD, cfg.NH
        KT = D // 128

        early = cfg.stop_after in ("tin", "pinproj", "pssd", "inproj", "ssd", "attn", "ssdonly")
        self.tiny = set()

        def ein(name, shape):
            if early and name in ("w_ffn_in", "w_ffn_out", "w_xa_kv", "w_xa_q", "w_xa_o", "w_out", "w_ssd_out",
                                  "w_sb_out") or (cfg.stop_after == "ssdonly" and name == "w_in"):
                shape = [128, 128]
                self.tiny.add(name)
            return nc.dram_tensor(name, list(shape), F32, kind="ExternalInput")

        self.packed, self.coffs, self.masks = make_consts()
        I = {}
        I["x_prev"] = ein("x_prev", [TT, D])
        I["x_own"] = ein("x_own", [TT, D])
        I["mem"] = ein("mem", [cfg.NMEM, D])
        I["flag"] = ein("flag", [128, 1])
        I["consts"] = ein("consts", list(self.packed.shape))
        I["cmask"] = ein("cmask", list(self.masks.shape))
        I["norm_mix"] = ein("norm_mix", [D])
        I["w_in"] = ein("w_in", [D, cfg.NIN])
        I["b_gate"] = ein("b_gate", [2 * D])
        I["conv_w"] = ein("conv_w", [4, CD])
        I["conv_b"] = ein("conv_b", [CD])
        I["dt_bias"] = ein("dt_bias", [H])
        I["a_log"] = ein("a_log", [H])
        I["d_skip"] = ein("d_skip", [H])
        I["ssd_norm"] = ein("ssd_norm", [DI])
        I["w_ssd_out"] = ein("w_ssd_out", [DI, D])
        I["w_sb_out"] = ein("w_sb_out", [D, D])
        I["w_out"] = ein("w_out", [D, D])
        I["norm_xa"] = ein("norm_xa", [D])
        I["norm_mem"] = ein("norm_mem", [D])
        I["w_xa_q"] = ein("w_xa_q", [D, D])
        I["w_xa_kv"] = ein("w_xa_kv", [D, 2 * D])
        I["w_xa_o"] = ein("w_xa_o", [D, D])
        I["norm_ffn"] = ein("norm_ffn", [D])
        I["w_ffn_in"] = ein("w_ffn_in", [D, 2 * cfg.DFF])
        I["w_ffn_out"] = ein("w_ffn_out", [cfg.DFF, D])
        I["norm_final"] = ein("norm_final", [D])
        self.I = I
        self.out = nc.dram_tensor("out", [TT, D], F32, kind="ExternalOutput")
        self.dbg = None

        self.ps = [nc.alloc_psum_tensor(f"ps{i}", [128, 512], F32) for i in range(8)]
        self.psb = [p.bitcast(BF16) for p in self.ps]

        self.arena_top = (nc.sbuf_base + 63) // 64 * 64
        self.sb_limit = nc.sbuf_top - 64
        C = {}
        ncol = self.packed.shape[1]
        cst = self.sb("cst", [128, ncol], F32)
        self.dma("sp", cst[:, :], I["consts"][:, :], (), ("cst",), "ld0")

        def cs(name):
            o, n = self.coffs[name]
            return cst[:, o:o + n]
        self.cs = cs
        identb = self.sb("identb", [128, 128], BF16)
        self.cp("dve", identb[:, :], cs("ident"), ("cst",), ("identb",))
        onesb = self.sb("onesb", [128, 128], BF16)
        self.cp("dve", onesb[:, :], cs("ones"), ("cst",), ("onesb",))
        self.identb, self.onesb = identb, onesb
        nstrictb = self.sb("nstrictb", [128, 128], BF16)
        self.cp("dve", nstrictb[:, :], cs("nstrict"), ("cst",), ("nstrictb",))
        self.nstrictb = nstrictb
        nonesb = self.sb("nonesb", [128, 128], BF16)
        self.ts("dve", nonesb[:, :], cs("ones"), -1.0, None, ALU.mult, None, ("cst",), ("nonesb",))
        self.nonesb = nonesb
        flag = self.sb("flag", [128, 1], F32)
        self.dma("sp", flag[:, :], I["flag"][:, :], (), ("flag",), "ld1")
        self.flag = flag

        vstage = self.sb("vstage", [128, 128], F32)
        self.vcount = 0

        def colvec_into(dst_ap, src1d, n, key):
            nt = n // 128
            b = self.vcount % 2
            self.vcount += 1
            self.dma("sp", vstage[:nt, :], src1d.rearrange("(t p) -> t p", p=128), (), ("vstage",), "ld0")
            self.tr(self.ps[b][:, :nt], vstage[:nt, :], self.cs("ident")[:nt, :nt], ("vstage", "cst"), (("ps", b),))
            self.cp("dve", dst_ap, self.ps[b][:, :nt], (("ps", b),), (key,))

        def colvec(name, src, n, chan):
            t_ = self.sb(name, [128, n // 128], F32)
            colvec_into(t_[:, :], src[:], n, name)
            return (t_, name)
        self.colvec = colvec
        self.v_norm_mix = colvec("v_norm_mix", I["norm_mix"], D, "ld0")
        self.v_norm_xa = colvec("v_norm_xa", I["norm_xa"], D, "ld1")
        self.v_norm_mem = colvec("v_norm_mem", I["norm_mem"], D, "ld0")
        self.v_norm_ffn = colvec("v_norm_ffn", I["norm_ffn"], D, "ld1")
        self.v_norm_final = colvec("v_norm_final", I["norm_final"], D, "ld0")
        self.v_ssd_norm = colvec("v_ssd_norm", I["ssd_norm"], DI, "ld1")
        self.v_b_gate = colvec("v_b_gate", I["b_gate"], 2 * D, "ld0")
        self.v_conv_b = colvec("v_conv_b", I["conv_b"], CD, "ld1")
        v_conv_w = self.sb("v_conv_w", [128, 4, CD // 128], F32)
        for j in range(4):
            colvec_into(v_conv_w[:, j, :], I["conv_w"][j, :], CD, "v_conv_w")
        self.v_conv_w = v_conv_w
        hv = self.sb("hv", [H, 4], F32)
        for j, nm in enumerate(["dt_bias", "a_log", "d_skip"]):
            self.dma("sp", hv[:, j:j + 1], I[nm].rearrange("(h o) -> h o", o=1), (), ("hv",), "ld1", slow=True)
        self.hv = hv
        negA = self.sb("negA", [H, 1], F32)
        self.act(negA[:, :], hv[:, 1:2], AF.Exp, ("hv",), ("negA",))
        self.ts("dve", negA[:, :], negA[:, :], -1.0, None, ALU.mult, None, ("negA",), ("negA",))
        self.negA = negA
        self.halo = self.sb("halo", [128, CD // 128, 3], F32)
        self.memset("pool", self.halo[:, :, :], 0.0, ("halo",))
        self.arena_base = self.arena_top

        self.s_xT = self.dr("s_xT", [KT, 128, TT], F32)
        self.s_h = self.dr("s_h", [KT, 128, TT], BF16)
        self.s_sz = self.dr("s_sz", [DI // 128, 128, TT], BF16)
        self.s_xbc = self.dr("s_xbc", [CD // 128, 128, TT], BF16)
        self.s_q = self.dr("s_q", [NH, 128, TT], BF16)
        self.s_k = self.dr("s_k", [NH, 128, 2 * TT], BF16)
        self.s_vT = self.dr("s_vT", [NH, 128, 2 * TT], BF16)
        self.s_gate = self.dr("s_gate", [2 * KT, 128, TT], BF16)
        self.s_yn = self.dr("s_yn", [DI // 128, 128, TT], BF16)
        self.s_o = self.dr("s_o", [NH, 128, TT], BF16)
        self.s_bs = self.dr("s_bs", [KT, 128, TT], BF16)
        self.s_mg = self.dr("s_mg", [KT, 128, TT], BF16)
        self.s_dt = self.dr("s_dt", [2, H, TT], F32)
        self.s_state = self.dr("s_state", [128, H * 64], F32)
        self.s_act = self.dr("s_act", [cfg.DFF // 128, 128, TT], BF16)
        self.s_xq = self.dr("s_xq", [KT, 128, TT], BF16)
        self.s_xo = self.dr("s_xo", [KT, 128, TT], BF16)
        self.s_hm = self.dr("s_hm", [KT, 128, cfg.NMEM], BF16)
        self.s_mT = self.dr("s_mT", [KT, 128, cfg.NMEM], F32)
        self.s_mk = self.dr("s_mk", [KT, 128, cfg.NMEM], BF16)
        self.s_mv = self.dr("s_mv", [KT, 128, cfg.NMEM], BF16)

        self.main()
        self.P.emit()
        return nc

    def stop(self, name):
        if self.cfg.stop_after == name:
            self.done = True
        return self.done

    def main(self):
        cfg = self.cfg
        I = self.I
        if cfg.stop_after == "ssdonly":
            self.phase_transpose_in(I["x_own"], "own")
            self.phase_reset()
            zf = self.sb("zf", [128, cfg.TT], F32)
            zb = self.sb("zb", [128, cfg.TT], BF16)
            self.memset("pool", zf[:, :], 0.01, ("zf",))
            self.memset("pool", zb[:, :], 0.01, ("zb",))
            self.dma("sp", self.s_dt[0, :, :], zf[:cfg.H, :], ("zf",), ("s_dt",), "st0")
            self.ts("dve", zf[:, :], zf[:, :], -1.0, None, ALU.mult, None, ("zf",), ("zf",))
            self.dma("sp", self.s_dt[1, :, :], zf[:cfg.H, :], ("zf",), ("s_dt",), "st0")
            for k in range(cfg.CD // 128):
                self.dma("sp", self.s_xbc[k, :, :], zb[:, :], ("zb",), ("s_xbc",), "st1")
            import os
            for k in range(cfg.DI // 128):
                self.dma("sp", self.s_sz[k, :, :], zb[:, :], ("zb",), ("s_sz",), "st1")
            self.phase_ssd(prev=True)
            if os.environ.get("SSD_OWN"):
                self.phase_ssd(prev=False)
            self.phase_final(norm=False)
            return
        if cfg.stop_after == "tin":
            self.phase_transpose_in(I["x_own"], "own")
            self.phase_norm(self.v_norm_mix, "mix")
            self.phase_final(norm=True)
            return
        self.phase_transpose_in(I["x_prev"], "prev")
        self.phase_norm(self.v_norm_mix, "mix")
        self.phase_inproj(prev=True)
        if self.stop("pinproj"):
            return self.finish_debug()
        self.phase_ssd(prev=True)
        if self.stop("pssd"):
            return self.finish_debug()
        self.phase_transpose_in(I["x_own"], "own")
        self.phase_norm(self.v_norm_mix, "mix")
        self.phase_inproj(prev=False)
        if self.stop("inproj"):
            return self.finish_debug()
        self.phase_ssd(prev=False)
        if self.stop("ssd"):
            return self.finish_debug()
        self.phase_attn()
        if self.stop("attn"):
            return self.finish_debug()
        self.phase_merge()
        if self.stop("merge"):
            return self.finish_debug()
        self.phase_xattn()
        if self.stop("xattn"):
            return self.finish_debug()
        self.phase_ffn()
        self.phase_final()

    def finish_debug(self):
        self.phase_final(norm=False)

    def phase_transpose_in(self, x, tag, TT=None, dst=None):
        cfg = self.cfg
        D = cfg.D
        TT = TT or cfg.TT
        dst = dst if dst is not None else self.s_xT
        KT = D // 128
        self.phase_reset()
        xt = [self.sb(f"xt{i}", [128, D], F32) for i in range(2)]
        ot = [self.sb(f"xo{i}", [128, KT, 128], F32) for i in range(2)]
        ident = self.cs("ident")
        for tt in range(TT // 128):
            s = tt % 2
            self.dma("sp", xt[s][:, :], x[tt * 128:(tt + 1) * 128, :], (), (f"xt{s}",), f"ld{s}")
            for g in range(KT // 4):
                bank = (tt * (KT // 4) + g) % 8
                for j in range(4):
                    ft = g * 4 + j
                    self.tr(self.ps[bank][:, j * 128:(j + 1) * 128], xt[s][:, ft * 128:(ft + 1) * 128],
                            ident, (f"xt{s}", "cst"), (("ps", bank),))
                eng = "act" if g % 2 == 0 else "dve"
                self.cp(eng, ot[s][:, g * 4:(g + 1) * 4, :],
                        self.ps[bank][:, :].rearrange("p (j t) -> p j t", j=4),
                        (("ps", bank),), (f"xo{s}",))
            self.dma_tiles("sp", ot[s], dst, 0, KT, slice(tt * 128, (tt + 1) * 128), (f"xo{s}",), (dst.name,),
                           (f"st{s}",), store=True)

    def phase_norm(self, wv, tag, src=None, dst=None, ntok=None, D=None):
        cfg = self.cfg
        wvec, wkey = wv
        D = D or cfg.D
        TT = ntok or cfg.TT
        KT = D // 128
        src = src if src is not None else self.s_xT
        dst = dst if dst is not None else self.s_h
        skey = src.name
        dkey = dst.name
        self.phase_reset()
        xin = [self.sb(f"nx{i}", [128, TT], F32) for i in range(3)]
        sq = [self.sb(f"nsq{i}", [128, TT], F32) for i in range(2)]
        acc = self.sb("nacc", [128, TT], F32)
        rstd = self.sb("nrstd", [128, TT], F32)
        ho = [self.sb(f"nho{i}", [128, TT], BF16) for i in range(2)]
        for kt in range(KT):
            s = kt % 3
            self.dma("sp", xin[s][:, :], src[kt, :, :], (skey,), (f"nx{s}",), f"ld{s}")
            if kt == 0:
                self.act(acc[:, :], xin[s][:, :], AF.Square, (f"nx{s}",), ("nacc",))
            else:
                s2 = kt % 2
                self.act(sq[s2][:, :], xin[s][:, :], AF.Square, (f"nx{s}",), (f"nsq{s2}",))
                self.tt("pool", acc[:, :], acc[:, :], sq[s2][:, :], ALU.add, ("nacc", f"nsq{s2}"), ("nacc",))
        ones = self.cs("ones")
        nb = (TT + 511) // 512
        for b in range(nb):
            w = min(512, TT - b * 512)
            self.mm(self.ps[b][:, :w], ones, acc[:, b * 512:b * 512 + w], True, True,
                    ("cst", "nacc"), (("ps", b),))
            self.act(rstd[:, b * 512:b * 512 + w], self.ps[b][:, :w], AF.Ln, (("ps", b),), ("nrstd",),
                     bias=EPS, scale=1.0 / D)
        self.act(rstd[:, :], rstd[:, :], AF.Exp, ("nrstd",), ("nrstd",), scale=-0.5)
        for kt in range(KT):
            s = kt % 3
            s2 = kt % 2
            self.dma("sp", xin[s][:, :], src[kt, :, :], (skey,), (f"nx{s}",), f"ld{s}")
            self.stt(ho[s2][:, :], xin[s][:, :], wvec[:, kt:kt + 1], rstd[:, :], ALU.mult, ALU.mult,
                     (f"nx{s}", "nrstd", wkey), (f"nho{s2}",))
            self.dma("sp", dst[kt, :, :], ho[s2][:, :], (f"nho{s2}",), (dkey,), f"st{s2}")


    def phase_final(self, norm=True):
        cfg = self.cfg
        D, TT = cfg.D, cfg.TT
        KT = D // 128
        NTT = TT // 128
        src = self.s_xT
        self.phase_reset()
        xin = [self.sb(f"fx{i}", [128, TT], F32) for i in range(2)]
        sq = [self.sb(f"fsq{i}", [128, TT], F32) for i in range(2)]
        acc = self.sb("facc", [128, TT], F32)
        rstd = self.sb("frstd", [128, TT], F32)
        yk = [self.sb(f"fy{i}", [128, TT], F32) for i in range(2)]
        ot = [self.sb(f"fo{i}", [128, NTT, 128], F32) for i in range(2)]
        wvec, wkey = self.v_norm_final
        if norm:
            for kt in range(KT):
                s = kt % 2
                self.dma("sp", xin[s][:, :], src[kt, :, :], ("s_xT",), (f"fx{s}",), f"ld{s}")
                if kt == 0:
                    self.act(acc[:, :], xin[s][:, :], AF.Square, (f"fx{s}",), ("facc",))
                else:
                    self.act(sq[s][:, :], xin[s][:, :], AF.Square, (f"fx{s}",), (f"fsq{s}",))
                    self.tt("pool", acc[:, :], acc[:, :], sq[s][:, :], ALU.add, ("facc", f"fsq{s}"), ("facc",))
            ones = self.cs("ones")
            for b in range((TT + 511) // 512):
                w = min(512, TT - b * 512)
                self.mm(self.ps[b][:, :w], ones, acc[:, b * 512:b * 512 + w], True, True,
                        ("cst", "facc"), (("ps", b),))
                self.act(rstd[:, b * 512:b * 512 + w], self.ps[b][:, :w], AF.Ln, (("ps", b),), ("frstd",),
                         bias=EPS, scale=1.0 / D)
            self.act(rstd[:, :], rstd[:, :], AF.Exp, ("frstd",), ("frstd",), scale=-0.5)
        ident = self.cs("ident")
        for kt in range(KT):
            s = kt % 2
            self.dma("sp", xin[s][:, :], src[kt, :, :], ("s_xT",), (f"fx{s}",), f"ld{s}")
            if norm:
                self.stt(yk[s][:, :], xin[s][:, :], wvec[:, kt:kt + 1], rstd[:, :], ALU.mult, ALU.mult,
                         (f"fx{s}", "frstd", wkey), (f"fy{s}",))
                y, ykey = yk[s], f"fy{s}"
            else:
                y, ykey = xin[s], f"fx{s}"
            for g in range(NTT // 4):
                bank = (kt * (NTT // 4) + g) % 8
                for j in range(4):
                    tt = g * 4 + j
                    self.tr(self.ps[bank][:, j * 128:(j + 1) * 128], y[:, tt * 128:(tt + 1) * 128],
                            ident, (ykey, "cst"), (("ps", bank),))
                eng = "act" if g % 2 == 0 else "dve"
                self.cp(eng, ot[s][:, g * 4:(g + 1) * 4, :],
                        self.ps[bank][:, :].rearrange("p (j t) -> p j t", j=4),
                        (("ps", bank),), (f"fo{s}",))
            self.dma("sp", self.out[:, kt * 128:(kt + 1) * 128].rearrange("(t p) f -> p t f", p=128),
                     ot[s][:, :, :], (f"fo{s}",), ("out",), f"st{s}")
        self.P.add("sp", None, ("out",), ())


_CACHE = {}


def _get_nc(cfg_key, cfg):
    if cfg_key not in _CACHE:
        b = Builder(cfg)
        nc = b.build()
        _CACHE[cfg_key] = (nc, b)
    return _CACHE[cfg_key]


def run_cfg(cfg, inputs, n_batch, cfg_key):
    nc, b = _get_nc(cfg_key, cfg)
    TT, D = cfg.TT, cfg.D
    f32 = np.float32
    x = np.ascontiguousarray(inputs["x"], dtype=f32)
    mem = np.ascontiguousarray(inputs["mem"], dtype=f32)
    shared = {}
    for k in ["norm_mix", "w_in", "b_gate", "conv_w", "conv_b", "dt_bias", "a_log", "d_skip", "ssd_norm",
              "w_ssd_out", "w_sb_out", "w_out", "norm_xa", "norm_mem", "w_xa_q", "w_xa_kv", "w_xa_o",
              "norm_ffn", "w_ffn_in", "w_ffn_out"]:
        shared[k] = np.ascontiguousarray(np.asarray(inputs[k], dtype=f32)[0])
    shared["norm_final"] = np.ascontiguousarray(inputs["norm_final"], dtype=f32)
    for k in b.tiny:
        shared[k] = np.zeros((128, 128), f32)
    shared["consts"] = b.packed
    shared["cmask"] = b.masks
    zeros = np.zeros((TT, D), f32)
    in_maps = []
    ncores = 2 * n_batch
    for c in range(ncores):
        bi, half = c // 2, c % 2
        m = dict(shared)
        m["x_own"] = np.ascontiguousarray(x[bi, half * TT:(half + 1) * TT])
        m["x_prev"] = np.ascontiguousarray(x[bi, 0:TT]) if half == 1 else zeros
        m["mem"] = np.ascontiguousarray(mem[bi])
        m["flag"] = np.full((128, 1), float(half), f32)
        in_maps.append(m)
    res = run_bass_kernel_spmd(nc, in_maps, core_ids=list(range(ncores)))
    out = np.zeros((n_batch, 2 * TT, D), f32)
    for c in range(ncores):
        bi, half = c // 2, c % 2
        out[bi, half * TT:(half + 1) * TT] = np.asarray(res.results[c]["out"], dtype=f32)
    return out


def kernel(**inputs):
    cfg = Cfg()
    return run_cfg(cfg, inputs, 4, "full")


KCH = 16


def gemm_setup(self, K, TTg):
    KT = K // 128
    self.g_panel = self.sb("panel", [128, KT, TTg], BF16)
    self.g_st = [self.sb(f"gst{i}", [128, KCH, 128], F32) for i in range(3)]
    self.g_wb = [self.sb(f"gwb{i}", [128, KCH, 128], BF16) for i in range(3)]
    self.g_u = 0
    self.g_nt = 0
    self.g_TT = TTg
    self.g_KT = KT


def gemm_load_panel(self, src, skey, t0=0):
    KT, TTg = self.g_KT, self.g_TT
    step = 8
    for k0 in range(0, KT, step):
        kn = min(step, KT - k0)
        self.dma("sp", self.g_panel[:, k0:k0 + kn, :],
                 src[k0:k0 + kn, :, t0:t0 + TTg].rearrange("k p t -> p k t"),
                 (skey,), ("panel",), f"ld{(k0 // step) % 3}")


def gemm(self, W, tiles, epi):
    KT, TTg = self.g_KT, self.g_TT
    nb = (TTg + 511) // 512
    nsets = 8 // nb
    units = []
    for i, (c0, wd) in enumerate(tiles):
        nk = (KT + KCH - 1) // KCH
        for kc in range(nk):
            units.append((i, c0, wd, kc, kc == nk - 1))

    def load(u):
        i, c0, wd, kc, last = units[u]
        s = (self.g_u + u) % 3
        k0 = kc * KCH
        kn = min(KCH, KT - k0)
        self.dma("sp", self.g_st[s][:, :kn, :wd],
                 W[k0 * 128:(k0 + kn) * 128, c0:c0 + wd].rearrange("(kt p) n -> p kt n", p=128),
                 (), (f"gst{s}",), f"w{s}")
        self.cp("act", self.g_wb[s][:, :kn, :wd], self.g_st[s][:, :kn, :wd], (f"gst{s}",), (f"gwb{s}",))

    LA = 2
    for u in range(min(LA, len(units))):
        load(u)
    for u in range(len(units)):
        if u + LA < len(units):
            load(u + LA)
        i, c0, wd, kc, last = units[u]
        s = (self.g_u + u) % 3
        k0 = kc * KCH
        kn = min(KCH, KT - k0)
        setn = (self.g_nt + i) % nsets
        banks = [setn * nb + b for b in range(nb)]
        for kt in range(kn):
            for b in range(nb):
                w = min(512, TTg - b * 512)
                self.mm(self.ps[banks[b]][:wd, :w], self.g_wb[s][:, kt, :wd],
                        self.g_panel[:, k0 + kt, b * 512:b * 512 + w],
                        (k0 + kt == 0), (k0 + kt == KT - 1),
                        (f"gwb{s}", "panel"), (("ps", banks[b]),))
        if last:
            epi(i, banks, wd)
    self.g_u += len(units)
    self.g_nt += len(tiles)


def phase_inproj(self, prev):
    cfg = self.cfg
    D, TT, G, H, DI, CD, NH = cfg.D, cfg.TT, cfg.G, cfg.H, cfg.DI, cfg.CD, cfg.NH
    off = cfg.off
    I = self.I
    self.phase_reset()
    gemm_setup(self, D, TT)
    ob = [self.sb(f"ob{i}", [128, TT], BF16) for i in range(2)]
    u_t = self.sb("cu", [128, TT + 3], F32)
    acc = self.sb("cacc", [128, TT], F32)
    gemm_load_panel(self, self.s_h, "s_h")
    koff = 0 if prev else TT
    tiles = []
    kinds = []

    def addseg(kind, seg, n):
        for j in range(0, n, 128):
            tiles.append((int(off[seg]) + j, min(128, n - j)))
            kinds.append((kind, j // 128))
    if not prev:
        addseg("z", 0, DI)
    addseg("xbc", 1, CD)
    addseg("dt", 2, H)
    if not prev:
        addseg("q", 3, D)
    addseg("k", 4, D)
    addseg("v", 5, D)
    if not prev:
        addseg("gate", 6, 2 * D)
    nb = (TT + 511) // 512
    cnt = [0]

    def evac(dst, banks, func, okey, wd=128, bias=None, scale=None):
        for b in range(nb):
            w = min(512, TT - b * 512)
            self.act(dst[:wd, b * 512:b * 512 + w], self.ps[banks[b]][:wd, :w], func,
                     (("ps", banks[b]),), (okey,), bias=bias, scale=scale)

    def epi(i, banks, wd):
        kind, idx = kinds[i]
        s = cnt[0] % 2
        cnt[0] += 1
        okey = f"ob{s}"
        o = ob[s]
        if kind == "z":
            evac(o, banks, AF.Silu, okey)
            self.dma("act", self.s_sz[idx, :, :], o[:, :], (okey,), ("s_sz",), f"st{s}")
        elif kind == "q":
            evac(o, banks, AF.Copy, okey, scale=float(128 ** -0.5))
            self.dma("act", self.s_q[idx, :, :], o[:, :], (okey,), ("s_q",), f"st{s}")
        elif kind == "k":
            evac(o, banks, AF.Copy, okey)
            self.dma("act", self.s_k[idx, :, koff:koff + TT], o[:, :], (okey,), ("s_k",), f"st{s}")
        elif kind == "v":
            if prev:
                evac(o, banks, AF.Copy, okey, scale=self.flag[:, 0:1])
            else:
                evac(o, banks, AF.Copy, okey)
            self.dma("act", self.s_vT[idx, :, koff:koff + TT], o[:, :], (okey,), ("s_vT",), f"st{s}")
        elif kind == "gate":
            evac(o, banks, AF.Sigmoid, okey, bias=self.v_b_gate[0][:, idx:idx + 1])
            self.dma("act", self.s_gate[idx, :, :], o[:, :], (okey,), ("s_gate",), f"st{s}")
        elif kind == "dt":
            evac(acc, banks, AF.Exp, "cacc", wd=H, bias=self.hv[:, 0:1])
            self.act(acc[:H, :], acc[:H, :], AF.Ln, ("cacc",), ("cacc",), bias=1.0)
            self.ts("dve", u_t[:H, :TT], acc[:H, :], self.negA[:, 0:1], None, ALU.mult, None,
                    ("cacc", "negA"), ("cu",))
            self.dma("act", self.s_dt[0, :, :], acc[:H, :], ("cacc",), ("s_dt",), "st0")
            self.dma("act", self.s_dt[1, :, :], u_t[:H, :TT], ("cu",), ("s_dt",), "st1")
        elif kind == "xbc":
            cw = self.v_conv_w
            self.cp("dve", u_t[:, 0:3], self.halo[:, idx, :], ("halo",), ("cu",))
            evac(u_t[:, 3:], banks, AF.Copy, "cu")
            self.cp("dve", self.halo[:, idx, :], u_t[:, TT:TT + 3], ("cu",), ("halo",))
            self.ts("dve", acc[:, :], u_t[:, 3:3 + TT], cw[:, 3, idx:idx + 1], self.v_conv_b[0][:, idx:idx + 1],
                    ALU.mult, ALU.add, ("cu", "v_conv_w", "v_conv_b"), ("cacc",))
            for j in (2, 1, 0):
                self.stt(acc[:, :], u_t[:, j:j + TT], cw[:, j, idx:idx + 1], acc[:, :], ALU.mult, ALU.add,
                         ("cu", "cacc", "v_conv_w"), ("cacc",))
            self.act(o[:, :], acc[:, :], AF.Silu, ("cacc",), (okey,))
            self.dma("act", self.s_xbc[idx, :, :], o[:, :], (okey,), ("s_xbc",), f"st{s}")

    gemm(self, I["w_in"], tiles, epi)


Builder.phase_inproj = phase_inproj


def phase_ssd(self, prev):
    cfg = self.cfg
    D, TT, G, H, DI, CD = cfg.D, cfg.TT, cfg.G, cfg.H, cfg.DI, cfg.CD
    NDI = DI // 128
    NCD = CD // 128
    BLK = min(TT, 256)
    NCH = BLK // 64
    self.phase_reset()
    sb = self.sb
    ident = self.cs("ident")
    ones = self.cs("ones")
    triinc = self.cs("triinc")
    negm = self.cs("negm")
    xbc = sb("xbc", [128, NCD, BLK], BF16)
    dtb = sb("dtb", [H, 2, BLK], F32)
    Hst = sb("Hst", [128, H * 64], F32)
    prevT = sb("prevT", [128, H * 64], BF16)
    X = sb("X", [64, DI], BF16)
    Btok = sb("Btok", [64, G * 128], BF16)
    dts = sb("dts", [64, 2 * H], F32)
    acum = sb("acum", [64, H], F32)
    cdec = sb("cdec", [128, H], F32)
    w2 = sb("w2", [64, H], F32)
    Xw = [sb(f"Xw{i}", [64, 8, 64], BF16) for i in range(2)]
    if not prev:
        szb = sb("szb", [128, NDI, BLK], BF16)
        gbuf = sb("gbuf", [128, NDI, BLK], F32)
        ynb = sb("ynb", [128, NDI, BLK], BF16)
        sq = sb("sq", [128, 4, BLK], BF16)
        rstd = sb("rstd", [128, BLK], F32)
        Dg = [sb(f"Dg{i}", [64, 8, 64], F32) for i in range(2)]
        Eg = [sb(f"Eg{i}", [64, 8, 64], F32) for i in range(2)]
        EA = [sb(f"EA{i}", [128, 8, 64], BF16) for i in range(2)]
        LT = [sb(f"LT{i}", [64, 8, 64], BF16) for i in range(2)]
        MT = [sb(f"MT{i}", [64, 8, 64], BF16) for i in range(2)]
        Cs = [sb(f"Cs{i}", [128, 8, 64], BF16) for i in range(2)]
        Xd = [sb(f"Xd{i}", [64, 8, 64], BF16) for i in range(2)]
        cb = [sb(f"cb{i}", [64, 64], F32) for i in range(2)]
        ysb = [sb(f"ysb{i}", [64, 512], F32) for i in range(2)]
        DSI = sb("DSI", [64, H, 64], BF16)
        dsr = sb("dsr", [64, H], F32)
        d2 = sb("d2", [H, H], F32)
        self.ts("dve", d2[:, :], ident[:H, :H], self.hv[:, 2:3], None, ALU.mult, None, ("cst", "hv"), ("d2",))
        self.mm(self.ps[0][:64, :H], ones[:H, :64], d2[:, :], True, True, ("cst", "d2"), (("ps", 0),))
        self.cp("dve", dsr[:, :], self.ps[0][:64, :H], (("ps", 0),), ("dsr",))
        self.tt("dve", DSI[:, :, :], ident[:64, :64].unsqueeze(1).to_broadcast([64, H, 64]),
                dsr[:, :].unsqueeze(2).to_broadcast([64, H, 64]), ALU.mult, ("cst", "dsr"), ("DSI",))
    if prev:
        self.memset("pool", Hst[:, :], 0.0, ("Hst",))
    else:
        self.dma("sp", Hst[:, :], self.s_state[:, :], ("s_state",), ("Hst",), "ld0")
        self.ts("dve", Hst[:, :], Hst[:, :], self.flag[:, 0:1], None, ALU.mult, None, ("Hst", "flag"), ("Hst",))
        self.cp("act", prevT[:, :], Hst[:, :], ("Hst",), ("prevT",))

    def bc_h(ap2, n=8):
        return ap2.unsqueeze(2).to_broadcast([ap2.shape[0], n, 64])

    def bc_m(ap2, n=8):
        return ap2.unsqueeze(1).to_broadcast([ap2.shape[0], n, 64])

    it = 0
    import os
    NBLK_ = int(os.environ.get("SSD_NBLK", TT // BLK))
    SEC_ = int(os.environ.get("SSD_SEC", 9))
    for blk in range(min(NBLK_, TT // BLK)):
        t0 = blk * BLK
        tiles_needed = list(range(NCD)) if not prev else list(range(NDI + G))
        nld = len(tiles_needed)
        self.dma_tiles("sp", xbc, self.s_xbc, 0, nld, slice(t0, t0 + BLK), ("s_xbc",), ("xbc",), ("ld0", "ld2"))
        self.dma("sp", dtb[:, :, :], self.s_dt[:, :, t0:t0 + BLK].rearrange("a h t -> h a t"),
                 ("s_dt",), ("dtb",), "ld1")
        if not prev:
            self.dma_tiles("sp", szb, self.s_sz, 0, NDI, slice(t0, t0 + BLK), ("s_sz",), ("szb",), ("ld2", "ld1"))
        for ch in range(NCH if SEC_ > 0 else 0):
            lo = ch * 64
            self.tr(self.ps[0][:64, 0:H], dtb[:, 0, lo:lo + 64], ident[:H, :H], ("dtb", "cst"), (("ps", 0),))
            self.tr(self.ps[0][:64, H:2 * H], dtb[:, 1, lo:lo + 64], ident[:H, :H], ("dtb", "cst"), (("ps", 0),))
            self.cp("dve", dts[:, :], self.ps[0][:64, :2 * H], (("ps", 0),), ("dts",))
            if SEC_ < 2:
                continue
            self.mm(self.ps[1][:64, 0:H], triinc[:64, :64], dts[:, H:2 * H], True, True, ("cst", "dts"), (("ps", 1),))
            self.cp("act", acum[:, :], self.ps[1][:64, 0:H], (("ps", 1),), ("acum",))
            if SEC_ < 3:
                continue
            self.mm(self.ps[3][:, 0:2 * H], ones[:64, :], dts[:, 0:2 * H], True, True, ("cst", "dts"), (("ps", 3),))
            if SEC_ < 4:
                continue
            self.act(cdec[:, :], self.ps[3][:, H:2 * H], AF.Exp, (("ps", 3),), ("cdec",))
            if SEC_ < 5:
                continue
            self.cp("act", w2[:, :], self.ps[3][:64, H:2 * H], (("ps", 3),), ("w2",))
            self.tt("dve", w2[:, :], w2[:, :], acum[:, :], ALU.subtract, ("w2", "acum"), ("w2",))
            if SEC_ < 6:
                continue
            self.act(w2[:, :], w2[:, :], AF.Exp, ("w2",), ("w2",))
            if SEC_ < 7:
                continue
            self.tt("dve", w2[:, :], w2[:, :], dts[:, 0:H], ALU.mult, ("w2", "dts"), ("w2",))
            if SEC_ < 8:
                continue
            for c8 in range(0, NDI, 8):
                n8 = min(8, NDI - c8)
                for j in range(n8):
                    self.tr(self.psb[2][:64, j * 128:(j + 1) * 128], xbc[:, c8 + j, lo:lo + 64], self.identb[:, :],
                            ("xbc", "identb"), (("ps", 2),))
                self.cp("act" if (c8 // 8) % 2 == 0 else "dve", X[:, c8 * 128:(c8 + n8) * 128],
                        self.psb[2][:64, :n8 * 128], (("ps", 2),), ("X",))
            for g in range(G):
                self.tr(self.psb[2][:64, g * 128:(g + 1) * 128], xbc[:, NDI + g, lo:lo + 64], self.identb[:, :],
                        ("xbc", "identb"), (("ps", 2),))
            self.cp("dve", Btok[:, :], self.psb[2][:64, :G * 128], (("ps", 2),), ("Btok",))
            for g in range(G if SEC_ > 8 else 0):
                s = it % 2
                it += 1
                hs = slice(8 * g, 8 * g + 8)
                Xg = X[:, g * 512:(g + 1) * 512].rearrange("p (h e) -> p h e", h=8)
                if not prev:
                    Bt = xbc[:, NDI + g, lo:lo + 64]
                    Ct = xbc[:, NDI + G + g, lo:lo + 64]
                    self.tt("dve", Dg[s][:, :, :], bc_h(acum[:, hs]), bc_m(ident[:64, :64]), ALU.mult,
                            ("acum", "cst"), (f"Dg{s}",))
                    self.tt("dve", Eg[s][:, :, :], bc_m(negm[:64, :64]), bc_h(acum[:, hs]), ALU.subtract,
                            ("acum", "cst"), (f"Eg{s}",))
                    Dg2 = Dg[s][:, :, :].rearrange("p h l -> p (h l)")
                    Eg2 = Eg[s][:, :, :].rearrange("p h l -> p (h l)")
                    self.mm(self.ps[3][:, :], ones[:64, :], Dg2, True, True, ("cst", f"Dg{s}"), (("ps", 3),))
                    self.mm(self.ps[4][:64, :], ones[:64, :64], Dg2, True, False, ("cst", f"Dg{s}"), (("ps", 4),))
                    self.mm(self.ps[4][:64, :], ident[:64, :64], Eg2, False, True, ("cst", f"Eg{s}"), (("ps", 4),))
                    self.act(EA[s][:, :, :].rearrange("p h l -> p (h l)"), self.ps[3][:, :], AF.Exp,
                             (("ps", 3),), (f"EA{s}",))
                    self.act(LT[s][:, :, :].rearrange("p h l -> p (h l)"), self.ps[4][:64, :], AF.Exp,
                             (("ps", 4),), (f"LT{s}",))
                    self.mm(self.ps[5][:64, 0:64], Bt, Ct, True, True, ("xbc",), (("ps", 5),))
                    self.cp("act", cb[s][:, :], self.ps[5][:64, 0:64], (("ps", 5),), (f"cb{s}",))
                    self.tt("dve", MT[s][:, :, :], LT[s][:, :, :], bc_m(cb[s][:, :]), ALU.mult,
                            (f"LT{s}", f"cb{s}"), (f"MT{s}",))
                    self.tt("pool", Cs[s][:, :, :], EA[s][:, :, :], bc_m(Ct), ALU.mult,
                            (f"EA{s}", "xbc"), (f"Cs{s}",))
                    self.tt("pool", Xd[s][:, :, :], Xg, bc_h(dts[:, hs]), ALU.mult, ("X", "dts"), (f"Xd{s}",))
                self.tt("dve", Xw[s][:, :, :], Xg, bc_h(w2[:, hs]), ALU.mult, ("X", "w2"), (f"Xw{s}",))
                if not prev:
                    for h in range(8):
                        hg = 8 * g + h
                        yo = self.ps[6][:64, h * 64:(h + 1) * 64]
                        self.mm(yo, MT[s][:, h, :], Xd[s][:, h, :], True, False, (f"MT{s}", f"Xd{s}"), (("ps", 6),))
                        self.mm(yo, DSI[:, hg, :], Xg[:, h, :], False, False, ("DSI", "X"), (("ps", 6),))
                        self.mm(yo, Cs[s][:, h, :], prevT[:, hg * 64:(hg + 1) * 64], False, True,
                                (f"Cs{s}", "prevT"), (("ps", 6),))
                    self.cp("act", ysb[s][:, :], self.ps[6][:64, :], (("ps", 6),), (f"ysb{s}",))
                    for j in range(4):
                        self.tr(self.ps[5][:, 128 + j * 64:128 + (j + 1) * 64], ysb[s][:, j * 128:(j + 1) * 128],
                                ident[:64, :64], (f"ysb{s}", "cst"), (("ps", 5),))
                    self.tt("dve", gbuf[:, 4 * g:4 * g + 4, lo:lo + 64],
                            self.ps[5][:, 128:384].rearrange("p (j l) -> p j l", j=4),
                            szb[:, 4 * g:4 * g + 4, lo:lo + 64], ALU.mult, (("ps", 5), "szb"), ("gbuf",))
                self.mm(self.ps[7][:, :], Btok[:, g * 128:(g + 1) * 128], Xw[s][:, :, :].rearrange("p h e -> p (h e)"),
                        True, True, ("Btok", f"Xw{s}"), (("ps", 7),))
                Hg = Hst[:, g * 512:(g + 1) * 512]
                Hg3 = Hg.rearrange("p (h e) -> p h e", h=8)
                self.tt("dve", Hg3, Hg3, bc_h(cdec[:, hs]), ALU.mult, ("Hst", "cdec"), ("Hst",))
                self.tt("dve", Hg, Hg, self.ps[7][:, :], ALU.add, ("Hst", ("ps", 7)), ("Hst",))
                if not prev:
                    self.cp("act", prevT[:, g * 512:(g + 1) * 512], Hg, ("Hst",), ("prevT",))
        if not prev:
            vn, vkey = self.v_ssd_norm
            for g in range(G):
                self.act(sq[:, :, :], gbuf[:, 4 * g:4 * g + 4, :], AF.Square, ("gbuf",), ("sq",))
                for j in range(4):
                    self.mm(self.ps[1][:, :BLK], self.onesb[:, :], sq[:, j, :], j == 0, j == 3,
                            ("onesb", "sq"), (("ps", 1),))
                self.act(rstd[:, :], self.ps[1][:, :BLK], AF.Ln, (("ps", 1),), ("rstd",), bias=EPS, scale=1.0 / 512)
                self.act(rstd[:, :], rstd[:, :], AF.Exp, ("rstd",), ("rstd",), scale=-0.5)
                for j in range(4):
                    ct = 4 * g + j
                    self.stt(ynb[:, ct, :], gbuf[:, ct, :], vn[:, ct:ct + 1], rstd[:, :], ALU.mult, ALU.mult,
                             ("gbuf", "rstd", vkey), ("ynb",))
            self.dma_tiles("act", ynb, self.s_yn, 0, NDI, slice(t0, t0 + BLK), ("ynb",), ("s_yn",), ("st0", "st1"),
                           store=True)
    if prev:
        self.dma("act", self.s_state[:, :], Hst[:, :], ("Hst",), ("s_state",), "st1")


Builder.phase_ssd = phase_ssd


def phase_attn(self):
    cfg = self.cfg
    D, TT, NH = cfg.D, cfg.TT, cfg.NH
    self.phase_reset()
    sb = self.sb
    NQG = TT // 512
    NPB = TT // 128
    m01f = sb("m01f", [128, 4, 512], F32)
    mneg = sb("mneg", [128, 4, 512], F32)
    m01 = sb("m01", [128, 4, 512], BF16)
    self.dma("sp", m01f[:, :, :], self.I["cmask"][:, 0:2048].rearrange("p (k t) -> p k t", k=4), (), ("m01f",), "ld0")
    self.dma("sp", mneg[:, :, :], self.I["cmask"][:, 2048:4096].rearrange("p (k t) -> p k t", k=4), (), ("mneg",), "ld1")
    self.cp("dve", m01[:, :, :], m01f[:, :, :], ("m01f",), ("m01",))
    Kh = [sb(f"Kh{i}", [128, 2 * TT], BF16) for i in range(2)]
    Qh = [sb(f"Qh{i}", [128, TT], BF16) for i in range(2)]
    Vs = [sb(f"Vs{i}", [128, 2 * TT], BF16) for i in range(2)]
    Vh = [sb(f"Vh{i}", [128, 2 * TT // 128, 128], BF16) for i in range(2)]
    oT = [sb(f"oT{i}", [128, TT], BF16) for i in range(2)]
    NZ = 4
    zbanks = [0, 1, 5, 6]
    et = [sb(f"et{i}", [128, 512], F32) for i in range(NZ)]
    sp_ = [sb(f"sp{i}", [128, 512], BF16) for i in range(NZ + 2)]
    T2 = [sb(f"T2{i}", [128, 512], F32) for i in range(NZ)]
    Wt = [sb(f"Wt{i}", [128, 512], BF16) for i in range(NZ)]
    At = [sb(f"At{i}", [128, 512], BF16) for i in range(3)]
    nkb_all = 2 * TT // 128

    def head_load(h):
        s = h % 2
        self.dma("sp", Kh[s][:, :], self.s_k[h, :, :], ("s_k",), (f"Kh{s}",), f"ld{s}")
        self.dma("sp", Qh[s][:, :], self.s_q[h, :, :], ("s_q",), (f"Qh{s}",), "ld2")
        self.dma("sp", Vs[s][:, :], self.s_vT[h, :, :], ("s_vT",), (f"Vs{s}",), f"ld{s}")
        for k8 in range(0, nkb_all, 8):
            for j in range(8):
                kb = k8 + j
                self.tr(self.psb[4][:, j * 128:(j + 1) * 128], Vs[s][:, kb * 128:(kb + 1) * 128], self.identb[:, :],
                        (f"Vs{s}", "identb"), (("ps", 4),))
            self.cp("dve", Vh[s][:, k8:k8 + 8, :], self.psb[4][:, :].rearrange("p (j d) -> p j d", j=8),
                    (("ps", 4),), (f"Vh{s}",))

    its = []
    og = 0
    for h in range(NH):
        for qg in range(NQG):
            nk = NPB + 4 * (qg + 1)
            ob = 2 + (og % 2)
            og += 1
            for idx, kb in enumerate(range(nk - 1, -1, -1)):
                its.append(dict(h=h, qg=qg, kb=kb, first=(idx == 0), last=(kb == 0),
                                kk=kb - (NPB + 4 * qg), ob=ob, newhead=(qg == 0 and idx == 0)))
    acur = [None]
    acnt = [0]
    NS = NZ + 2

    def stageA(i):
        c = its[i]
        if c["newhead"]:
            head_load(c["h"])
        s = c["h"] % 2
        z = i % NZ
        spi = i % NS
        zb = self.ps[zbanks[z]]
        kb, qg, kk = c["kb"], c["qg"], c["kk"]
        self.mm(zb[:, :], Kh[s][:, kb * 128:(kb + 1) * 128], Qh[s][:, qg * 512:(qg + 1) * 512], True, False,
                (f"Kh{s}", f"Qh{s}"), (("ps", zbanks[z]),))
        self.act(et[z][:, :], zb[:, :], AF.Exp, (("ps", zbanks[z]),), (f"et{z}",))
        self.act(sp_[spi][:, :], et[z][:, :], AF.Ln, (f"et{z}",), (f"sp{spi}",), bias=1.0)
        if kk >= 0:
            self.tt("pool", sp_[spi][:, :], sp_[spi][:, :], m01[:, kk, :], ALU.mult,
                    (f"sp{spi}", "m01"), (f"sp{spi}",))

    def stageB(i):
        c = its[i]
        z = i % NZ
        spi = i % NS
        zbk = zbanks[z]
        zb = self.ps[zbk]
        kk, first, last = c["kk"], c["first"], c["last"]
        self.mm(zb[:, :], self.nstrictb[:, :], sp_[spi][:, :], False, first,
                ("nstrictb", f"sp{spi}"), (("ps", zbk),))
        if not first:
            self.mm(zb[:, :], self.nonesb[:, :], acur[0][0][:, :], False, True,
                    ("nonesb", acur[0][1]), (("ps", zbk),))
        self.tt("dve", T2[z][:, :], zb[:, :], sp_[spi][:, :], ALU.subtract,
                (("ps", zbk), f"sp{spi}"), (f"T2{z}",))
        if kk >= 0:
            self.tt("pool", T2[z][:, :], T2[z][:, :], mneg[:, kk, :], ALU.add, (f"T2{z}", "mneg"), (f"T2{z}",))
        if not last:
            if first:
                acur[0] = (sp_[spi], f"sp{spi}")
            else:
                a = acnt[0] % 3
                acnt[0] += 1
                self.tt("dve", At[a][:, :], acur[0][0][:, :], sp_[spi][:, :], ALU.add,
                        (acur[0][1], f"sp{spi}"), (f"At{a}",))
                acur[0] = (At[a], f"At{a}")

    def stageC(i):
        z = i % NZ
        self.act(Wt[z][:, :], T2[z][:, :], AF.Exp, (f"T2{z}",), (f"Wt{z}",))

    def stageD(i):
        c = its[i]
        s = c["h"] % 2
        z = i % NZ
        kb, qg, first, last, ob = c["kb"], c["qg"], c["first"], c["last"], c["ob"]
        self.mm(self.ps[ob][:, :], Vh[s][:, kb, :], Wt[z][:, :], first, last,
                (f"Vh{s}", f"Wt{z}"), (("ps", ob),))
        if last:
            self.act(oT[s][:, qg * 512:(qg + 1) * 512], self.ps[ob][:, :], AF.Copy, (("ps", ob),), (f"oT{s}",))
            if qg == NQG - 1:
                h = c["h"]
                self.dma("act", self.s_o[h, :, :], oT[s][:, :], (f"oT{s}",), (("s_o", h),), f"st{s}")

    n = len(its)
    for j in range(n + 3):
        if j < n:
            stageA(j)
        if 0 <= j - 1 < n:
            stageB(j - 1)
        if 0 <= j - 2 < n:
            stageC(j - 2)
        if 0 <= j - 3 < n:
            stageD(j - 3)


Builder.phase_attn = phase_attn


def res_epi(self, xo, t0, TTg):
    nb = (TTg + 511) // 512
    cnt = [0]

    def epi(i, banks, wd):
        s = cnt[0] % 2
        cnt[0] += 1
        self.dma("sp", xo[s][:, :TTg], self.s_xT[i, :, t0:t0 + TTg], (("s_xT", i, t0),), (f"xo{s}",), f"ld{s}")
        for b in range(nb):
            w = min(512, TTg - b * 512)
            self.tt("dve", xo[s][:, b * 512:b * 512 + w], xo[s][:, b * 512:b * 512 + w], self.ps[banks[b]][:, :w],
                    ALU.add, (f"xo{s}", ("ps", banks[b])), (f"xo{s}",))
        self.dma("act", self.s_xT[i, :, t0:t0 + TTg], xo[s][:, :TTg], (f"xo{s}",), (("s_xT", i, t0),), f"st{s}")
    return epi


def phase_merge(self):
    cfg = self.cfg
    D, TT, DI = cfg.D, cfg.TT, cfg.DI
    KT = D // 128
    I = self.I
    assert DI == D
    nb = (TT + 511) // 512
    tiles = [(j * 128, 128) for j in range(KT)]
    cnt = [0]
    self.phase_reset()
    gemm_setup(self, D, TT)
    ob = [self.sb(f"ob{i}", [128, TT], BF16) for i in range(2)]
    gemm_load_panel(self, self.s_yn, "s_yn")

    def epi1(i, banks, wd):
        s = cnt[0] % 2
        cnt[0] += 1
        for b in range(nb):
            w = min(512, TT - b * 512)
            self.act(ob[s][:, b * 512:b * 512 + w], self.ps[banks[b]][:, :w], AF.Copy, (("ps", banks[b]),), (f"ob{s}",))
        self.dma("act", self.s_bs[i, :, :], ob[s][:, :], (f"ob{s}",), (("s_bs", i),), f"st{s}")
    gemm(self, I["w_ssd_out"], tiles, epi1)
    self.phase_reset()
    gemm_setup(self, D, TT)
    ob2 = [self.sb(f"ob{i}", [128, TT], BF16) for i in range(2)]
    g1 = self.sb("g1", [128, TT], BF16)
    g2 = self.sb("g2", [128, TT], BF16)
    bsb = self.sb("bsb", [128, TT], BF16)
    t1 = self.sb("t1", [128, TT], F32)
    gemm_load_panel(self, self.s_o, "s_o")

    def epi2(i, banks, wd):
        s = cnt[0] % 2
        cnt[0] += 1
        self.dma("sp", g1[:, :], self.s_gate[i, :, :], ("s_gate",), ("g1",), "ld0")
        self.dma("sp", g2[:, :], self.s_gate[KT + i, :, :], ("s_gate",), ("g2",), "ld1")
        self.dma("sp", bsb[:, :], self.s_bs[i, :, :], ("s_bs",), ("bsb",), "ld2")
        self.tt("pool", t1[:, :], g1[:, :], bsb[:, :], ALU.mult, ("g1", "bsb"), ("t1",))
        for b in range(nb):
            w = min(512, TT - b * 512)
            sl = slice(b * 512, b * 512 + w)
            self.tt("dve", g2[:, sl], g2[:, sl], self.ps[banks[b]][:, :w], ALU.mult, ("g2", ("ps", banks[b])), ("g2",))
        self.tt("dve", ob2[s][:, :], g2[:, :], t1[:, :], ALU.add, ("g2", "t1"), (f"ob{s}",))
        self.dma("act", self.s_mg[i, :, :], ob2[s][:, :], (f"ob{s}",), (("s_mg", i),), f"st{s}")
    gemm(self, I["w_sb_out"], tiles, epi2)
    self.phase_reset()
    gemm_setup(self, D, TT)
    xo = [self.sb(f"xo{i}", [128, TT], F32) for i in range(2)]
    gemm_load_panel(self, self.s_mg, "s_mg")
    gemm(self, I["w_out"], tiles, res_epi(self, xo, 0, TT))


Builder.phase_merge = phase_merge


def phase_xattn(self):
    cfg = self.cfg
    D, TT, NM = cfg.D, cfg.TT, cfg.NMEM
    KT = D // 128
    XH, XD = cfg.XH, cfg.XD
    NDT = XD // 128
    I = self.I
    self.phase_transpose_in(I["mem"], "mem", TT=NM, dst=self.s_mT)
    self.phase_norm(self.v_norm_mem, "mem", src=self.s_mT, dst=self.s_hm, ntok=NM)
    self.phase_reset()
    gemm_setup(self, D, NM)
    obm = [self.sb(f"obm{i}", [128, NM], BF16) for i in range(2)]
    gemm_load_panel(self, self.s_hm, "s_hm")
    cnt = [0]

    def epikv(i, banks, wd):
        s = cnt[0] % 2
        cnt[0] += 1
        self.act(obm[s][:, :], self.ps[banks[0]][:, :NM], AF.Copy, (("ps", banks[0]),), (f"obm{s}",))
        dst = self.s_mk if i < KT else self.s_mv
        self.dma("act", dst[i % KT, :, :], obm[s][:, :], (f"obm{s}",), ((dst.name, i),), f"st{s}")
    gemm(self, I["w_xa_kv"], [(j * 128, 128) for j in range(2 * KT)], epikv)
    self.phase_norm(self.v_norm_xa, "xa")
    self.phase_reset()
    gemm_setup(self, D, TT)
    nb = (TT + 511) // 512
    ob = [self.sb(f"ob{i}", [128, TT], BF16) for i in range(2)]
    xo = [self.sb(f"xo{i}", [128, TT], F32) for i in range(2)]
    gemm_load_panel(self, self.s_h, "s_h")

    def epiq(i, banks, wd):
        s = cnt[0] % 2
        cnt[0] += 1
        for b in range(nb):
            w = min(512, TT - b * 512)
            self.act(ob[s][:, b * 512:b * 512 + w], self.ps[banks[b]][:, :w], AF.Copy, (("ps", banks[b]),), (f"ob{s}",),
                     scale=float(XD ** -0.5))
        self.dma("act", self.s_xq[i, :, :], ob[s][:, :], (f"ob{s}",), (("s_xq", i),), f"st{s}")
    tiles = [(j * 128, 128) for j in range(KT)]
    gemm(self, I["w_xa_q"], tiles, epiq)
    self.phase_reset()
    sb = self.sb
    NMT = NM // 128
    mk = sb("mk", [128, NDT, NM], BF16)
    mvs = sb("mvs", [128, NDT, NM], BF16)
    mvt = sb("mvt", [128, NMT, XD], BF16)
    xq = sb("xq", [128, NDT, TT], BF16)
    sc = [sb(f"sc{i}", [128, NM], F32) for i in range(2)]
    mx = [sb(f"mx{i}", [128, 2], F32) for i in range(2)]
    pb = [sb(f"pb{i}", [128, NM], BF16) for i in range(2)]
    pT = [sb(f"pT{i}", [128, NMT, 512], BF16) for i in range(2)]
    oo = [sb(f"oo{i}", [128, 512], BF16) for i in range(2)]
    it = 0
    for xh in range(XH):
        k0 = xh * NDT
        self.dma("sp", mk[:, :, :], self.s_mk[k0:k0 + NDT, :, :].rearrange("k p t -> p k t"), ("s_mk",), ("mk",), "ld0")
        self.dma("sp", mvs[:, :, :], self.s_mv[k0:k0 + NDT, :, :].rearrange("k p t -> p k t"), ("s_mv",), ("mvs",), "ld1")
        self.dma("sp", xq[:, :, :], self.s_xq[k0:k0 + NDT, :, :].rearrange("k p t -> p k t"), ("s_xq",), ("xq",), "ld2")
        for dvt in range(NDT):
            for mt in range(NMT):
                self.tr(self.psb[7][:, mt * 128:(mt + 1) * 128], mvs[:, dvt, mt * 128:(mt + 1) * 128], self.identb[:, :],
                        ("mvs", "identb"), (("ps", 7),))
            self.cp("dve", mvt[:, :, dvt * 128:(dvt + 1) * 128],
                    self.psb[7][:, :NMT * 128].rearrange("p (m d) -> p m d", m=NMT), (("ps", 7),), ("mvt",))
        for tg in range(TT // 512):
            pg = tg % 2
            for t4 in range(4):
                tq = tg * 4 + t4
                z = it % 2
                it += 1
                for dt_ in range(NDT):
                    self.mm(self.ps[z][:, :NM], xq[:, dt_, tq * 128:(tq + 1) * 128], mk[:, dt_, :],
                            dt_ == 0, dt_ == NDT - 1, ("xq", "mk"), (("ps", z),))
                self.P.add("dve", (lambda zz: (lambda e: e.tensor_reduce(out=mx[zz][:, 0:1], in_=self.ps[zz][:, :NM],
                                                                          axis=AX.X, op=ALU.max)))(z),
                           (("ps", z),), (f"mx{z}",))
                self.ts("dve", mx[z][:, 0:1], mx[z][:, 0:1], -1.0, None, ALU.mult, None, (f"mx{z}",), (f"mx{z}",))
                self.act(sc[z][:, :], self.ps[z][:, :NM], AF.Exp, (("ps", z), f"mx{z}"), (f"sc{z}", f"mx{z}"),
                         bias=mx[z][:, 0:1], accum_out=mx[z][:, 1:2])
                self.P.add("dve", (lambda zz: (lambda e: e.reciprocal(out=mx[zz][:, 1:2], in_=mx[zz][:, 1:2])))(z),
                           (f"mx{z}",), (f"mx{z}",))
                self.ts("dve", pb[z][:, :], sc[z][:, :], mx[z][:, 1:2], None, ALU.mult, None,
                        (f"sc{z}", f"mx{z}"), (f"pb{z}",))
                for mt in range(NMT):
                    self.tr(self.psb[4 + z][:, mt * 128:(mt + 1) * 128], pb[z][:, mt * 128:(mt + 1) * 128],
                            self.identb[:, :], (f"pb{z}", "identb"), (("ps", 4 + z),))
                self.cp("dve" if z else "act", pT[pg][:, :, t4 * 128:(t4 + 1) * 128],
                        self.psb[4 + z][:, :NMT * 128].rearrange("p (m t) -> p m t", m=NMT),
                        (("ps", 4 + z),), (f"pT{pg}",))
            for dvt in range(NDT):
                o = it % 2
                it += 1
                for mt in range(NMT):
                    self.mm(self.ps[2 + o][:, :], mvt[:, mt, dvt * 128:(dvt + 1) * 128], pT[pg][:, mt, :],
                            mt == 0, mt == NMT - 1, ("mvt", f"pT{pg}"), (("ps", 2 + o),))
                self.act(oo[o][:, :], self.ps[2 + o][:, :], AF.Copy, (("ps", 2 + o),), (f"oo{o}",))
                self.dma("act", self.s_xo[k0 + dvt, :, tg * 512:(tg + 1) * 512], oo[o][:, :], (f"oo{o}",),
                         (("s_xo", k0 + dvt, tg),), f"st{o}")
    self.phase_reset()
    gemm_setup(self, D, TT)
    xo = [self.sb(f"xo{i}", [128, TT], F32) for i in range(2)]
    gemm_load_panel(self, self.s_xo, "s_xo")
    gemm(self, I["w_xa_o"], tiles, res_epi(self, xo, 0, TT))


Builder.phase_xattn = phase_xattn


def phase_ffn(self):
    cfg = self.cfg
    D, TT, DFF = cfg.D, cfg.TT, cfg.DFF
    KT = D // 128
    NF = DFF // 128
    I = self.I
    self.phase_norm(self.v_norm_ffn, "ffn")
    self.phase_reset()
    gemm_setup(self, D, TT)
    nb = (TT + 511) // 512
    sg = self.sb("sg", [128, TT], BF16)
    ob = [self.sb(f"ob{i}", [128, TT], BF16) for i in range(2)]
    gemm_load_panel(self, self.s_h, "s_h")
    tiles = []
    for j in range(NF):
        tiles.append((j * 128, 128))
        tiles.append((DFF + j * 128, 128))
    cnt = [0]

    def epi(i, banks, wd):
        j = i // 2
        if i % 2 == 0:
            for b in range(nb):
                w = min(512, TT - b * 512)
                self.act(sg[:, b * 512:b * 512 + w], self.ps[banks[b]][:, :w], AF.Silu, (("ps", banks[b]),), ("sg",))
        else:
            s = cnt[0] % 2
            cnt[0] += 1
            for b in range(nb):
                w = min(512, TT - b * 512)
                sl = slice(b * 512, b * 512 + w)
                self.tt("dve", ob[s][:, sl], sg[:, sl], self.ps[banks[b]][:, :w], ALU.mult,
                        ("sg", ("ps", banks[b])), (f"ob{s}",))
            self.dma("act", self.s_act[j, :, :], ob[s][:, :], (f"ob{s}",), (("s_act", j),), f"st{s}")
    gemm(self, I["w_ffn_in"], tiles, epi)
    self.phase_reset()
    NP = 3
    per = (NF + NP - 1) // NP
    gemm_setup(self, per * 128, TT)
    xo = [self.sb(f"xo{i}", [128, TT], F32) for i in range(2)]
    otiles = [(j * 128, 128) for j in range(KT)]
    for part in range(NP):
        kt0 = part * per
        nkt = min(per, NF - kt0)
        if nkt <= 0:
            break
        self.g_KT = nkt
        self.P.barrier()
        gemm_load_panel(self, self.s_act[kt0:kt0 + nkt], "s_act")
        gemm(self, I["w_ffn_out"][kt0 * 128:(kt0 + nkt) * 128, :], otiles, res_epi(self, xo, 0, TT))


Builder.phase_ffn = phase_ffn
```
